# Optimizing a Trainium2 kernel written in Bass

```python
import math
import jax, jax.numpy as jnp
from jax import lax
import numpy as np

D_MODEL = 1024
BATCH = 4
SEQ = 8192
DEPTH = 2

GRID_W = 64
CTX_LEN = 256
HY_CH = 512
HY_ORDER = 2
HY_EMB = 33
HY_BANDS = (HY_EMB - 1) // 2
HY_FFN = 64
HY_SHORT = 3
HY_DECAY_MIN = math.log(100.0) / 1.5
HY_DECAY_MAX = math.log(100.0) / 0.3
NA_HEADS = 8
NA_HD = 64
NA_WIN_R = 8
NA_WIN_C = 16
NA_QCB = 16
NA_KCB = 32
MLA_HEADS = 8
MLA_Q_RANK = 384
MLA_KV_RANK = 256
MLA_NOPE = 64
MLA_ROPE = 32
MLA_V = 96
ROPE_THETA = 10000.0
Q_BLOCK = 128
FN_CH = 256
FN_GROUPS = 4
FN_GD = FN_CH // FN_GROUPS
D_FF = 4 * D_MODEL
N_EVEN = (DEPTH + 1) // 2
N_ODD = DEPTH // 2
ALPHA = (2.0 * DEPTH) ** 0.25
OUT_SCALE = (8.0 * DEPTH) ** -0.25
LN_EPS = 1e-5

kernel_name = 'hybrid_hyena_natten_mla_fnet_dit'


def _norm_stats(x):
    xf = x.astype(jnp.float32)
    mu = jnp.mean(xf, -1, keepdims=True)
    var = jnp.mean(jnp.square(xf - mu), -1, keepdims=True)
    return (xf - mu) * lax.rsqrt(var + LN_EPS)


def _layer_norm(x, g, b):
    return (_norm_stats(x) * g + b).astype(x.dtype)


def _modulate(x, shift, scale):
    return (_norm_stats(x) * (1.0 + scale) + shift).astype(x.dtype)


def _rms_norm(x, g):
    xf = x.astype(jnp.float32)
    y = xf * lax.rsqrt(jnp.mean(jnp.square(xf), -1, keepdims=True) + LN_EPS)
    return (y * g).astype(x.dtype)


def _mlp(h, w1, w2):
    return jnp.square(jax.nn.relu(h @ w1)) @ w2


def _dense_attention(q, k, v, scale):
    s = jnp.einsum('bhqd,bhkd->bhqk', q, k).astype(jnp.float32) * scale
    p = jax.nn.softmax(s, axis=-1).astype(v.dtype)
    return jnp.einsum('bhqk,bhkd->bhqd', p, v)


def _short_conv(u, w):
    L = u.shape[1]
    up = jnp.pad(u, ((0, 0), (1, 1), (0, 0)))
    return up[:, :L] * w[0] + up[:, 1:L + 1] * w[1] + up[:, 2:] * w[2]


def _hyena_filters(L, w1, b1, freq, w2, b2, w3, log_decay):
    pos = jnp.arange(L, dtype=jnp.float32)
    t = pos / max(L - 1, 1)
    w = 2.0 * math.pi * pos / L
    f = jnp.linspace(1e-4, HY_BANDS - 1, HY_BANDS, dtype=jnp.float32)
    ang = w[:, None] * f[None, :]
    z = jnp.concatenate([t[:, None], jnp.cos(ang), -jnp.sin(ang)], -1).astype(w1.dtype)
    hid = jnp.sin(freq * (z @ w1 + b1))
    hid = jnp.sin(freq * (hid @ w2 + b2))
    h = hid @ w3
    decay = jnp.exp(-t[:, None] * jnp.exp(log_decay.astype(jnp.float32)))
    return h * decay.astype(h.dtype)


def _bidir_long_conv(z, h_fwd, h_bwd, skip):
    L = z.shape[1]
    filt = jnp.concatenate([h_fwd, jnp.zeros_like(h_fwd[:1]), h_bwd[:0:-1]], 0)
    zf = jnp.fft.rfft(z.astype(jnp.float32), n=2 * L, axis=1)
    hf = jnp.fft.rfft(filt.astype(jnp.float32), n=2 * L, axis=0)
    y = jnp.fft.irfft(zf * hf[None], n=2 * L, axis=1)[:, :L]
    return (y + z.astype(jnp.float32) * skip.astype(jnp.float32)).astype(z.dtype)


def _hyena(u, conv_w, w1, b1, freq, w2, b2, w3, log_decay, skip):
    L = u.shape[1]
    x1, x2, v = jnp.split(_short_conv(u, conv_w), 3, axis=-1)
    h = _hyena_filters(L, w1, b1, freq, w2, b2, w3, log_decay).reshape(L, HY_ORDER, 2, HY_CH)
    z = x1 * _bidir_long_conv(v, h[:, 0, 0], h[:, 0, 1], skip[0])
    return x2 * _bidir_long_conv(z, h[:, 1, 0], h[:, 1, 1], skip[1])


def _natten(q, k, v, kc, vc, rpb):
    B, L, _ = q.shape
    rows = L // GRID_W
    kr = min(NA_WIN_R, rows)
    shp = (B, rows, GRID_W, NA_HEADS, NA_HD)
    qg, kg, vg = q.reshape(shp), k.reshape(shp), v.reshape(shp)
    ncb = GRID_W // NA_QCB
    qcol = np.arange(GRID_W).reshape(ncb, NA_QCB)
    cs = np.clip(qcol - NA_WIN_C // 2, 0, GRID_W - NA_WIN_C)
    kb = np.clip(np.arange(ncb) * NA_QCB - NA_WIN_C // 2, 0, GRID_W - NA_KCB)
    kcol = kb[:, None] + np.arange(NA_KCB)[None, :]
    col_mask = (kcol[:, None, :] >= cs[..., None]) & (kcol[:, None, :] < cs[..., None] + NA_WIN_C)
    dc_idx = np.clip(kcol[:, None, :] - qcol[..., None] + NA_WIN_C - 1, 0, 2 * NA_WIN_C - 2)
    rpb_c = rpb[:, :, dc_idx]
    scale = NA_HD ** -0.5
    nlat = kr * NA_KCB

    def row_fn(r):
        rs = jnp.clip(r - kr // 2, 0, rows - kr)
        k_blk = lax.dynamic_slice_in_dim(kg, rs, kr, axis=1)[:, :, kcol]
        v_blk = lax.dynamic_slice_in_dim(vg, rs, kr, axis=1)[:, :, kcol]
        q_row = qg[:, r].reshape(B, ncb, NA_QCB, NA_HEADS, NA_HD)
        s_lat = jnp.einsum('bjqhd,brjkhd->bhjqrk', q_row, k_blk).astype(jnp.float32) * scale
        dr = rs + jnp.arange(kr) - r + NA_WIN_R - 1
        bias = jnp.take(rpb_c, dr, axis=1).transpose(0, 2, 3, 1, 4).astype(jnp.float32)
        s_lat = jnp.where(col_mask[:, :, None, :], s_lat + bias, -jnp.inf)
        s_lat = s_lat.reshape(B, NA_HEADS, ncb, NA_QCB, nlat)
        s_ctx = jnp.einsum('bjqhd,bkhd->bhjqk', q_row, kc).astype(jnp.float32) * scale
        p = jax.nn.softmax(jnp.concatenate([s_lat, s_ctx], -1), axis=-1).astype(v.dtype)
        p_lat = p[..., :nlat].reshape(B, NA_HEADS, ncb, NA_QCB, kr, NA_KCB)
        o = jnp.einsum('bhjqrk,brjkhd->bjqhd', p_lat, v_blk) + jnp.einsum('bhjqk,bkhd->bjqhd', p[..., nlat:], vc)
        return o.reshape(B, GRID_W, NA_HEADS * NA_HD)

    out = lax.map(row_fn, jnp.arange(rows))
    return out.transpose(1, 0, 2, 3).reshape(B, L, NA_HEADS * NA_HD)


def _mixer_ab(h, hc, need_ctx, w_in, w_out, conv_w, w1, b1, freq, w2, b2, w3, log_decay, skip, rpb):
    B, L, _ = h.shape
    Lc = hc.shape[1]
    n_hy = 3 * HY_CH
    na_w = NA_HEADS * NA_HD
    hy_args = (conv_w, w1, b1, freq, w2, b2, w3, log_decay, skip)
    u = h @ w_in
    q, k, v = jnp.split(u[..., n_hy:], 3, axis=-1)
    if need_ctx:
        uc = hc @ w_in
        kvc = uc[..., n_hy + na_w:]
    else:
        kvc = hc @ w_in[:, n_hy + na_w:]
    kc, vc = jnp.split(kvc, 2, axis=-1)
    kc = kc.reshape(B, Lc, NA_HEADS, NA_HD)
    vc = vc.reshape(B, Lc, NA_HEADS, NA_HD)
    y_hy = _hyena(u[..., :n_hy], *hy_args)
    y_na = _natten(q, k, v, kc, vc, rpb)
    yl = jnp.concatenate([y_hy, y_na], -1) @ w_out
    if not need_ctx:
        return yl, None
    qc = uc[..., n_hy:n_hy + na_w].reshape(B, Lc, NA_HEADS, NA_HD).transpose(0, 2, 1, 3)
    yc_na = _dense_attention(qc, kc.transpose(0, 2, 1, 3), vc.transpose(0, 2, 1, 3), NA_HD ** -0.5)
    yc_na = yc_na.transpose(0, 2, 1, 3).reshape(B, Lc, na_w)
    yc_hy = _hyena(uc[..., :n_hy], *hy_args)
    yc = jnp.concatenate([yc_hy, yc_na], -1) @ w_out
    return yl, yc


def _axial_rope_tables(L):
    t = jnp.arange(L, dtype=jnp.int32)
    rows = (t // GRID_W).astype(jnp.float32)
    cols = (t % GRID_W).astype(jnp.float32)
    half = MLA_ROPE // 2
    inv = ROPE_THETA ** (-jnp.arange(0, half, 2, dtype=jnp.float32) / half)
    ar = rows[:, None] * inv[None, :]
    ac = cols[:, None] * inv[None, :]
    ang = jnp.concatenate([ar, ar, ac, ac], -1)
    return jnp.cos(ang), jnp.sin(ang)


def _rotate(x, cos, sin):
    qd = MLA_ROPE // 4
    xf = x.astype(jnp.float32)
    a, b, c2, d = xf[..., :qd], xf[..., qd:2 * qd], xf[..., 2 * qd:3 * qd], xf[..., 3 * qd:]
    rot = jnp.concatenate([-b, a, -d, c2], -1)
    return (xf * cos + rot * sin).astype(x.dtype)


def _mla_q(cq, g, w_uq, rope):
    B, L, _ = cq.shape
    q = (_rms_norm(cq, g) @ w_uq).reshape(B, L, MLA_HEADS, MLA_NOPE + MLA_ROPE)
    q_nope, q_pe = q[..., :MLA_NOPE], q[..., MLA_NOPE:]
    if rope is not None:
        q_pe = _rotate(q_pe, *rope)
    return jnp.concatenate([q_nope, q_pe], -1).transpose(0, 2, 1, 3)


def _mla_kv(ckv, kpe, g, w_ukv, rope):
    B, L, _ = ckv.shape
    kv = (_rms_norm(ckv, g) @ w_ukv).reshape(B, L, MLA_HEADS, MLA_NOPE + MLA_V)
    k_nope, v = kv[..., :MLA_NOPE], kv[..., MLA_NOPE:]
    if rope is not None:
        kpe = _rotate(kpe, *rope)
    k_pe = jnp.broadcast_to(kpe[:, :, None, :], (B, L, MLA_HEADS, MLA_ROPE))
    k = jnp.concatenate([k_nope, k_pe], -1)
    return k.transpose(0, 2, 1, 3), v.transpose(0, 2, 1, 3)


def _mla_blocked_attention(q, k, v, kc, vc):
    B, H, L, dq = q.shape
    nb = L // Q_BLOCK
    scale = dq ** -0.5
    qb = q.reshape(B, H, nb, Q_BLOCK, dq).transpose(2, 0, 1, 3, 4)

    def block(qi):
        s = jnp.concatenate([jnp.einsum('bhqd,bhkd->bhqk', qi, k), jnp.einsum('bhqd,bhkd->bhqk', qi, kc)], -1)
        p = jax.nn.softmax(s.astype(jnp.float32) * scale, axis=-1).astype(v.dtype)
        return jnp.einsum('bhqk,bhkd->bhqd', p[..., :L], v) + jnp.einsum('bhqk,bhkd->bhqd', p[..., L:], vc)

    o = lax.map(block, qb)
    return o.transpose(1, 0, 3, 2, 4).reshape(B, L, H * MLA_V)


def _fnet(u, g, b):
    B, L, _ = u.shape
    ug = _layer_norm(u.reshape(B, L, FN_GROUPS, FN_GD), g.reshape(FN_GROUPS, FN_GD), b.reshape(FN_GROUPS, FN_GD))
    y = jnp.fft.fft2(ug.astype(jnp.float32), axes=(1, 3), norm='ortho').real
    return y.astype(u.dtype).reshape(B, L, FN_CH)


def _mixer_cd(h, hc, need_ctx, w_in, w_out, q_norm, w_uq, kv_norm, w_ukv, fn_g, fn_b):
    B, L, _ = h.shape
    Lc = hc.shape[1]
    o_kv = MLA_Q_RANK
    o_pe = o_kv + MLA_KV_RANK
    o_fn = o_pe + MLA_ROPE
    cos, sin = _axial_rope_tables(L)
    u = h @ w_in
    q = _mla_q(u[..., :o_kv], q_norm, w_uq, (cos[:, None, :], sin[:, None, :]))
    k, v = _mla_kv(u[..., o_kv:o_pe], u[..., o_pe:o_fn], kv_norm, w_ukv, (cos, sin))
    kvc = hc @ w_in[:, o_kv:o_fn]
    kc, vc = _mla_kv(kvc[..., :MLA_KV_RANK], kvc[..., MLA_KV_RANK:], kv_norm, w_ukv, None)
    y_mla = _mla_blocked_attention(q, k, v, kc, vc)
    y_fn = _fnet(u[..., o_fn:], fn_g, fn_b)
    yl = jnp.concatenate([y_mla, y_fn], -1) @ w_out
    if not need_ctx:
        return yl, None
    qc = _mla_q(hc @ w_in[:, :o_kv], q_norm, w_uq, None)
    yc_mla = _dense_attention(qc, kc, vc, (MLA_NOPE + MLA_ROPE) ** -0.5)
    yc_mla = yc_mla.transpose(0, 2, 1, 3).reshape(B, Lc, MLA_HEADS * MLA_V)
    yc_fn = _fnet(hc @ w_in[:, o_fn:], fn_g, fn_b)
    yc = jnp.concatenate([yc_mla, yc_fn], -1) @ w_out
    return yl, yc


def setup_inputs(seed: int = 0) -> dict:
    key = jax.random.key(seed)
    ks = iter(jax.random.split(key, 40))
    f32 = jnp.float32

    def nrm(shape, s):
        return jax.random.normal(next(ks), shape, f32) * s

    D = D_MODEL
    ab_in = 3 * HY_CH + 3 * NA_HEADS * NA_HD
    ab_out = HY_CH + NA_HEADS * NA_HD
    cd_in = MLA_Q_RANK + MLA_KV_RANK + MLA_ROPE + FN_CH
    cd_out = MLA_HEADS * MLA_V + FN_CH
    return {
        'x': nrm((BATCH, SEQ, D), 1.0),
        'c': nrm((BATCH, D), 1.0),
        'ctx': nrm((BATCH, CTX_LEN, D), 1.0),
        'c_ctx': nrm((D,), 1.0),
        'mod_w': nrm((DEPTH, D, 6 * D), 0.5 * D ** -0.5),
        'mod_b': nrm((DEPTH, 6 * D), 0.02),
        'ln_g': 1.0 + nrm((DEPTH, 2, D), 0.02),
        'ln_b': nrm((DEPTH, 2, D), 0.02),
        'mlp_w1': nrm((DEPTH, D, D_FF), D ** -0.5),
        'mlp_w2': nrm((DEPTH, D_FF, D), OUT_SCALE * D_FF ** -0.5),
        'ab_w_in': nrm((N_EVEN, D, ab_in), D ** -0.5),
        'ab_w_out': nrm((N_EVEN, ab_out, D), OUT_SCALE * ab_out ** -0.5),
        'hy_conv_w': nrm((N_EVEN, HY_SHORT, 3 * HY_CH), HY_SHORT ** -0.5),
        'hy_w1': nrm((N_EVEN, HY_EMB, HY_FFN), HY_EMB ** -0.5),
        'hy_b1': nrm((N_EVEN, HY_FFN), 0.1),
        'hy_freq': 1.0 + nrm((N_EVEN, HY_FFN), 0.1),
        'hy_w2': nrm((N_EVEN, HY_FFN, HY_FFN), HY_FFN ** -0.5),
        'hy_b2': nrm((N_EVEN, HY_FFN), 0.1),
        'hy_w3': nrm((N_EVEN, HY_FFN, HY_ORDER * 2 * HY_CH), 0.01),
        'hy_log_decay': jnp.log(jax.random.uniform(next(ks), (N_EVEN, HY_ORDER * 2 * HY_CH), f32, HY_DECAY_MIN, HY_DECAY_MAX)),
        'hy_skip': nrm((N_EVEN, HY_ORDER, HY_CH), 0.5),
        'na_rpb': nrm((N_EVEN, NA_HEADS, 2 * NA_WIN_R - 1, 2 * NA_WIN_C - 1), 0.02),
        'cd_w_in': nrm((N_ODD, D, cd_in), D ** -0.5),
        'cd_w_out': nrm((N_ODD, cd_out, D), OUT_SCALE * cd_out ** -0.5),
        'mla_q_norm': 1.0 + nrm((N_ODD, MLA_Q_RANK), 0.02),
        'mla_w_uq': nrm((N_ODD, MLA_Q_RANK, MLA_HEADS * (MLA_NOPE + MLA_ROPE)), MLA_Q_RANK ** -0.5),
        'mla_kv_norm': 1.0 + nrm((N_ODD, MLA_KV_RANK), 0.02),
        'mla_w_ukv': nrm((N_ODD, MLA_KV_RANK, MLA_HEADS * (MLA_NOPE + MLA_V)), MLA_KV_RANK ** -0.5),
        'fn_norm_g': 1.0 + nrm((N_ODD, FN_CH), 0.02),
        'fn_norm_b': nrm((N_ODD, FN_CH), 0.02),
    }


def reference(x, c, ctx, c_ctx, mod_w, mod_b, ln_g, ln_b, mlp_w1, mlp_w2,
              ab_w_in, ab_w_out, hy_conv_w, hy_w1, hy_b1, hy_freq, hy_w2, hy_b2, hy_w3, hy_log_decay, hy_skip, na_rpb,
              cd_w_in, cd_w_out, mla_q_norm, mla_w_uq, mla_kv_norm, mla_w_ukv, fn_norm_g, fn_norm_b):
    xl, xc = x, ctx
    for l in range(DEPTH):
        need_ctx = l < DEPTH - 1
        i = l // 2
        m_lat = jax.nn.silu(c) @ mod_w[l] + mod_b[l]
        m_ctx = jax.nn.silu(c_ctx) @ mod_w[l] + mod_b[l]
        sh1, sc1, g1, sh2, sc2, g2 = jnp.split(m_lat[:, None, :], 6, axis=-1)
        sh1c, sc1c, g1c, sh2c, sc2c, g2c = jnp.split(m_ctx, 6, axis=-1)
        h = _modulate(xl, sh1, sc1)
        hc = _modulate(xc, sh1c, sc1c)
        if l % 2 == 0:
            yl, yc = _mixer_ab(h, hc, need_ctx, ab_w_in[i], ab_w_out[i], hy_conv_w[i], hy_w1[i], hy_b1[i],
                               hy_freq[i], hy_w2[i], hy_b2[i], hy_w3[i], hy_log_decay[i], hy_skip[i], na_rpb[i])
        else:
            yl, yc = _mixer_cd(h, hc, need_ctx, cd_w_in[i], cd_w_out[i], mla_q_norm[i], mla_w_uq[i],
                               mla_kv_norm[i], mla_w_ukv[i], fn_norm_g[i], fn_norm_b[i])
        xl = _layer_norm(ALPHA * xl + g1 * yl, ln_g[l, 0], ln_b[l, 0])
        xl = _layer_norm(ALPHA * xl + g2 * _mlp(_modulate(xl, sh2, sc2), mlp_w1[l], mlp_w2[l]), ln_g[l, 1], ln_b[l, 1])
        if need_ctx:
            xc = _layer_norm(ALPHA * xc + g1c * yc, ln_g[l, 0], ln_b[l, 0])
            xc = _layer_norm(ALPHA * xc + g2c * _mlp(_modulate(xc, sh2c, sc2c), mlp_w1[l], mlp_w2[l]), ln_g[l, 1], ln_b[l, 1])
    return xl
```

```python
import math
import numpy as np
import ml_dtypes
import concourse.bass as bass
import concourse.mybir as mybir
from concourse.bass_utils import run_bass_kernel_spmd

F32 = mybir.dt.float32
BF16 = mybir.dt.bfloat16
AF = mybir.ActivationFunctionType
ALU = mybir.AluOpType
AX = mybir.AxisListType
NPBF = ml_dtypes.bfloat16

NCORES = 8
N_DMA_SEMS = 24


class Buf:
    __slots__ = ("name", "w", "r")

    def __init__(self, name=""):
        self.name = name
        self.w = None
        self.r = {}


class _Op:
    __slots__ = ("eng", "fn", "waits", "signaled", "key", "seq", "is_dma", "count")


class Prog:
    def __init__(self, nc):
        self.nc = nc
        self.engs = {"pe": nc.tensor, "act": nc.scalar, "dve": nc.vector, "pool": nc.gpsimd, "sp": nc.sync}
        self.ops = []
        self.known = {e: {} for e in self.engs}
        self.nseq = {}
        self.dma_rr = 0
        self.dma_last = {}

    def _deps(self, E, reads, writes):
        deps = []
        for b in reads:
            if b.w is not None:
                deps.append(b.w)
        for b in writes:
            if b.w is not None:
                deps.append(b.w)
            deps.extend(b.r.values())
        return deps

    def op(self, E, fn, reads=(), writes=(), dma=False):
        o = _Op()
        o.eng = E
        o.fn = fn
        o.is_dma = dma
        o.signaled = dma
        known = self.known[E]
        waits = {}
        for (key, seq, clk, prod) in self._deps(E, reads, writes):
            if known.get(key, 0) >= seq:
                continue
            if key == "pe" and E == "pe" and not dma:
                continue
            if waits.get(key, (0, None))[0] < seq:
                waits[key] = (seq, prod)
            for k2, s2 in clk.items():
                if known.get(k2, 0) < s2:
                    known[k2] = s2
            known[key] = max(known.get(key, 0), seq)
        if dma:
            j = self.dma_rr
            self.dma_rr = (self.dma_rr + 1) % N_DMA_SEMS
            key = ("dma", j)
            prev = self.dma_last.get(j)
            if prev is not None and known.get(key, 0) < prev.seq:
                if waits.get(key, (0, None))[0] < prev.seq:
                    waits[key] = (prev.seq, prev)
                known[key] = prev.seq
            self.dma_last[j] = o
        else:
            key = E
        for k, (s, prod) in waits.items():
            prod.signaled = True
        o.waits = [(k, prod) for k, (s, prod) in waits.items()]
        seq = self.nseq.get(key, 0) + 1
        self.nseq[key] = seq
        o.key = key
        o.seq = seq
        clk = dict(known)
        clk[key] = seq
        ent = (key, seq, clk, o)
        for b in writes:
            b.w = ent
            b.r = {}
        for b in reads:
            if b in writes:
                continue
            b.r[key] = ent
        self.ops.append(o)
        return o

    def dma(self, q, out, in_, reads=(), writes=()):
        eng = self.engs[q]
        return self.op(q, lambda: eng.dma_start(out=out, in_=in_), reads, writes, dma=True)

    def fence(self, q, bufs):
        return self.op(q, None, reads=bufs)

    def emit(self):
        nc = self.nc
        sems = {}
        counts = {}

        def sem_of(key):
            if key not in sems:
                nm = key if isinstance(key, str) else f"dma{key[1]}"
                sems[key] = nc.alloc_semaphore("s_" + nm)
            return sems[key]

        for o in self.ops:
            eng = self.engs[o.eng]
            for (k, prod) in o.waits:
                eng.wait_ge(sem_of(k), prod.count)
            inst = o.fn() if o.fn is not None else None
            if o.signaled and inst is not None:
                step = 16 if o.is_dma else 1
                c = counts.get(o.key, 0) + step
                counts[o.key] = c
                o.count = c
                inst.then_inc(sem_of(o.key), step)
        self.ops = []


def new_nc():
    return bass.Bass("TRN2", target_bir_lowering=False)


class Ctx:
    def __init__(self):
        self.nc = new_nc()
        self.P = Prog(self.nc)
        self._n = 0

    def din(self, name, shape, dt=F32):
        return self.nc.dram_tensor(name, list(shape), dt, kind="ExternalInput").ap()

    def dout(self, name, shape, dt=F32):
        return self.nc.dram_tensor(name, list(shape), dt, kind="ExternalOutput").ap()

    def sb(self, shape, dt=F32, name=None):
        self._n += 1
        return self.nc.alloc_sbuf_tensor(name or f"sb{self._n}", list(shape), dt)

    def ps(self, shape, dt=F32, name=None):
        self._n += 1
        return self.nc.alloc_psum_tensor(name or f"ps{self._n}", list(shape), dt)


def run_spmd(cx, in_maps):
    cx.P.emit()
    res = run_bass_kernel_spmd(cx.nc, in_maps, core_ids=list(range(len(in_maps))))
    t_ns = getattr(res, "exec_time_ns", None)
    if t_ns:
        print(f"[stage] exec_ns={t_ns}", flush=True)
    return res.results


LN_EPS = 1e-5


def emit_ln_rows(cx, xs, xs_b, xh, xh_b, st, mv, rstd, tmp_b, ncols=1024):
    P, nc = cx.P, cx.nc
    nch = ncols // 512 if ncols >= 512 else 1
    w = min(512, ncols)
    for i in range(nch):
        P.op("dve", lambda i=i: nc.vector.bn_stats(out=st[:, i, :], in_=xs[:, i * w:(i + 1) * w]),
             reads=[xs_b], writes=[tmp_b] if i == 0 else [tmp_b])
    P.op("dve", lambda: nc.vector.bn_aggr(out=mv[:, :], in_=st[:, 0:nch, :]), reads=[tmp_b], writes=[tmp_b])
    P.op("dve", lambda: nc.vector.tensor_scalar_add(out=rstd[:, :], in0=mv[:, 1:2], scalar1=LN_EPS), reads=[tmp_b], writes=[tmp_b])
    P.op("act", lambda: nc.scalar.activation(out=rstd[:, :], in_=rstd[:, :], func=AF.Sqrt), reads=[tmp_b], writes=[tmp_b])
    P.op("dve", lambda: nc.vector.reciprocal(out=rstd[:, :], in_=rstd[:, :]), reads=[tmp_b], writes=[tmp_b])
    P.op("dve", lambda: nc.vector.tensor_scalar(out=xh[:, 0:ncols], in0=xs[:, 0:ncols], scalar1=mv[:, 0:1],
                                                scalar2=rstd[:, 0:1], op0=ALU.subtract, op1=ALU.mult),
         reads=[xs_b, tmp_b], writes=[xh_b])


def load_weight_bf16(cx, w_dram, K, N, wb, wb_bufs, stage_tiles, q="sp", cast_eng="pool", col_chunk=512, col_major=True):
    P, nc = cx.P, cx.nc
    kc = K // 128
    i = 0
    order = [(k, c0) for c0 in range(0, N, col_chunk) for k in range(kc)] if col_major else [(k, c0) for k in range(kc) for c0 in range(0, N, col_chunk)]
    for (k, c0) in order:
        if True:
            cw = min(col_chunk, N - c0)
            stg, stg_b = stage_tiles[i % len(stage_tiles)]
            i += 1
            P.dma(q, stg[:, 0:cw], w_dram[k * 128:(k + 1) * 128, c0:c0 + cw], writes=[stg_b])
            eng = cx.P.engs[cast_eng]
            P.op(cast_eng, lambda eng=eng, k=k, c0=c0, cw=cw, stg=stg: eng.tensor_copy(out=wb[:, k, c0:c0 + cw], in_=stg[:, 0:cw]),
                 reads=[stg_b], writes=[wb_bufs[k][c0 // col_chunk]])


def build_stage_proj(NT, groups, NOUT, out_dt=F32):
    cx = Ctx()
    nc, P = cx.nc, cx.P
    G = max(groups) + 1
    xt = cx.din("xt", [NT, 128, 1024])
    msc = cx.din("msc", [128, G, 8])
    msh = cx.din("msh", [128, G, 8])
    w = cx.din("w", [1024, NOUT])
    ident_d = cx.din("ident", [128, 128], BF16)
    u = cx.dout("u", [NT, 128, NOUT], out_dt)

    ident = cx.sb([128, 128], BF16); ident_b = Buf()
    sc1 = cx.sb([128, G, 8]); sh1 = cx.sb([128, G, 8]); mod_b = Buf()
    P.dma("sp", ident[:, :], ident_d[:, :], writes=[ident_b])
    P.dma("sp", sc1[:, :, :], msc[:, :, :], writes=[mod_b])
    P.dma("sp", sh1[:, :, :], msh[:, :, :], writes=[mod_b])
    P.op("dve", lambda: nc.vector.tensor_scalar_add(out=sc1[:, :, :], in0=sc1[:, :, :], scalar1=1.0), reads=[mod_b], writes=[mod_b])

    wb = cx.sb([128, 8, NOUT], BF16)
    nchunk = (NOUT + 511) // 512
    wb_bufs = [[Buf() for _ in range(nchunk)] for _ in range(8)]
    stages = [(cx.sb([128, 512]), Buf()) for _ in range(3)]
    load_weight_bf16(cx, w, 1024, NOUT, wb, wb_bufs, stages, q="pool")

    NB = 3
    xs = [(cx.sb([128, 1024]), Buf()) for _ in range(NB)]
    xh = [(cx.sb([128, 1024], BF16), Buf()) for _ in range(2)]
    hT = [(cx.sb([128, 8, 128], BF16), Buf()) for _ in range(2)]
    st = cx.sb([128, 2, 6]); mv = cx.sb([128, 2]); rstd = cx.sb([128, 1]); tmp_b = Buf()
    pT = [(cx.ps([128, 1024], BF16), Buf()) for _ in range(2)]
    pO = [(cx.ps([128, 512]), Buf()) for _ in range(4)]
    ot = [(cx.sb([128, NOUT], out_dt), Buf()) for _ in range(2)]
    out_bufs = []
    po_cnt = [0]

    def prologue(t):
        g = groups[t]
        x_s, x_b = xs[t % NB]
        P.dma("sp", x_s[:, :], xt[t, :, :], writes=[x_b])
        xh_s, xh_b = xh[t % 2]
        emit_ln_rows(cx, x_s, x_b, xh_s, xh_b, st, mv, rstd, tmp_b)
        pt_s, pt_b = pT[t % 2]
        for k in range(8):
            _tr(cx, pt_s[:, k * 128:(k + 1) * 128], xh_s[:, k * 128:(k + 1) * 128], ident[:, :], [xh_b, ident_b], [pt_b])
        h_s, h_b = hT[t % 2]
        for k in range(8):
            _act(cx, h_s[:, k, :], pt_s[:, k * 128:(k + 1) * 128], AF.Identity, [pt_b, mod_b], [h_b], bias=sh1[:, g, k:k + 1], scale=sc1[:, g, k:k + 1])

    def main(t):
        h_s, h_b = hT[t % 2]
        o_s, o_b = ot[t % 2]
        for c in range(nchunk):
            c0 = c * 512
            cw = min(512, NOUT - c0)
            po_s, po_b = pO[po_cnt[0] % 4]
            po_cnt[0] += 1
            for k in range(8):
                _mm(cx, po_s[:, 0:cw], h_s[:, k, :], wb[:, k, c0:c0 + cw], k == 0, k == 7, [h_b, wb_bufs[k][c]], [po_b])
            _cp(cx, "dve" if c % 2 == 0 else "act", o_s[:, c0:c0 + cw], po_s[:, 0:cw], [po_b], [o_b])
        ob = Buf()
        P.dma("pool", u[t, :, :], o_s[:, :], reads=[o_b], writes=[ob])
        out_bufs.append(ob)

    prologue(0)
    for t in range(NT):
        if t + 1 < NT:
            prologue(t + 1)
        main(t)
    P.fence("sp", out_bufs)
    return cx


ALPHA = (2.0 * 2) ** 0.25


def emit_epilogue(cx, pos, x_s, x_b, gv, lng, lnb, vec_b, r_s, r_b, o_s, o_b, st, mv, rstd, tmp_b):
    P, nc = cx.P, cx.nc
    for (po_s, po_b, c0, cw) in pos:
        P.op("dve", lambda po_s=po_s, c0=c0, cw=cw: nc.vector.tensor_tensor(out=r_s[:, c0:c0 + cw], in0=po_s[:, 0:cw], in1=gv[:, c0:c0 + cw], op=ALU.mult),
             reads=[po_b, vec_b], writes=[r_b])
    P.op("dve", lambda: nc.vector.scalar_tensor_tensor(out=r_s[:, :], in0=x_s[:, :], scalar=ALPHA, in1=r_s[:, :], op0=ALU.mult, op1=ALU.add),
         reads=[x_b, r_b], writes=[r_b])
    for i in range(2):
        P.op("dve", lambda i=i: nc.vector.bn_stats(out=st[:, i, :], in_=r_s[:, i * 512:(i + 1) * 512]), reads=[r_b], writes=[tmp_b])
    P.op("dve", lambda: nc.vector.bn_aggr(out=mv[:, :], in_=st[:, 0:2, :]), reads=[tmp_b], writes=[tmp_b])
    P.op("dve", lambda: nc.vector.tensor_scalar_add(out=rstd[:, :], in0=mv[:, 1:2], scalar1=LN_EPS), reads=[tmp_b], writes=[tmp_b])
    P.op("act", lambda: nc.scalar.activation(out=rstd[:, :], in_=rstd[:, :], func=AF.Sqrt), reads=[tmp_b], writes=[tmp_b])
    P.op("dve", lambda: nc.vector.reciprocal(out=rstd[:, :], in_=rstd[:, :]), reads=[tmp_b], writes=[tmp_b])
    P.op("dve", lambda: nc.vector.tensor_scalar(out=r_s[:, :], in0=r_s[:, :], scalar1=mv[:, 0:1], scalar2=rstd[:, 0:1], op0=ALU.subtract, op1=ALU.mult),
         reads=[r_b, tmp_b], writes=[r_b])
    P.op("pool", lambda: nc.gpsimd.tensor_tensor(out=o_s[:, :], in0=r_s[:, :], in1=lng[:, :], op=ALU.mult), reads=[r_b, vec_b], writes=[o_b])
    P.op("pool", lambda: nc.gpsimd.tensor_tensor(out=o_s[:, :], in0=o_s[:, :], in1=lnb[:, :], op=ALU.add), reads=[o_b, vec_b], writes=[o_b])


def build_stage_mix(NT, groups):
    cx = Ctx()
    nc, P = cx.nc, cx.P
    G = max(groups) + 1
    yt = cx.din("yt", [NT, 128, 1024])
    xt = cx.din("xt", [NT, 128, 1024])
    w = cx.din("w", [1024, 1024])
    gvd = cx.din("gv", [128, G, 1024])
    lngd = cx.din("lng", [128, 1024])
    lnbd = cx.din("lnb", [128, 1024])
    ident_d = cx.din("ident", [128, 128], BF16)
    xo = cx.dout("xo", [NT, 128, 1024])

    ident = cx.sb([128, 128], BF16); ident_b = Buf()
    P.dma("sp", ident[:, :], ident_d[:, :], writes=[ident_b])
    gv = cx.sb([128, G, 1024]); lng = cx.sb([128, 1024]); lnb = cx.sb([128, 1024]); vec_b = Buf()
    P.dma("sp", gv[:, :, :], gvd[:, :, :], writes=[vec_b])
    P.dma("sp", lng[:, :], lngd[:, :], writes=[vec_b])
    P.dma("sp", lnb[:, :], lnbd[:, :], writes=[vec_b])
    wb = cx.sb([128, 8, 1024], BF16)
    wb_bufs = [[Buf() for _ in range(2)] for _ in range(8)]
    stages = [(cx.sb([128, 512]), Buf()) for _ in range(3)]
    load_weight_bf16(cx, w, 1024, 1024, wb, wb_bufs, stages, q="pool")

    ys = [(cx.sb([128, 1024]), Buf()) for _ in range(3)]
    xs = [(cx.sb([128, 1024]), Buf()) for _ in range(3)]
    yh = [(cx.sb([128, 1024], BF16), Buf()) for _ in range(2)]
    yT = [(cx.sb([128, 8, 128], BF16), Buf()) for _ in range(2)]
    rs = [(cx.sb([128, 1024]), Buf()) for _ in range(2)]
    os_ = [(cx.sb([128, 1024]), Buf()) for _ in range(2)]
    st = cx.sb([128, 2, 6]); mv = cx.sb([128, 2]); rstd = cx.sb([128, 1]); tmp_b = Buf()
    pT = [(cx.ps([128, 1024], BF16), Buf()) for _ in range(2)]
    pO = [(cx.ps([128, 512]), Buf()) for _ in range(4)]
    out_bufs = []

    def prologue(t):
        y_s, y_b = ys[t % 3]; x_s, x_b = xs[t % 3]
        P.dma("sp", y_s[:, :], yt[t, :, :], writes=[y_b])
        P.dma("sp", x_s[:, :], xt[t, :, :], writes=[x_b])
        yh_s, yh_b = yh[t % 2]
        _cp(cx, "act", yh_s[:, :], y_s[:, :], [y_b], [yh_b])
        pt_s, pt_b = pT[t % 2]
        for k in range(8):
            _tr(cx, pt_s[:, k * 128:(k + 1) * 128], yh_s[:, k * 128:(k + 1) * 128], ident[:, :], [yh_b, ident_b], [pt_b])
        yT_s, yT_b = yT[t % 2]
        _cp(cx, "act", yT_s[:, :, :], pt_s[:, :].rearrange("p (k q) -> p k q", q=128), [pt_b], [yT_b])

    def main(t):
        g = groups[t]
        x_s, x_b = xs[t % 3]
        yT_s, yT_b = yT[t % 2]
        pos = []
        for c in range(2):
            po_s, po_b = pO[(2 * t + c) % 4]
            for k in range(8):
                _mm(cx, po_s[:, :], yT_s[:, k, :], wb[:, k, c * 512:(c + 1) * 512], k == 0, k == 7, [yT_b, wb_bufs[k][c]], [po_b])
            pos.append((po_s, po_b, c * 512, 512))
        r_s, r_b = rs[t % 2]; o_s, o_b = os_[t % 2]
        emit_epilogue(cx, pos, x_s, x_b, gv[:, g, :], lng, lnb, vec_b, r_s, r_b, o_s, o_b, st, mv, rstd, tmp_b)
        ob = Buf()
        P.dma("pool", xo[t, :, :], o_s[:, :], reads=[o_b], writes=[ob])
        out_bufs.append(ob)

    prologue(0)
    for t in range(NT):
        if t + 1 < NT:
            prologue(t + 1)
        main(t)
    P.fence("sp", out_bufs)
    return cx


def build_stage_mlp(NT, groups):
    cx = Ctx()
    nc, P = cx.nc, cx.P
    G = max(groups) + 1
    xt = cx.din("xt", [NT, 128, 1024])
    msc = cx.din("msc", [128, G, 8])
    msh = cx.din("msh", [128, G, 8])
    w1 = cx.din("w1", [1024, 4096])
    w2 = cx.din("w2", [4096, 1024])
    gvd = cx.din("gv", [128, G, 1024])
    lngd = cx.din("lng", [128, 1024])
    lnbd = cx.din("lnb", [128, 1024])
    ident_d = cx.din("ident", [128, 128], BF16)
    xo = cx.dout("xo", [NT, 128, 1024])

    ident = cx.sb([128, 128], BF16); ident_b = Buf()
    P.dma("sp", ident[:, :], ident_d[:, :], writes=[ident_b])
    sc1 = cx.sb([128, G, 8]); sh1 = cx.sb([128, G, 8]); mod_b = Buf()
    P.dma("sp", sc1[:, :, :], msc[:, :, :], writes=[mod_b])
    P.dma("sp", sh1[:, :, :], msh[:, :, :], writes=[mod_b])
    P.op("dve", lambda: nc.vector.tensor_scalar_add(out=sc1[:, :, :], in0=sc1[:, :, :], scalar1=1.0), reads=[mod_b], writes=[mod_b])
    gv = cx.sb([128, G, 1024]); lng = cx.sb([128, 1024]); lnb = cx.sb([128, 1024]); vec_b = Buf()
    P.dma("sp", gv[:, :, :], gvd[:, :, :], writes=[vec_b])
    P.dma("sp", lng[:, :], lngd[:, :], writes=[vec_b])
    P.dma("sp", lnb[:, :], lnbd[:, :], writes=[vec_b])
    w1b = cx.sb([128, 8, 4096], BF16)
    w1_bufs = [[Buf() for _ in range(8)] for _ in range(8)]
    w2b = cx.sb([128, 32, 1024], BF16)
    w2_bufs = [[Buf() for _ in range(2)] for _ in range(32)]
    stages = [(cx.sb([128, 512]), Buf()) for _ in range(2)]
    load_weight_bf16(cx, w1, 1024, 4096, w1b, w1_bufs, stages, q="pool")
    load_weight_bf16(cx, w2, 4096, 1024, w2b, w2_bufs, stages, q="pool", col_major=False)

    assert NT % 2 == 0
    xs = [(cx.sb([128, 1024]), Buf()) for _ in range(4)]
    xh = [(cx.sb([128, 1024], BF16), Buf()) for _ in range(2)]
    hT = [cx.sb([128, 8, 256], BF16) for _ in range(2)]
    hT_b = [[Buf(), Buf()] for _ in range(2)]
    aR = [(cx.sb([128, 256], BF16), Buf()) for _ in range(2)]
    aT = cx.sb([128, 32, 256], BF16)
    aT_bufs = [Buf() for _ in range(32)]
    rs = [(cx.sb([128, 1024]), Buf()) for _ in range(1)]
    os_ = [(cx.sb([128, 1024]), Buf()) for _ in range(2)]
    st = cx.sb([128, 2, 6]); mv = cx.sb([128, 2]); rstd = cx.sb([128, 1]); tmp_b = Buf()
    pT = [(cx.ps([128, 1024], BF16), Buf()) for _ in range(2)]
    pA = [(cx.ps([128, 512]), Buf()) for _ in range(2)]
    pO = [(cx.ps([128, 512]), Buf()) for _ in range(4)]
    out_bufs = []
    ai = [0]

    def prologue(t):
        g = groups[t]
        x_s, x_b = xs[t % 4]
        P.dma("sp", x_s[:, :], xt[t, :, :], writes=[x_b])
        xh_s, xh_b = xh[t % 2]
        emit_ln_rows(cx, x_s, x_b, xh_s, xh_b, st, mv, rstd, tmp_b)
        pt_s, pt_b = pT[t % 2]
        for k in range(8):
            _tr(cx, pt_s[:, k * 128:(k + 1) * 128], xh_s[:, k * 128:(k + 1) * 128], ident[:, :], [xh_b, ident_b], [pt_b])
        pr = (t // 2) % 2
        hh = t % 2
        for k in range(8):
            _act(cx, hT[pr][:, k, hh * 128:(hh + 1) * 128], pt_s[:, k * 128:(k + 1) * 128], AF.Identity, [pt_b, mod_b], [hT_b[pr][hh]], bias=sh1[:, g, k:k + 1], scale=sc1[:, g, k:k + 1])

    def main_pair(p):
        pr = p % 2
        h_s = hT[pr]
        for j in range(32):
            pa_s, pa_b = pA[ai[0] % 2]
            ar_s, ar_b = aR[ai[0] % 2]
            ai[0] += 1
            for k in range(8):
                _mm(cx, pa_s[:, 0:256], w1b[:, k, j * 128:(j + 1) * 128], h_s[:, k, :], k == 0, k == 7, hT_b[pr] + [w1_bufs[k][j // 4]], [pa_b])
            _act(cx, ar_s[:, :], pa_s[:, 0:256], AF.Relu, [pa_b], [ar_b])
            _tt(cx, "pool", aT[:, j, :], ar_s[:, :], ar_s[:, :], ALU.mult, [ar_b], [aT_bufs[j]])
        for hh in range(2):
            t = 2 * p + hh
            g = groups[t]
            x_s, x_b = xs[t % 4]
            pos = []
            for c in range(2):
                po_s, po_b = pO[hh * 2 + c]
                for j in range(32):
                    _mm(cx, po_s[:, :], aT[:, j, hh * 128:(hh + 1) * 128], w2b[:, j, c * 512:(c + 1) * 512], j == 0, j == 31, [aT_bufs[j], w2_bufs[j][c]], [po_b])
                pos.append((po_s, po_b, c * 512, 512))
            r_s, r_b = rs[0]; o_s, o_b = os_[t % 2]
            emit_epilogue(cx, pos, x_s, x_b, gv[:, g, :], lng, lnb, vec_b, r_s, r_b, o_s, o_b, st, mv, rstd, tmp_b)
            ob = Buf()
            P.dma("pool", xo[t, :, :], o_s[:, :], reads=[o_b], writes=[ob])
            out_bufs.append(ob)

    prologue(0); prologue(1)
    for p in range(NT // 2):
        if 2 * p + 2 < NT:
            prologue(2 * p + 2); prologue(2 * p + 3)
        main_pair(p)
    P.fence("sp", out_bufs)
    return cx


def build_stage_mod():
    cx = Ctx()
    nc, P = cx.nc, cx.P
    cT = cx.din("cT", [128, 8, 8])
    w = cx.din("w", [1024, 1536])
    bias = cx.din("bias", [8, 1536])
    m = cx.dout("m", [8, 1536])
    c_s = cx.sb([128, 8, 8]); c_b = Buf()
    sg = cx.sb([128, 8, 8])
    P.dma("sp", c_s[:, :, :], cT[:, :, :], writes=[c_b])
    P.op("act", lambda: nc.scalar.activation(out=sg[:, :, :], in_=c_s[:, :, :], func=AF.Sigmoid), reads=[c_b], writes=[c_b])
    P.op("dve", lambda: nc.vector.tensor_tensor(out=c_s[:, :, :], in0=c_s[:, :, :], in1=sg[:, :, :], op=ALU.mult), reads=[c_b], writes=[c_b])
    b_s = cx.sb([8, 1536]); b_b = Buf()
    P.dma("sp", b_s[:, :], bias[:, :], writes=[b_b])
    ws = cx.sb([128, 8, 1536]); w_bufs = [Buf() for _ in range(8)]
    for k in range(8):
        P.dma("sp" if k % 2 == 0 else "pool", ws[:, k, :], w[k * 128:(k + 1) * 128, :], writes=[w_bufs[k]])
    o_s = cx.sb([8, 1536]); o_b = Buf()
    pO = [(cx.ps([8, 512]), Buf()) for _ in range(3)]
    for c in range(3):
        po_s, po_b = pO[c]
        for k in range(8):
            P.op("pe", lambda k=k, c=c, po_s=po_s: nc.tensor.matmul(out=po_s[:, :], lhsT=c_s[:, k, :], rhs=ws[:, k, c * 512:(c + 1) * 512], start=(k == 0), stop=(k == 7)),
                 reads=[c_b, w_bufs[k]], writes=[po_b])
        P.op("dve", lambda c=c, po_s=po_s: nc.vector.tensor_tensor(out=o_s[:, c * 512:(c + 1) * 512], in0=po_s[:, :], in1=b_s[:, c * 512:(c + 1) * 512], op=ALU.add),
             reads=[po_b, b_b], writes=[o_b])
    ob = Buf()
    P.dma("sp", m[:, :], o_s[:, :], reads=[o_b], writes=[ob])
    P.fence("sp", [ob])
    return cx


NA_KROWS = 68
NA_NPAT = 9
NA_WIN = 12


def na_items():
    items = []
    for j in range(64):
        w = min(max(j - 4, 0), 56)
        pid = j if j < 5 else (5 if j <= 60 else j - 55)
        items.append((j * 64, w, pid))
    items.append((64 * 64, None, None))
    items.append((65 * 64, None, None))
    return items


def build_stage_na(items):
    NI = len(items)
    NQ = NI * 64
    NK = NA_KROWS * 64
    WK = NA_WIN * 64
    cx = Ctx()
    nc, P = cx.nc, cx.P
    qT = cx.din("qT", [4, 128, NQ])
    kT = cx.din("kT", [4, 128, NK])
    vv = cx.din("v", [4, 64, NA_KROWS, 128])
    kcT = cx.din("kcT", [4, 128, 256])
    vc = cx.din("vc", [4, 64, 4, 128])
    biasd = cx.din("bias", [4, 128, NA_NPAT, WK])
    y = cx.dout("y", [NI, 64, 512])
    scale = 64 ** -0.5

    stg = [(cx.sb([128, 2048]), Buf()) for _ in range(2)]
    q_s = cx.sb([128, NQ], BF16); q_b = Buf()
    k_s = cx.sb([128, NK], BF16); k_b = Buf()
    kc_s = cx.sb([128, 256], BF16); kc_b = Buf()
    Kbd = cx.sb([128, NA_KROWS, 128], BF16); kbd_b = Buf()
    Kbc = cx.sb([128, 4, 128], BF16); kbc_b = Buf()
    v_s = cx.sb([128, NA_KROWS, 65], BF16); v_b = Buf()
    vc_s = cx.sb([128, 4, 65], BF16); vc_b = Buf()
    b_s = cx.sb([128, NA_NPAT, WK]); b_b = Buf()
    onesbd = cx.sb([128, 128], BF16); c_b = Buf()
    P.op("pool", lambda: nc.gpsimd.memset(onesbd[:, :], 0.0), writes=[c_b])
    P.op("pool", lambda: nc.gpsimd.memset(onesbd[0:64, 0:64], 1.0), writes=[c_b])
    P.op("pool", lambda: nc.gpsimd.memset(onesbd[64:128, 64:128], 1.0), writes=[c_b])
    P.op("pool", lambda: nc.gpsimd.memset(Kbd[:, :, :], 0.0), writes=[kbd_b])
    P.op("pool", lambda: nc.gpsimd.memset(Kbc[:, :, :], 0.0), writes=[kbc_b])
    P.op("pool", lambda: nc.gpsimd.memset(v_s[:, :, 64:65], 1.0), writes=[v_b])
    P.op("pool", lambda: nc.gpsimd.memset(vc_s[:, :, 64:65], 1.0), writes=[vc_b])
    sq = [(cx.sb([128, 512], BF16), Buf()) for _ in range(2)]
    acc = cx.sb([128, 32]); acc_b = Buf()
    negM = cx.sb([128, 1]); nm_b = Buf()
    s_sb = [(cx.sb([128, WK]), Buf()) for _ in range(2)]
    pT_sb = [(cx.sb([128, WK + 256], BF16), Buf()) for _ in range(3)]
    rinv = cx.sb([64, 2]); rinv_b = Buf()
    y_sb = [(cx.sb([64, 128]), Buf()) for _ in range(2)]
    pS = [cx.ps([128, 1024]) for _ in range(2)]
    pS_b = [[Buf(), Buf()] for _ in range(2)]
    pO = [[(cx.ps([64, 512]), Buf()) for _h in range(2)] for _ in range(2)]
    pM = [(pS[i][:, 0:512], pS_b[i][0]) for i in range(2)]
    out_bufs = []
    si = [0]

    def load_cast(dst_ap, src_ap, width, dst_b):
        st_s, st_b = stg[si[0] % 2]
        si[0] += 1
        P.dma("sp", st_s[:, 0:width], src_ap, writes=[st_b])
        _cp(cx, "pool", dst_ap, st_s[:, 0:width], [st_b], [dst_b])

    gi = [0]
    for hp in range(4):
        for c0 in range(0, NQ, 2048):
            cw = min(2048, NQ - c0)
            load_cast(q_s[:, c0:c0 + cw], qT[hp, :, c0:c0 + cw], cw, q_b)
        for c0 in range(0, NK, 2048):
            cw = min(2048, NK - c0)
            load_cast(k_s[:, c0:c0 + cw], kT[hp, :, c0:c0 + cw], cw, k_b)
        load_cast(kc_s[:, :], kcT[hp, :, :], 256, kc_b)
        for hh in range(2):
            ps_ = slice(hh * 64, (hh + 1) * 64)
            _cp(cx, "pool", Kbd[ps_, :, hh * 64:(hh + 1) * 64], k_s[ps_, :].rearrange("p (r c) -> p r c", c=64), [k_b], [kbd_b])
            _cp(cx, "pool", Kbc[ps_, :, hh * 64:(hh + 1) * 64], kc_s[ps_, :].rearrange("p (r c) -> p r c", c=64), [kc_b], [kbc_b])
        for r0 in range(0, NA_KROWS, 16):
            rw = min(16, NA_KROWS - r0)
            st_s, st_b = stg[si[0] % 2]; si[0] += 1
            for hh in range(2):
                P.dma("sp", st_s[hh * 64:(hh + 1) * 64, 0:rw * 128].rearrange("p (r c) -> p r c", c=128), vv[hp, :, r0:r0 + rw, :], writes=[st_b])
            for hh in range(2):
                _cp(cx, "pool", v_s[hh * 64:(hh + 1) * 64, r0:r0 + rw, 0:64],
                    st_s[hh * 64:(hh + 1) * 64, 0:rw * 128].rearrange("p (r c) -> p r c", c=128)[:, :, hh * 64:(hh + 1) * 64], [st_b], [v_b])
        st_s, st_b = stg[si[0] % 2]; si[0] += 1
        for hh in range(2):
            P.dma("sp", st_s[hh * 64:(hh + 1) * 64, 0:512].rearrange("p (r c) -> p r c", c=128), vc[hp, :, :, :], writes=[st_b])
        for hh in range(2):
            _cp(cx, "pool", vc_s[hh * 64:(hh + 1) * 64, :, 0:64],
                st_s[hh * 64:(hh + 1) * 64, 0:512].rearrange("p (r c) -> p r c", c=128)[:, :, hh * 64:(hh + 1) * 64], [st_b], [vc_b])
        P.dma("sp", b_s[:, :, :], biasd[hp, :, :, :], writes=[b_b])
        na_ = 0
        for (src, src_b, n) in ((q_s, q_b, NQ), (k_s, k_b, NK), (kc_s, kc_b, 256)):
            isq = 0 if src is q_s else 1
            for c0 in range(0, n, 512):
                cw = min(512, n - c0)
                s_s, s_bb = sq[na_ % 2]
                _act(cx, s_s[:, 0:cw], src[:, c0:c0 + cw], AF.Square, [src_b], [s_bb])
                pm, pmb = pM[na_ % 2]
                _mm(cx, pm[:, 0:cw], onesbd[:, :], s_s[:, 0:cw], True, True, [s_bb, c_b], [pmb])
                col = na_ if isq == 0 else 10 + (na_ - 9)
                P.op("dve", lambda pm=pm, cw=cw, col=col: nc.vector.reduce_max(out=acc[:, col:col + 1], in_=pm[:, 0:cw], axis=AX.X), reads=[pmb], writes=[acc_b])
                na_ += 1
        nq_ch = (NQ + 511) // 512
        assert nq_ch == 9 and na_ == 9 + 9 + 1
        P.op("dve", lambda: nc.vector.reduce_max(out=acc[:, 30:31], in_=acc[:, 0:9], axis=AX.X), reads=[acc_b], writes=[acc_b])
        P.op("dve", lambda: nc.vector.reduce_max(out=acc[:, 31:32], in_=acc[:, 10:20], axis=AX.X), reads=[acc_b], writes=[acc_b])
        _tt(cx, "dve", negM[:, :], acc[:, 30:31], acc[:, 31:32], ALU.mult, [acc_b], [nm_b])
        _act(cx, negM[:, :], negM[:, :], AF.Sqrt, [nm_b], [nm_b])
        _ts(cx, "dve", negM[:, :], negM[:, :], -scale, None, ALU.mult, None, [nm_b], [nm_b])

        def emit_S(ii, g):
            qtok0, w, pid = items[ii]
            ps = pS[g % 2]; pb = pS_b[g % 2]
            if w is not None:
                for i in range(NA_WIN):
                    bank = 0 if i < 8 else 1
                    _mm(cx, ps[:, i * 64:(i + 1) * 64], Kbd[:, w + i, :], q_s[:, qtok0:qtok0 + 64], True, True, [kbd_b, q_b], [pb[bank]])
            for blk in range(4):
                _mm(cx, ps[:, WK + blk * 64:WK + (blk + 1) * 64], Kbc[:, blk, :], q_s[:, qtok0:qtok0 + 64], True, True, [kbc_b, q_b], [pb[1]])

        emit_S(0, gi[0])
        for ii, (qtok0, w, pid) in enumerate(items):
            g = gi[0]
            if ii + 1 < NI:
                emit_S(ii + 1, g + 1)
            ps = pS[g % 2]; pb = pS_b[g % 2]
            s_s, s_bb = s_sb[g % 2]
            p_s, p_b = pT_sb[g % 3]
            lat = w is not None
            if lat:
                bview = b_s[:, pid, :]
                P.op("dve", lambda ps=ps, s_s=s_s, bview=bview: nc.vector.scalar_tensor_tensor(out=s_s[:, 0:512], in0=ps[:, 0:512], scalar=scale, in1=bview[:, 0:512], op0=ALU.mult, op1=ALU.add),
                     reads=[pb[0], b_b], writes=[s_bb])
                P.op("dve", lambda ps=ps, s_s=s_s, bview=bview: nc.vector.scalar_tensor_tensor(out=s_s[:, 512:WK], in0=ps[:, 512:WK], scalar=scale, in1=bview[:, 512:WK], op0=ALU.mult, op1=ALU.add),
                     reads=[pb[1], b_b], writes=[s_bb])
                _act(cx, p_s[:, 0:WK], s_s[:, :], AF.Exp, [s_bb, nm_b], [p_b], bias=negM[:, 0:1], scale=1.0)
            _act(cx, p_s[:, WK:WK + 256], ps[:, WK:WK + 256], AF.Exp, [pb[1], nm_b], [p_b], bias=negM[:, 0:1], scale=scale)
            ysb_s, ysb_b = y_sb[g % 2]
            for hh in range(2):
                po_s, po_b = pO[g % 2][hh]
                hs = slice(hh * 64, (hh + 1) * 64)
                first = True
                if lat:
                    for i in range(NA_WIN):
                        _mm(cx, po_s[:, 0:65], p_s[hs, i * 64:(i + 1) * 64], v_s[hs, w + i, :], first, False, [p_b, v_b], [po_b])
                        first = False
                for blk in range(4):
                    _mm(cx, po_s[:, 0:65], p_s[hs, WK + blk * 64:WK + (blk + 1) * 64], vc_s[hs, blk, :], first, blk == 3, [p_b, vc_b], [po_b])
                    first = False
            for hh in range(2):
                po_s, po_b = pO[g % 2][hh]
                P.op("dve", lambda po_s=po_s, hh=hh: nc.vector.reciprocal(out=rinv[:, hh:hh + 1], in_=po_s[:, 64:65]), reads=[po_b], writes=[rinv_b])
                _ts(cx, "dve", ysb_s[:, hh * 64:(hh + 1) * 64], po_s[:, 0:64], rinv[:, hh:hh + 1], None, ALU.mult, None, [po_b, rinv_b], [ysb_b])
            ob = Buf()
            P.dma("pool", y[ii, :, hp * 128:(hp + 1) * 128], ysb_s[:, :], reads=[ysb_b], writes=[ob])
            out_bufs.append(ob)
            gi[0] += 1
        gi[0] += 1
    P.fence("sp", out_bufs)
    return cx


def na_bias_tables(rpb, half):
    GW, WR, WC = 64, 8, 16
    R0 = 0 if half == 0 else 60
    qcol = np.arange(64)
    cs = np.clip(qcol - WC // 2, 0, GW - WC)
    kcol = np.arange(64)
    mask = (kcol[None, :] >= cs[:, None]) & (kcol[None, :] < cs[:, None] + WC)
    dc = np.clip(kcol[None, :] - qcol[:, None] + WC - 1, 0, 2 * WC - 2)
    tabs = {}
    for (qtok0, w, pid) in na_items()[:64]:
        j = qtok0 // 64
        r = 64 * half + j
        rs = int(np.clip(r - WR // 2, 0, 128 - WR))
        t = np.full((8, 64, NA_WIN, 64), -30000.0, np.float32)
        for i in range(NA_WIN):
            grow = R0 + w + i
            if rs <= grow < rs + WR:
                dr = grow - r + WR - 1
                vals = rpb[:, dr][:, dc]
                t[:, :, i, :] = np.where(mask[None], vals, np.float32(-30000.0))
        t = t.reshape(8, 64, NA_WIN * 64)
        if pid in tabs:
            assert np.array_equal(tabs[pid], t)
        tabs[pid] = t
    T = np.stack([tabs[p] for p in range(NA_NPAT)], 0)
    T = T.reshape(NA_NPAT, 4, 2, 64, NA_WIN, 64).transpose(1, 2, 5, 0, 4, 3).reshape(4, 128, NA_NPAT, NA_WIN * 64)
    return np.ascontiguousarray(T)


def na_inputs(u_b, uc_b, rpb, half):
    R0 = 0 if half == 0 else 60
    q = u_b[:, 1536:2048].reshape(128, 64, 4, 128)[64 * half:64 * half + 64]
    qc = uc_b[:, 1536:2048].reshape(4, 64, 4, 128)[2 * half:2 * half + 2]
    qT = np.concatenate([q, qc], 0).reshape(-1, 4, 128).transpose(1, 2, 0)
    k = u_b[:, 2048:2560].reshape(128, 64, 4, 128)[R0:R0 + NA_KROWS]
    kT = k.reshape(-1, 4, 128).transpose(1, 2, 0)
    v = u_b[:, 2560:3072].reshape(128, 64, 4, 128)[R0:R0 + NA_KROWS].transpose(2, 1, 0, 3)
    kcT = uc_b[:, 2048:2560].reshape(256, 4, 128).transpose(1, 2, 0)
    vc = uc_b[:, 2560:3072].reshape(4, 64, 4, 128).transpose(2, 1, 0, 3)
    c = np.ascontiguousarray
    return dict(qT=c(qT), kT=c(kT), v=c(v), kcT=c(kcT), vc=c(vc), bias=na_bias_tables(rpb, half))


HY_L = 8192
HY_DEBUG = False
HY_NBUF = 1
HY_TSPLIT = 1


def hy_perm_mats():
    def z():
        return np.zeros((128, 128), np.float32)
    m = {}
    a = z(); a[np.arange(0, 127), np.arange(1, 128)] = 1; m["down"] = a
    a = z(); a[127, 0] = 1; m["downB"] = a
    a = z(); a[np.arange(1, 128), np.arange(0, 127)] = 1; m["up"] = a
    a = z(); a[0, 127] = 1; m["upB"] = a
    a = z(); a[np.arange(128), 127 - np.arange(128)] = 1; m["J"] = a
    for nme in ("down", "downB", "up", "upB"):
        m["J" + nme] = np.ascontiguousarray(m[nme][:, ::-1])
    order = ["down", "downB", "up", "upB", "J", "Jdown", "JdownB", "Jup", "JupB"]
    return np.stack([m[k] for k in order], 1).astype(NPBF)


def hy_lag_tables(L):
    lag = np.arange(2 * L) - L
    pos = np.abs(lag).astype(np.float32)
    t = pos / np.float32(max(L - 1, 1))
    w = np.float32(2.0 * math.pi) * pos / np.float32(L)
    nb = 16
    f = np.linspace(1e-4, nb - 1, nb, dtype=np.float32)
    ang = w[:, None] * f[None, :]
    z = np.concatenate([t[:, None], np.cos(ang), -np.sin(ang)], -1).astype(np.float32)
    zT = np.ascontiguousarray(z.T)
    tb = np.ascontiguousarray(np.broadcast_to(t[None, :], (128, 2 * L))).astype(np.float32)
    return zT, tb


def build_stage_hy():
    cx = Ctx()
    nc, P = cx.nc, cx.P
    NBm, NBc = 64, 2
    um = cx.din("um", [3, 64, 128, 4 * NBm])
    uc = cx.din("uc", [3, 64, 128, 4 * NBc])
    wcd = cx.din("wc", [128, 3, 3, 64])
    permd = cx.din("perm", [128, 9, 128], BF16)
    zTm = cx.din("zTm", [33, 2 * HY_L]); tbm = cx.din("tbm", [128, 2 * HY_L])
    zTc = cx.din("zTc", [33, 512]); tbc = cx.din("tbc", [128, 512])
    w1d = cx.din("w1", [33, 64]); w2d = cx.din("w2", [64, 64])
    b1d = cx.din("b1", [64, 1]); b2d = cx.din("b2", [64, 1]); frd = cx.din("freq", [64, 1])
    w3d = cx.din("w3", [64, 2, 128])
    ldd = cx.din("ld", [128, 2])
    skd = cx.din("skip", [128, 1])
    ym = cx.dout("ym", [64, 128, 4 * NBm])
    yc = cx.dout("yc", [64, 128, 4 * NBc])
    Adm = nc.dram_tensor("Adm", [128, 2 * HY_L], BF16)
    adbg = cx.dout("adbg", [128, 2 * HY_L]) if HY_DEBUG else None
    out_bufs = []
    Adc = nc.dram_tensor("Adc", [128, 512], BF16)

    perm = cx.sb([128, 9, 128], BF16); c_b = Buf()
    wc = cx.sb([128, 3, 3, 64])
    w1 = cx.sb([33, 64]); w2 = cx.sb([64, 64]); b1 = cx.sb([64, 1]); b2 = cx.sb([64, 1]); fr = cx.sb([64, 1])
    w3 = cx.sb([64, 2, 128]); ld = cx.sb([128, 2]); sk = cx.sb([128, 1]); negpi = cx.sb([128, 1])
    for (d_, s_) in ((perm[:, :, :], permd[:, :, :]), (wc[:, :, :, :], wcd[:, :, :, :]), (w1[:, :], w1d[:, :]), (w2[:, :], w2d[:, :]),
                     (b1[:, :], b1d[:, :]), (b2[:, :], b2d[:, :]), (fr[:, :], frd[:, :]), (w3[:, :, :], w3d[:, :, :]), (ld[:, :], ldd[:, :]), (sk[:, :], skd[:, :])):
        P.dma("sp", d_, s_, writes=[c_b])
    P.op("pool", lambda: nc.gpsimd.memset(negpi[:, :], -math.pi), writes=[c_b])
    fr2 = cx.sb([64, 1])
    P.op("dve", lambda: nc.vector.tensor_scalar_mul(out=fr2[:, :], in0=fr[:, :], scalar1=1.0 / (2.0 * math.pi)), reads=[c_b], writes=[c_b])
    nea = cx.sb([128, 2])
    P.op("act", lambda: nc.scalar.activation(out=nea[:, :], in_=ld[:, :], func=AF.Exp), reads=[c_b], writes=[c_b])
    P.op("dve", lambda: nc.vector.tensor_scalar_mul(out=nea[:, :], in0=nea[:, :], scalar1=-1.0), reads=[c_b], writes=[c_b])

    pA = [(cx.ps([128, 512]), Buf()) for _ in range(8)]

    TWO_PI = 2.0 * math.pi
    zs = [(cx.sb([33, 512]), Buf()) for _ in range(4)]
    ts = [(cx.sb([128, 512]), Buf()) for _ in range(4)]
    h1 = [(cx.sb([64, 512]), Buf()) for _ in range(4)]
    h2 = [(cx.sb([64, 512]), Buf()) for _ in range(4)]
    dec = [(cx.sb([128, 512]), Buf()) for _ in range(4)]
    ab = [(cx.sb([128, 512], BF16), Buf()) for _ in range(4)]

    def gen_filter(zT, tb, L, Ad):
        a_b = Buf()
        CW = min(512, L)
        nchunk = (2 * L) // CW
        GI = 4

        def wrap_op(h_s, h_b):
            P.op("dve", lambda: nc.vector.scalar_tensor_tensor(out=h_s[:, 0:CW], in0=h_s[:, 0:CW], scalar=0.5, in1=h_s[:, 0:CW], op0=ALU.is_gt, op1=ALU.subtract), reads=[h_b], writes=[h_b])

        def chunk_steps(ci):
            m0 = ci * CW
            dirn = 1 if (m0 + CW - 1) < L else 0
            sl = ci % GI
            z_s, z_b = zs[sl]; t_s, t_b = ts[sl]
            pp_, ppb = pA[sl]
            h1_s, h1_b = h1[sl]; h2_s, h2_b = h2[sl]; d_s, d_b = dec[sl]; a_s, a_sb = ab[sl]
            steps = []

            def s0():
                P.dma("sp", z_s[:, 0:CW], zT[:, m0:m0 + CW], writes=[z_b])
                P.dma("sp", t_s[:, 0:CW], tb[:, m0:m0 + CW], writes=[t_b])
                _mm(cx, pp_[0:64, 0:CW], w1[:, :], z_s[:, 0:CW], True, True, [z_b, c_b], [ppb])
            steps.append(s0)
            steps.append(lambda: _ts(cx, "dve", h1_s[:, 0:CW], pp_[0:64, 0:CW], b1[:, 0:1], fr2[:, 0:1], ALU.add, ALU.mult, [ppb, c_b], [h1_b]))
            for _ in range(4):
                steps.append(lambda: wrap_op(h1_s, h1_b))
            steps.append(lambda: _act(cx, h1_s[:, 0:CW], h1_s[:, 0:CW], AF.Sin, [h1_b, c_b], [h1_b], scale=TWO_PI))
            steps.append(lambda: _mm(cx, pp_[0:64, 0:CW], w2[:, :], h1_s[:, 0:CW], True, True, [h1_b, c_b], [ppb]))
            steps.append(lambda: _ts(cx, "dve", h2_s[:, 0:CW], pp_[0:64, 0:CW], b2[:, 0:1], fr2[:, 0:1], ALU.add, ALU.mult, [ppb, c_b], [h2_b]))
            for _ in range(4):
                steps.append(lambda: wrap_op(h2_s, h2_b))
            steps.append(lambda: _act(cx, h2_s[:, 0:CW], h2_s[:, 0:CW], AF.Sin, [h2_b, c_b], [h2_b], scale=TWO_PI))
            steps.append(lambda: _mm(cx, pp_[:, 0:CW], w3[:, dirn, :], h2_s[:, 0:CW], True, True, [h2_b, c_b], [ppb]))
            steps.append(lambda: _act(cx, d_s[:, 0:CW], t_s[:, 0:CW], AF.Exp, [t_b, c_b], [d_b], scale=nea[:, dirn:dirn + 1]))
            steps.append(lambda: _tt(cx, "dve", d_s[:, 0:CW], pp_[:, 0:CW], d_s[:, 0:CW], ALU.mult, [ppb, d_b], [d_b]))
            if m0 <= L < m0 + CW:
                o = L - m0
                steps.append(lambda: _tt(cx, "dve", d_s[:, o:o + 1], d_s[:, o:o + 1], sk[:, 0:1], ALU.add, [d_b, c_b], [d_b]))
            else:
                steps.append(lambda: None)

            def s_last():
                _cp(cx, "pool", a_s[:, 0:CW], d_s[:, 0:CW], [d_b], [a_sb])
                P.dma("sp", Ad.ap()[:, m0:m0 + CW], a_s[:, 0:CW], reads=[a_sb], writes=[a_b])
                if HY_DEBUG and L == HY_L:
                    dbb = Buf()
                    P.dma("sp", adbg[:, m0:m0 + CW], d_s[:, 0:CW], reads=[d_b], writes=[dbb])
                    out_bufs.append(dbb)
            steps.append(s_last)
            return steps

        for g0 in range(0, nchunk, GI):
            lists = [chunk_steps(ci) for ci in range(g0, min(g0 + GI, nchunk))]
            for si_ in range(len(lists[0])):
                for l_ in lists:
                    l_[si_]()
        return a_b

    adm_b = gen_filter(zTm, tbm, HY_L, Adm)
    adc_b = gen_filter(zTc, tbc, 256, Adc)

    TWm = 2 * HY_L - 127
    Tm = [(cx.sb([128, TWm], BF16), Buf()) for _ in range(3)]
    Tc = [(cx.sb([128, 512 - 127], BF16), Buf()) for _ in range(3)]

    def path(U, Yo, Ad, ad_b, L, nblk, Tt):
        NBS = 4 * nblk
        TW = 2 * L - 127
        uf = [(cx.sb([128, 3, NBS]), Buf()) for _ in range(HY_NBUF)]
        ubf = [(cx.sb([128, 3, NBS], BF16), Buf()) for _ in range(HY_NBUF)]
        g1 = [(cx.sb([128, NBS]), Buf()) for _ in range(HY_NBUF)]
        g2 = [(cx.sb([128, NBS]), Buf()) for _ in range(HY_NBUF)]
        vr = [(cx.sb([128, NBS], BF16), Buf()) for _ in range(HY_NBUF)]
        z2 = [(cx.sb([128, NBS], BF16), Buf()) for _ in range(HY_NBUF)]
        z2r = [(cx.sb([128, NBS], BF16), Buf()) for _ in range(HY_NBUF)]
        osb = [(cx.sb([128, NBS]), Buf()) for _ in range(HY_NBUF)]
        ti = 0
        Dlist = [0]
        for dd in range(1, nblk):
            Dlist += [dd, -dd]

        def v4(ap_):
            return ap_.rearrange("p (b s) -> p b s", s=nblk)

        def conv(ps, psb, T_s, T_b, x_s, x_b):
            for n_, D in enumerate(Dlist):
                x0 = L - 127 + 128 * D
                s0, s1 = (0, nblk - D) if D >= 0 else (-D, nblk)
                t0, t1 = s0 + D, s1 + D
                P.op("pe", lambda x0=x0, s0=s0, s1=s1, t0=t0, t1=t1, n_=n_: nc.tensor.matmul(
                    out=v4(ps[:, 0:NBS])[:, :, t0:t1], lhsT=T_s[:, x0:x0 + 128], rhs=v4(x_s[:, :])[:, :, s0:s1], start=(n_ == 0), stop=(n_ == len(Dlist) - 1)),
                    reads=[T_b, x_b], writes=[psb])

        def do_channel(c, ti):
            u_s, u_b = uf[c % HY_NBUF]; ub_s, ub_b = ubf[c % HY_NBUF]
            for sg in range(3):
                P.dma("pool", u_s[:, sg, :], U[sg, c, :, :], writes=[u_b])
            P.op("pool", lambda u_s=u_s, ub_s=ub_s: nc.gpsimd.tensor_copy(out=ub_s[:, :, :], in_=u_s[:, :, :]), reads=[u_b], writes=[ub_b])
            T0, T0b = Tt[ti % 3]
            T1, T1b = Tt[(ti + 1) % 3]
            for (T_s, T_b, rc) in ((T0, T0b, c), (T1, T1b, 64 + c)):
                for qi in range(HY_TSPLIT):
                    pn = 128 // HY_TSPLIT
                    src = bass.AP(tensor=Ad, offset=rc * 2 * L + qi * pn, ap=[[1, pn], [1, TW]])
                    P.dma(("sp", "act")[qi % 2], T_s[qi * pn:(qi + 1) * pn, 0:TW], src, reads=[ad_b], writes=[T_b])
            pd, pdb = pA[0]; pu, pub = pA[1]; pvc, pvcb = pA[2]; pvdu, pvdub = pA[3]
            def shift(ps, psb, col0, mat, matB, src_ap, down):
                P.op("pe", lambda: nc.tensor.matmul(out=ps[:, col0:col0 + NBS], lhsT=perm[:, mat, :], rhs=src_ap, start=True, stop=(nblk == 1)), reads=[ub_b, c_b], writes=[psb])
                if nblk > 1:
                    if down:
                        o_ap = v4(ps[:, col0:col0 + NBS])[:, :, 1:nblk]; r_ap = v4(src_ap)[:, :, 0:nblk - 1]
                    else:
                        o_ap = v4(ps[:, col0:col0 + NBS])[:, :, 0:nblk - 1]; r_ap = v4(src_ap)[:, :, 1:nblk]
                    P.op("pe", lambda: nc.tensor.matmul(out=o_ap, lhsT=perm[:, matB, :], rhs=r_ap, start=False, stop=True), reads=[ub_b, c_b], writes=[psb])
            for sg in range(2):
                shift(pd, pdb, sg * NBS, 0, 1, ub_s[:, sg, :], True)
                shift(pu, pub, sg * NBS, 2, 3, ub_s[:, sg, :], False)
            P.op("pe", lambda: nc.tensor.matmul(out=pvc[:, 0:NBS], lhsT=perm[:, 4, :], rhs=ub_s[:, 2, :], start=True, stop=True), reads=[ub_b, c_b], writes=[pvcb])
            shift(pvdu, pvdub, 0, 5, 6, ub_s[:, 2, :], True)
            shift(pvdu, pvdub, NBS, 7, 8, ub_s[:, 2, :], False)
            g1_s, g1_b = g1[c % HY_NBUF]; g2_s, g2_b = g2[c % HY_NBUF]; vr_s, vr_b = vr[c % HY_NBUF]
            for sg, (g_s, g_b) in enumerate(((g1_s, g1_b), (g2_s, g2_b))):
                P.op("dve", lambda sg=sg, g_s=g_s: nc.vector.tensor_scalar_mul(out=g_s[:, :], in0=u_s[:, sg, :], scalar1=wc[:, 1, sg, c:c + 1]), reads=[u_b, c_b], writes=[g_b])
                P.op("dve", lambda sg=sg, g_s=g_s: nc.vector.scalar_tensor_tensor(out=g_s[:, :], in0=pd[:, sg * NBS:(sg + 1) * NBS], scalar=wc[:, 0, sg, c:c + 1], in1=g_s[:, :], op0=ALU.mult, op1=ALU.add), reads=[pdb, c_b, g_b], writes=[g_b])
                P.op("dve", lambda sg=sg, g_s=g_s: nc.vector.scalar_tensor_tensor(out=g_s[:, :], in0=pu[:, sg * NBS:(sg + 1) * NBS], scalar=wc[:, 2, sg, c:c + 1], in1=g_s[:, :], op0=ALU.mult, op1=ALU.add), reads=[pub, c_b, g_b], writes=[g_b])
            vtmp, vtmp_b = osb[c % HY_NBUF]
            P.op("dve", lambda: nc.vector.tensor_scalar_mul(out=vtmp[:, :], in0=pvc[:, 0:NBS], scalar1=wc[:, 1, 2, c:c + 1]), reads=[pvcb, c_b], writes=[vtmp_b])
            P.op("dve", lambda: nc.vector.scalar_tensor_tensor(out=vtmp[:, :], in0=pvdu[:, 0:NBS], scalar=wc[:, 0, 2, c:c + 1], in1=vtmp[:, :], op0=ALU.mult, op1=ALU.add), reads=[pvdub, c_b, vtmp_b], writes=[vtmp_b])
            P.op("dve", lambda: nc.vector.scalar_tensor_tensor(out=vr_s[:, :], in0=pvdu[:, NBS:2 * NBS], scalar=wc[:, 2, 2, c:c + 1], in1=vtmp[:, :], op0=ALU.mult, op1=ALU.add), reads=[pvdub, c_b, vtmp_b], writes=[vr_b])
            py, pyb = pA[4 + (c % HY_NBUF)]
            conv(py, pyb, T0, T0b, vr_s, vr_b)
            z2_s, z2_b = z2[c % HY_NBUF]; z2r_s, z2r_b = z2r[c % HY_NBUF]
            P.op("dve", lambda: nc.vector.tensor_tensor(out=z2_s[:, :], in0=py[:, 0:NBS], in1=g1_s[:, :], op=ALU.mult), reads=[pyb, g1_b], writes=[z2_b])
            pz, pzb = pA[6]
            P.op("pe", lambda: nc.tensor.matmul(out=pz[:, 0:NBS], lhsT=perm[:, 4, :], rhs=z2_s[:, :], start=True, stop=True), reads=[z2_b, c_b], writes=[pzb])
            P.op("act", lambda: nc.scalar.copy(out=z2r_s[:, :], in_=pz[:, 0:NBS]), reads=[pzb], writes=[z2r_b])
            py2, py2b = pA[7]
            conv(py2, py2b, T1, T1b, z2r_s, z2r_b)
            o_s, o_b = osb[c % HY_NBUF]
            P.op("dve", lambda: nc.vector.tensor_tensor(out=o_s[:, :], in0=py2[:, 0:NBS], in1=g2_s[:, :], op=ALU.mult), reads=[py2b, g2_b], writes=[o_b])
            ob = Buf()
            P.dma("pool", Yo[c, :, :], o_s[:, :], reads=[o_b], writes=[ob])
            out_bufs.append(ob)

        for c in range(64):
            do_channel(c, 2 * c)

    path(um, ym, Adm, adm_b, HY_L, NBm, Tm)
    def ctx_path():
        L, nblk, NBS = 256, 2, 8
        W = 64 * NBS
        TW = 2 * L - 127
        u_s = cx.sb([128, 3, W]); u_b = Buf()
        ub_s = cx.sb([128, 3, W], BF16); ub_b = Buf()
        for sg in range(3):
            P.dma("pool", u_s[:, sg, :].rearrange("p (c n) -> p c n", n=NBS), uc[sg, :, :, :].rearrange("c j n -> j c n"), writes=[u_b])
        _cp(cx, "pool", ub_s[:, :, :], u_s[:, :, :], [u_b], [ub_b])
        gg = [(cx.sb([128, W]), Buf()) for _ in range(2)]
        vr_s = cx.sb([128, W], BF16); vr_b = Buf()
        tmp_s = cx.sb([128, W]); tmp_bb = Buf()

        def v3(ap_):
            return ap_.rearrange("p (cb s) -> p cb s", s=nblk)

        def wbc(tap, sg):
            return wc[:, tap, sg, :].unsqueeze(2).to_broadcast([128, 64, NBS])

        def c3(ap_):
            return ap_.rearrange("p (c n) -> p c n", n=NBS)

        def shift(ps, psb, mat, matB, src_ap, down):
            _mm(cx, ps[:, 0:W], perm[:, mat, :], src_ap, True, False, [ub_b, c_b], [psb])
            if down:
                o_ap = v3(ps[:, 0:W])[:, :, 1:nblk]; r_ap = v3(src_ap)[:, :, 0:nblk - 1]
            else:
                o_ap = v3(ps[:, 0:W])[:, :, 0:nblk - 1]; r_ap = v3(src_ap)[:, :, 1:nblk]
            _mm(cx, o_ap, perm[:, matB, :], r_ap, False, True, [ub_b, c_b], [psb])

        pa, pab = pA[0]; pb, pbb = pA[1]; pc_, pcb = pA[2]
        for sg in range(2):
            g_s, g_b = gg[sg]
            shift(pa, pab, 0, 1, ub_s[:, sg, :], True)
            shift(pb, pbb, 2, 3, ub_s[:, sg, :], False)
            _tt(cx, "dve", c3(g_s[:, :]), c3(u_s[:, sg, :]), wbc(1, sg), ALU.mult, [u_b, c_b], [g_b])
            _tt(cx, "dve", c3(tmp_s[:, :]), c3(pa[:, 0:W]), wbc(0, sg), ALU.mult, [pab, c_b], [tmp_bb])
            _tt(cx, "dve", g_s[:, :], g_s[:, :], tmp_s[:, :], ALU.add, [g_b, tmp_bb], [g_b])
            _tt(cx, "dve", c3(tmp_s[:, :]), c3(pb[:, 0:W]), wbc(2, sg), ALU.mult, [pbb, c_b], [tmp_bb])
            _tt(cx, "dve", g_s[:, :], g_s[:, :], tmp_s[:, :], ALU.add, [g_b, tmp_bb], [g_b])
        _mm(cx, pc_[:, 0:W], perm[:, 4, :], ub_s[:, 2, :], True, True, [ub_b, c_b], [pcb])
        shift(pa, pab, 5, 6, ub_s[:, 2, :], True)
        shift(pb, pbb, 7, 8, ub_s[:, 2, :], False)
        acc_s = cx.sb([128, W]); acc_bb = Buf()
        _tt(cx, "dve", c3(acc_s[:, :]), c3(pc_[:, 0:W]), wbc(1, 2), ALU.mult, [pcb, c_b], [acc_bb])
        _tt(cx, "dve", c3(tmp_s[:, :]), c3(pa[:, 0:W]), wbc(0, 2), ALU.mult, [pab, c_b], [tmp_bb])
        _tt(cx, "dve", acc_s[:, :], acc_s[:, :], tmp_s[:, :], ALU.add, [acc_bb, tmp_bb], [acc_bb])
        _tt(cx, "dve", c3(tmp_s[:, :]), c3(pb[:, 0:W]), wbc(2, 2), ALU.mult, [pbb, c_b], [tmp_bb])
        _tt(cx, "dve", vr_s[:, :], acc_s[:, :], tmp_s[:, :], ALU.add, [acc_bb, tmp_bb], [vr_b])

        def load_T(order, halfi, T_s, T_b):
            rc0 = order * 64 + halfi * 32
            src = bass.AP(tensor=Adc, offset=rc0 * 2 * L, ap=[[1, 128], [2 * L, 32], [1, TW]])
            P.dma("sp", T_s[:, 0:32 * TW].rearrange("p (c x) -> p c x", x=TW), src, reads=[adc_b], writes=[T_b])

        def conv_all(ps, psb, order, x_s, x_b, Tbufs):
            for halfi in range(2):
                T_s, T_b = Tbufs[halfi]
                load_T(order, halfi, T_s, T_b)
                Tv = T_s[:, 0:32 * TW].rearrange("p (c x) -> p c x", x=TW)
                for cl in range(32):
                    c = halfi * 32 + cl
                    for n_, D in enumerate((0, 1, -1)):
                        x0 = L - 127 + 128 * D
                        s0, s1 = (0, nblk - D) if D >= 0 else (-D, nblk)
                        t0, t1 = s0 + D, s1 + D
                        xin = x_s[:, c * NBS:(c + 1) * NBS].rearrange("p (b s) -> p b s", s=nblk)[:, :, s0:s1]
                        oo = ps[:, c * NBS:(c + 1) * NBS].rearrange("p (b s) -> p b s", s=nblk)[:, :, t0:t1]
                        _mm(cx, oo, Tv[:, cl, x0:x0 + 128], xin, n_ == 0, n_ == 2, [T_b, x_b], [psb])

        py, pyb = pA[3]
        conv_all(py, pyb, 0, vr_s, vr_b, [Tm[0], Tm[1]])
        z2_s = cx.sb([128, W], BF16); z2_b = Buf()
        z2r_s = cx.sb([128, W], BF16); z2r_b = Buf()
        _tt(cx, "dve", z2_s[:, :], py[:, 0:W], gg[0][0][:, :], ALU.mult, [pyb, gg[0][1]], [z2_b])
        pz, pzb = pA[4]
        _mm(cx, pz[:, 0:W], perm[:, 4, :], z2_s[:, :], True, True, [z2_b, c_b], [pzb])
        _cp(cx, "act", z2r_s[:, :], pz[:, 0:W], [pzb], [z2r_b])
        py2, py2b = pA[5]
        conv_all(py2, py2b, 1, z2r_s, z2r_b, [Tm[2], Tm[0]])
        o_s = cx.sb([128, W]); o_b = Buf()
        _tt(cx, "dve", o_s[:, :], py2[:, 0:W], gg[1][0][:, :], ALU.mult, [py2b, gg[1][1]], [o_b])
        ob = Buf()
        P.dma("pool", yc[:, :, :].rearrange("c j n -> j c n"), o_s[:, :].rearrange("p (c n) -> p c n", n=NBS), reads=[o_b], writes=[ob])
        out_bufs.append(ob)

    ctx_path()
    P.fence("sp", out_bufs)
    return cx


def hy_inputs(u0, uc0, cg, I):
    c = np.ascontiguousarray
    def lay(u, nblk):
        out = []
        for sg in range(3):
            a = u[:, :, sg * 512 + cg * 64: sg * 512 + cg * 64 + 64].reshape(4, nblk, 128, 64)
            out.append(a.transpose(3, 2, 0, 1).reshape(64, 128, 4 * nblk))
        return c(np.stack(out, 0))
    zTm, tbm = hy_lag_tables(HY_L)
    zTc, tbc = hy_lag_tables(256)
    cw = I["hy_conv_w"][0]
    wc = np.stack([np.stack([cw[tap, sg * 512 + cg * 64: sg * 512 + cg * 64 + 64] for sg in range(3)], 0) for tap in range(3)], 0)
    wc = c(np.broadcast_to(wc[None], (128, 3, 3, 64))).astype(np.float32)
    w3 = I["hy_w3"][0].reshape(64, 2, 2, 512)[:, :, :, cg * 64:cg * 64 + 64]
    w3 = c(w3.transpose(0, 2, 1, 3).reshape(64, 2, 128))
    ld = I["hy_log_decay"][0].reshape(2, 2, 512)[:, :, cg * 64:cg * 64 + 64]
    ld = c(ld.transpose(0, 2, 1).reshape(128, 2))
    sk = c(I["hy_skip"][0][:, cg * 64:cg * 64 + 64].reshape(128, 1))
    return dict(um=lay(u0, 64), uc=lay(uc0, 2), wc=wc, perm=hy_perm_mats(), zTm=zTm, tbm=tbm, zTc=zTc, tbc=tbc,
                w1=c(I["hy_w1"][0]), w2=c(I["hy_w2"][0]), b1=c(I["hy_b1"][0].reshape(64, 1)), b2=c(I["hy_b2"][0].reshape(64, 1)),
                freq=c(I["hy_freq"][0].reshape(64, 1)), w3=w3, ld=ld, skip=sk)


def hy_unlayout(ys, nblk):
    out = np.empty((4, nblk * 128, 512), np.float32)
    for cg, y in enumerate(ys):
        a = y.reshape(64, 128, 4, nblk).transpose(2, 3, 1, 0).reshape(4, nblk * 128, 64)
        out[:, :, cg * 64:(cg + 1) * 64] = a
    return out


def _mm(cx, out, lhsT, rhs, start, stop, reads, writes):
    nc = cx.nc
    cx.P.op("pe", lambda: nc.tensor.matmul(out=out, lhsT=lhsT, rhs=rhs, start=start, stop=stop), reads=reads, writes=writes)


def _tr(cx, out, in_, ident, reads, writes):
    nc = cx.nc
    cx.P.op("pe", lambda: nc.tensor.transpose(out=out, in_=in_, identity=ident), reads=reads, writes=writes)


def _act(cx, out, in_, func, reads, writes, bias=None, scale=None, accum_out=None):
    nc = cx.nc
    kw = {}
    if bias is not None:
        kw["bias"] = bias
    if scale is not None:
        kw["scale"] = scale
    if accum_out is not None:
        kw["accum_out"] = accum_out
    cx.P.op("act", lambda: nc.scalar.activation(out=out, in_=in_, func=func, **kw), reads=reads, writes=writes)


def _ts(cx, eng, out, in0, s1, s2, op0, op1, reads, writes):
    e = cx.P.engs[eng]
    if op1 is None:
        cx.P.op(eng, lambda: e.tensor_scalar(out=out, in0=in0, scalar1=s1, scalar2=None, op0=op0), reads=reads, writes=writes)
    else:
        cx.P.op(eng, lambda: e.tensor_scalar(out=out, in0=in0, scalar1=s1, scalar2=s2, op0=op0, op1=op1), reads=reads, writes=writes)


def _tt(cx, eng, out, in0, in1, op, reads, writes):
    e = cx.P.engs[eng]
    cx.P.op(eng, lambda: e.tensor_tensor(out=out, in0=in0, in1=in1, op=op), reads=reads, writes=writes)


def _cp(cx, eng, out, in_, reads, writes):
    e = cx.P.engs[eng]
    if eng == "act":
        cx.P.op(eng, lambda: e.copy(out=out, in_=in_), reads=reads, writes=writes)
    else:
        cx.P.op(eng, lambda: e.tensor_copy(out=out, in_=in_), reads=reads, writes=writes)


def _emit_rms_rows(cx, x_s, x_b, n, xh_s, xh_b, junk, ss, tmp_b):
    nc = cx.nc
    _act(cx, junk[:, 0:n], x_s[:, 0:n], AF.Square, [x_b], [tmp_b], accum_out=ss[:, 0:1])
    _ts(cx, "dve", ss[:, 0:1], ss[:, 0:1], 1.0 / n, LN_EPS, ALU.mult, ALU.add, [tmp_b], [tmp_b])
    _act(cx, ss[:, 0:1], ss[:, 0:1], AF.Sqrt, [tmp_b], [tmp_b])
    cx.P.op("dve", lambda: nc.vector.reciprocal(out=ss[:, 0:1], in_=ss[:, 0:1]), reads=[tmp_b], writes=[tmp_b])
    _ts(cx, "dve", xh_s[:, 0:n], x_s[:, 0:n], ss[:, 0:1], None, ALU.mult, None, [x_b, tmp_b], [xh_b])


MLA_NQ = 4096
MLA_NK = 8448


def build_stage_mla():
    cx = Ctx()
    nc, P = cx.nc, cx.P
    NQT, NKT = MLA_NQ // 128, MLA_NK // 128
    scale = 96 ** -0.5
    cq = cx.din("cq", [NQT, 128, 384])
    ckv = cx.din("ckv", [NKT, 128, 256])
    kpeT = cx.din("kpeT", [32, MLA_NK]); kpeTp = cx.din("kpeTp", [32, MLA_NK])
    cosk = cx.din("cosk", [32, MLA_NK]); sink = cx.din("sink", [32, MLA_NK])
    cosq = cx.din("cosq", [32, MLA_NQ], BF16); sinq = cx.din("sinq", [32, MLA_NQ], BF16)
    wuq = cx.din("wuq", [384, 768]); wuqp = cx.din("wuqp", [384, 256])
    gq = cx.din("gq", [128, 3]); gkv = cx.din("gkv", [128, 2])
    wukv = cx.din("wukv", [256, 1280])
    ident_d = cx.din("ident", [128, 128], BF16)
    y = cx.dout("y", [NQT, 128, 768])

    ident = cx.sb([128, 128], BF16); c_b = Buf()
    P.dma("sp", ident[:, :], ident_d[:, :], writes=[c_b])
    identf = cx.sb([128, 128], F32)
    P.op("pool", lambda: nc.gpsimd.tensor_copy(out=identf[:, :], in_=ident[:, :]), reads=[c_b], writes=[c_b])
    gq_s = cx.sb([128, 3]); gkv_s = cx.sb([128, 2])
    P.dma("sp", gq_s[:, :], gq[:, :], writes=[c_b]); P.dma("sp", gkv_s[:, :], gkv[:, :], writes=[c_b])
    ones = cx.sb([128, 128], BF16)
    P.op("pool", lambda: nc.gpsimd.memset(ones[:, :], 1.0), writes=[c_b])
    stg = [(cx.sb([128, 2048]), Buf()) for _ in range(2)]
    si = [0]

    def stage():
        t = stg[si[0] % 2]; si[0] += 1
        return t

    wqb = cx.sb([128, 3, 768], BF16); wqpb = cx.sb([128, 3, 8, 96], BF16); wkvb = cx.sb([128, 2, 1280], BF16); w_b = Buf()
    P.op("pool", lambda: nc.gpsimd.memset(wqpb[:, :, :, :], 0.0), writes=[w_b])
    for k in range(3):
        st_s, st_b = stage()
        P.dma("sp", st_s[:, 0:768], wuq[k * 128:(k + 1) * 128, :], writes=[st_b])
        P.dma("sp", st_s[:, 768:1024], wuqp[k * 128:(k + 1) * 128, :], writes=[st_b])
        _ts(cx, "pool", wqb[:, k, :], st_s[:, 0:768], gq_s[:, k:k + 1], None, ALU.mult, None, [st_b, c_b], [w_b])
        _ts(cx, "pool", wqpb[:, k, :, 64:96], st_s[:, 768:1024].rearrange("p (h d) -> p h d", d=32), gq_s[:, k:k + 1], None, ALU.mult, None, [st_b, c_b], [w_b])
    for k in range(2):
        st_s, st_b = stage()
        P.dma("sp", st_s[:, 0:1280], wukv[k * 128:(k + 1) * 128, :], writes=[st_b])
        _ts(cx, "pool", wkvb[:, k, :], st_s[:, 0:1280], gkv_s[:, k:k + 1], None, ALU.mult, None, [st_b, c_b], [w_b])

    cosq_s = cx.sb([96, MLA_NQ], BF16); sinq_s = cx.sb([96, MLA_NQ], BF16); tq_b = Buf()
    P.dma("sp", cosq_s[64:96, :], cosq[:, :], writes=[tq_b]); P.dma("sp", sinq_s[64:96, :], sinq[:, :], writes=[tq_b])
    KT = cx.sb([96, MLA_NK], BF16); kt_b = Buf()
    ktmp = [(cx.sb([96, 4, 1056]), Buf()) for _ in range(1)]
    for c0 in range(0, MLA_NK, 1056):
        k_s, k_b = ktmp[0]
        for i_, src in enumerate((kpeT, cosk, kpeTp, sink)):
            P.dma("sp", k_s[64:96, i_, :], src[:, c0:c0 + 1056], writes=[k_b])
        _tt(cx, "dve", k_s[64:96, 0, :], k_s[64:96, 0, :], k_s[64:96, 1, :], ALU.mult, [k_b], [k_b])
        _tt(cx, "dve", k_s[64:96, 2, :], k_s[64:96, 2, :], k_s[64:96, 3, :], ALU.mult, [k_b], [k_b])
        _tt(cx, "dve", KT[64:96, c0:c0 + 1056], k_s[64:96, 0, :], k_s[64:96, 2, :], ALU.add, [k_b], [kt_b])

    qnT = cx.sb([128, 3, MLA_NQ], BF16); qn_b = Buf()
    kvnT = cx.sb([128, 2, MLA_NK], BF16); kvn_b = Buf()
    xs = [(cx.sb([128, 384]), Buf()) for _ in range(2)]
    xh = [(cx.sb([128, 384], BF16), Buf()) for _ in range(2)]
    junk = cx.sb([128, 384]); ss = cx.sb([128, 1]); tmp_b = Buf()
    NPS = 3
    PS2 = [cx.ps([128, 1024]) for _ in range(NPS)]
    ps2_bufs = [[Buf(), Buf()] for _ in range(NPS)]
    pOt = cx.ps([128, 512]); pO_b = Buf()
    _pp = (cx.ps([128, 512]), Buf())
    prepP = [_pp, _pp, _pp]
    for (src, ntile, ncol, dst, dst_b) in ((cq, NQT, 384, qnT, qn_b), (ckv, NKT, 256, kvnT, kvn_b)):
        nk = ncol // 128
        for t in range(ntile):
            x_s, x_b = xs[t % 2]; xh_s, xh_b = xh[t % 2]
            P.dma("sp", x_s[:, 0:ncol], src[t, :, :], writes=[x_b])
            _emit_rms_rows(cx, x_s, x_b, ncol, xh_s, xh_b, junk, ss, tmp_b)
            pt_s, pt_b = prepP[t % 2]
            ptv = pt_s[:, :].bitcast(BF16)
            for k in range(nk):
                _tr(cx, ptv[:, k * 128:(k + 1) * 128], xh_s[:, k * 128:(k + 1) * 128], ident[:, :], [xh_b, c_b], [pt_b])
            _cp(cx, "act" if t % 2 == 0 else "dve", dst[:, 0:nk, t * 128:(t + 1) * 128], ptv[:, 0:nk * 128].rearrange("p (k q) -> p k q", q=128), [pt_b], [dst_b])

    NQC = MLA_NQ // 512
    QT = [cx.sb([96, MLA_NQ], BF16) for _ in range(2)]
    qt_bufs = [[Buf() for _ in range(NQC)] for _ in range(2)]
    nkc = (MLA_NK + 511) // 512
    kt_bufs = [Buf() for _ in range(nkc)]
    NVG = (NKT + 4) // 5
    V = [cx.sb([128, NKT, 97], BF16) for _ in range(2)]
    v1_b = Buf()
    v_bufs = [[Buf() for _ in range(NVG)] for _ in range(2)]
    for i_ in range(2):
        P.op("pool", lambda i_=i_: nc.gpsimd.memset(V[i_][:, :, 96:97], 1.0), writes=[v1_b])
    sq = [(cx.sb([96, 512], BF16), Buf()) for _ in range(3)]
    accq = [cx.sb([128, 32]) for _ in range(2)]; acc_b = [Buf(), Buf()]
    negM = [cx.sb([128, 1]) for _ in range(2)]; nm_b = [Buf(), Buf()]
    rope_t = [(cx.sb([96, 2, 512]), Buf()) for _ in range(2)]
    pt = [(cx.sb([128, 1024], BF16), Buf()) for _ in range(3)]
    rinv = cx.sb([128, 4]); rinv_b = Buf()
    ysb = [(cx.sb([128, 96]), Buf()) for _ in range(4)]
    oT_s = cx.sb([97, 512]); oT_b = Buf()
    pTr, pTr_b = prepP[2]
    out_bufs = []
    NPR = NKT // 2
    sqi = [0]

    def unit_qa(h, ci):
        hb = h % 2
        cs = slice(ci * 512, (ci + 1) * 512)
        pq, pqb = prepP[0]; pp, ppb = prepP[1]
        r_s, r_b = rope_t[ci % 2]
        for k in range(3):
            _mm(cx, pq[0:96, :], wqb[:, k, h * 96:(h + 1) * 96], qnT[:, k, cs], k == 0, k == 2, [w_b, qn_b], [pqb])
        _cp(cx, "dve", QT[hb][0:64, cs], pq[0:64, :], [pqb], [qt_bufs[hb][ci]])
        _tt(cx, "dve", r_s[64:96, 0, :], pq[64:96, :], cosq_s[64:96, cs], ALU.mult, [pqb, tq_b], [r_b])
        for k in range(3):
            _mm(cx, pp[0:96, :], wqpb[:, k, h, :], qnT[:, k, cs], k == 0, k == 2, [w_b, qn_b], [ppb])
        _tt(cx, "dve", r_s[64:96, 1, :], pp[64:96, :], sinq_s[64:96, cs], ALU.mult, [ppb, tq_b], [r_b])
        _tt(cx, "dve", QT[hb][64:96, cs], r_s[64:96, 0, :], r_s[64:96, 1, :], ALU.add, [r_b], [qt_bufs[hb][ci]])

    def unit_va(h, gi):
        hb = h % 2
        g0 = gi * 5
        gn = min(5, NKT - g0)
        pv, pvb = prepP[2]
        for t in range(gn):
            for k in range(2):
                _mm(cx, pv[:, t * 96:(t + 1) * 96], kvnT[:, k, (g0 + t) * 128:(g0 + t + 1) * 128], wkvb[:, k, h * 160 + 64:h * 160 + 160], k == 0, k == 1, [w_b, kvn_b], [pvb])
        _cp(cx, "dve", V[hb][:, g0:g0 + gn, 0:96], pv[:, 0:gn * 96].rearrange("p (t d) -> p t d", d=96), [pvb], [v_bufs[hb][gi]])

    def unit_qb(h, ci):
        hb = h % 2
        cs = slice(ci * 512, (ci + 1) * 512)
        s_s, s_b = sq[sqi[0] % 3]; sqi[0] += 1
        _act(cx, s_s[:, :], QT[hb][:, cs], AF.Square, [qt_bufs[hb][ci]], [s_b])
        pm, pmb = prepP[2]
        _mm(cx, pm[:, :], ones[0:96, :], s_s[:, :], True, True, [s_b, c_b], [pmb])
        P.op("dve", lambda: nc.vector.reduce_max(out=accq[hb][:, ci:ci + 1], in_=pm[:, :], axis=AX.X), reads=[pmb], writes=[acc_b[hb]])

    def units_early(h):
        return ([lambda ci=ci: unit_qa(h, ci) for ci in range(NQC)] + [lambda gi=gi: unit_va(h, gi) for gi in range(NVG)]
                + [lambda ci=ci: unit_qb(h, ci) for ci in range(NQC)])

    def prep_k(h):
        hb = h % 2
        for ci in range(nkc):
            c0 = ci * 512; cw = min(512, MLA_NK - c0)
            pk, pkb = prepP[ci % 3]
            for k in range(2):
                _mm(cx, pk[0:64, 0:cw], wkvb[:, k, h * 160:h * 160 + 64], kvnT[:, k, c0:c0 + cw], k == 0, k == 1, [w_b, kvn_b], [pkb])
            _cp(cx, "act" if ci % 2 == 0 else "dve", KT[0:64, c0:c0 + cw], pk[0:64, 0:cw], [pkb], [kt_bufs[ci]])
        for ci in range(nkc):
            c0 = ci * 512; cw = min(512, MLA_NK - c0)
            s_s, s_b = sq[sqi[0] % 3]; sqi[0] += 1
            _act(cx, s_s[:, 0:cw], KT[:, c0:c0 + cw], AF.Square, [kt_bufs[ci], kt_b], [s_b])
            pm, pmb = prepP[ci % 3]
            _mm(cx, pm[:, 0:cw], ones[0:96, :], s_s[:, 0:cw], True, True, [s_b, c_b], [pmb])
            P.op("dve", lambda pm=pm, ci=ci, cw=cw: nc.vector.reduce_max(out=accq[hb][:, 8 + ci:9 + ci], in_=pm[:, 0:cw], axis=AX.X), reads=[pmb], writes=[acc_b[hb]])
        P.op("dve", lambda: nc.vector.reduce_max(out=accq[hb][:, 30:31], in_=accq[hb][:, 0:8], axis=AX.X), reads=[acc_b[hb]], writes=[acc_b[hb]])
        P.op("dve", lambda: nc.vector.reduce_max(out=accq[hb][:, 31:32], in_=accq[hb][:, 8:8 + nkc], axis=AX.X), reads=[acc_b[hb]], writes=[acc_b[hb]])
        _tt(cx, "dve", negM[hb][:, :], accq[hb][:, 30:31], accq[hb][:, 31:32], ALU.mult, [acc_b[hb]], [nm_b[hb]])
        _act(cx, negM[hb][:, :], negM[hb][:, :], AF.Sqrt, [nm_b[hb]], [nm_b[hb]])
        _ts(cx, "dve", negM[hb][:, :], negM[hb][:, :], -scale, None, ALU.mult, None, [nm_b[hb]], [nm_b[hb]])

    def attention(h, pending):
        hb = h % 2

        def emit_S(st, pr, nn):
            qs = slice(st * 512, (st + 1) * 512)
            for hf in range(2):
                kc = 2 * pr + hf
                _mm(cx, PS2[nn % NPS][:, hf * 512:(hf + 1) * 512], KT[:, kc * 128:(kc + 1) * 128], QT[hb][:, qs], True, True,
                    [kt_bufs[kc // 4], kt_b, qt_bufs[hb][st]], [ps2_bufs[nn % NPS][hf]])
        seqs = [(st, pr) for st in range(NQC) for pr in range(NPR)]
        stride = max(1, (len(seqs) - 8) // max(1, len(pending)))
        emit_S(*seqs[0], 0)
        emit_S(*seqs[1], 1)
        for n_, (st, pr) in enumerate(seqs):
            if n_ + 2 < len(seqs):
                emit_S(*seqs[n_ + 2], n_ + 2)
            p_s, p_b = pt[n_ % 3]
            _act(cx, p_s[:, :], PS2[n_ % NPS][:, :], AF.Exp, ps2_bufs[n_ % NPS] + [nm_b[hb]], [p_b], bias=negM[hb][:, 0:1], scale=scale)
            for hf in range(2):
                kc = 2 * pr + hf
                _mm(cx, pOt[0:97, :], V[hb][:, kc, :], p_s[:, hf * 512:(hf + 1) * 512], kc == 0, kc == NKT - 1, [p_b, v_bufs[hb][kc // 5], v1_b], [pO_b])
            if pending and n_ % stride == stride // 2:
                pending.pop(0)()
            if pr == NPR - 1:
                _cp(cx, "dve", oT_s[:, :], pOt[0:97, :], [pO_b], [oT_b])
                for sub in range(4):
                    _tr(cx, pTr[:, sub * 97:(sub + 1) * 97], oT_s[:, sub * 128:(sub + 1) * 128], identf[0:97, 0:97], [oT_b, c_b], [pTr_b])
                for sub in range(4):
                    y_s, y_b = ysb[sub]
                    P.op("dve", lambda sub=sub: nc.vector.reciprocal(out=rinv[:, sub:sub + 1], in_=pTr[:, sub * 97 + 96:sub * 97 + 97]), reads=[pTr_b], writes=[rinv_b])
                    _ts(cx, "dve", y_s[:, :], pTr[:, sub * 97:sub * 97 + 96], rinv[:, sub:sub + 1], None, ALU.mult, None, [pTr_b, rinv_b], [y_b])
                    ob = Buf()
                    P.dma("pool", y[st * 4 + sub, :, h * 96:(h + 1) * 96], y_s[:, :], reads=[y_b], writes=[ob])
                    out_bufs.append(ob)
        while pending:
            pending.pop(0)()

    for u_ in units_early(0):
        u_()
    prep_k(0)
    for h in range(8):
        pending = units_early(h + 1) if h + 1 < 8 else []
        attention(h, pending)
        if h + 1 < 8:
            prep_k(h + 1)
    P.fence("sp", out_bufs)
    return cx


ROPE_PERM = np.concatenate([np.arange(8, 16), np.arange(0, 8), np.arange(24, 32), np.arange(16, 24)])
ROPE_SIGN = np.concatenate([-np.ones(8), np.ones(8), -np.ones(8), np.ones(8)]).astype(np.float32)


def rope_tables(L):
    t = np.arange(L)
    rows = (t // 64).astype(np.float32); cols = (t % 64).astype(np.float32)
    half = 16
    inv = (np.float32(10000.0) ** (-np.arange(0, half, 2, dtype=np.float32) / np.float32(half))).astype(np.float32)
    ar = rows[:, None] * inv[None, :]; ac = cols[:, None] * inv[None, :]
    ang = np.concatenate([ar, ar, ac, ac], -1)
    return np.cos(ang).astype(np.float32), np.sin(ang).astype(np.float32)


def mla_inputs(u1_b, kvc_b, half, I):
    c = np.ascontiguousarray
    cos, sin = rope_tables(8192)
    cosT = cos.T; sinS = (sin * ROPE_SIGN[None, :]).T
    qs = slice(half * 4096, half * 4096 + 4096)
    cq = u1_b[qs, 0:384].reshape(32, 128, 384)
    ckv = np.concatenate([u1_b[:, 384:640], kvc_b[:, 0:256]], 0).reshape(66, 128, 256)
    kpe = np.concatenate([u1_b[:, 640:672], kvc_b[:, 256:288]], 0)
    cosk = np.concatenate([cosT, np.ones((32, 256), np.float32)], 1)
    sink = np.concatenate([sinS, np.zeros((32, 256), np.float32)], 1)
    wuq = I["mla_w_uq"][0]
    wuqp = wuq.reshape(384, 8, 96)[:, :, 64:96][:, :, ROPE_PERM].reshape(384, 256)
    return dict(cq=c(cq), ckv=c(ckv), kpeT=c(kpe.T), kpeTp=c(kpe[:, ROPE_PERM].T), cosk=c(cosk), sink=c(sink),
                cosq=c(cosT[:, qs]).astype(NPBF), sinq=c(sinS[:, qs]).astype(NPBF), wuq=c(wuq), wuqp=c(wuqp),
                gq=c(I["mla_q_norm"][0].reshape(3, 128).T), gkv=c(I["mla_kv_norm"][0].reshape(2, 128).T),
                wukv=c(I["mla_w_ukv"][0]), ident=np.eye(128, dtype=NPBF))


FN_KT = 17


def build_stage_fn():
    cx = Ctx()
    nc, P = cx.nc, cx.P
    ufn = cx.din("ufn", [64, 128, 256])
    gd = cx.din("g", [128, 256]); bd = cx.din("b", [128, 256])
    c64d = cx.din("c64", [128, 128], BF16); s64d = cx.din("s64", [128, 128], BF16)
    cmd = cx.din("cm", [FN_KT, 128, 64, 128], BF16); smd = cx.din("sm", [FN_KT, 128, 64, 128], BF16)
    ident_d = cx.din("ident", [128, 128], BF16)
    y = cx.dout("y", [FN_KT, 2, 128, 256])
    ident = cx.sb([128, 128], BF16); c64 = cx.sb([128, 128], BF16); s64 = cx.sb([128, 128], BF16); g_s = cx.sb([128, 256]); b_s = cx.sb([128, 256]); c_b = Buf()
    for (d_, s_) in ((ident, ident_d), (c64, c64d), (s64, s64d), (g_s, gd), (b_s, bd)):
        P.dma("sp", d_[:, :], s_[:, :], writes=[c_b])
    PQ = cx.sb([128, 64, 512], BF16); pq_bufs = [Buf() for _ in range(64)]
    xs = [(cx.sb([128, 256]), Buf()) for _ in range(2)]
    xn = [(cx.sb([128, 256]), Buf()) for _ in range(2)]
    xg = [(cx.sb([128, 256], BF16), Buf()) for _ in range(2)]
    xT = [(cx.sb([128, 2, 128], BF16), Buf()) for _ in range(2)]
    st = cx.sb([128, 4, 6]); mv = cx.sb([128, 4, 2]); rstd = cx.sb([128, 4]); tmp_b = Buf()
    pA = [(cx.ps([128, 512]), Buf()) for _ in range(8)]
    def part1(t):
        x_s, x_b = xs[t % 2]; n_s, n_b = xn[t % 2]; g_t, g_b = xg[t % 2]
        P.dma("sp", x_s[:, :], ufn[t, :, :], writes=[x_b])
        for gi in range(4):
            P.op("dve", lambda gi=gi, x_s=x_s: nc.vector.bn_stats(out=st[:, gi, :], in_=x_s[:, gi * 64:(gi + 1) * 64]), reads=[x_b], writes=[tmp_b])
        for gi in range(4):
            P.op("dve", lambda gi=gi: nc.vector.bn_aggr(out=mv[:, gi, :], in_=st[:, gi:gi + 1, :]), reads=[tmp_b], writes=[tmp_b])
        _ts(cx, "dve", rstd[:, :], mv[:, :, 1], LN_EPS, None, ALU.add, None, [tmp_b], [tmp_b])
        _act(cx, rstd[:, :], rstd[:, :], AF.Sqrt, [tmp_b], [tmp_b])
        P.op("dve", lambda: nc.vector.reciprocal(out=rstd[:, :], in_=rstd[:, :]), reads=[tmp_b], writes=[tmp_b])
        for gi in range(4):
            _ts(cx, "dve", n_s[:, gi * 64:(gi + 1) * 64], x_s[:, gi * 64:(gi + 1) * 64], mv[:, gi, 0:1], rstd[:, gi:gi + 1], ALU.subtract, ALU.mult, [x_b, tmp_b], [n_b])
        _tt(cx, "pool", n_s[:, :], n_s[:, :], g_s[:, :], ALU.mult, [n_b, c_b], [n_b])
        _tt(cx, "pool", g_t[:, :], n_s[:, :], b_s[:, :], ALU.add, [n_b, c_b], [g_b])

    def part2(t):
        g_t, g_b = xg[t % 2]; t_s, t_b = xT[t % 2]
        pt_s, pt_b = pA[4 + (t % 2)]
        ptv = pt_s[:, :].bitcast(BF16)
        for k in range(2):
            _tr(cx, ptv[:, k * 128:(k + 1) * 128], g_t[:, k * 128:(k + 1) * 128], ident[:, :], [g_b, c_b], [pt_b])
        _cp(cx, "act", t_s[:, :, :], ptv[:, 0:256].rearrange("p (k q) -> p k q", q=128), [pt_b], [t_b])
        pp_s, pp_b = pA[6 + (t % 2)]
        for k in range(2):
            _mm(cx, pp_s[:, k * 128:(k + 1) * 128], t_s[:, k, :], c64[:, :], True, True, [t_b, c_b], [pp_b])
            _mm(cx, pp_s[:, 256 + k * 128:256 + (k + 1) * 128], t_s[:, k, :], s64[:, :], True, True, [t_b, c_b], [pp_b])
        _cp(cx, "act", PQ[:, t, 0:256], pp_s[:, 0:256], [pp_b], [pq_bufs[t]])
        _ts(cx, "dve", PQ[:, t, 256:512], pp_s[:, 256:512], -1.0, None, ALU.mult, None, [pp_b], [pq_bufs[t]])

    part1(0)
    for t in range(64):
        if t + 1 < 64:
            part1(t + 1)
        part2(t)
    ck = [(cx.sb([128, 64, 128], BF16), Buf()) for _ in range(2)]
    sk = [(cx.sb([128, 64, 128], BF16), Buf()) for _ in range(2)]
    ysb = [(cx.sb([128, 2, 256]), Buf()) for _ in range(2)]
    tmp2 = [(cx.sb([128, 256]), Buf()) for _ in range(2)]
    out_bufs = []
    for kt in range(FN_KT):
        c_s, cb = ck[kt % 2]; s_s, sb_ = sk[kt % 2]
        P.dma("sp", c_s[:, :, :], cmd[kt, :, :, :], writes=[cb])
        P.dma("pool", s_s[:, :, :], smd[kt, :, :, :], writes=[sb_])
        accC, accC_b = pA[2 * (kt % 2)]
        accS, accS_b = pA[2 * (kt % 2) + 1]
        for nt in range(64):
            _mm(cx, accC[:, 0:256], c_s[:, nt, :], PQ[:, nt, 0:256], nt == 0, nt == 63, [cb, pq_bufs[nt]], [accC_b])
            _mm(cx, accS[:, 0:256], s_s[:, nt, :], PQ[:, nt, 256:512], nt == 0, nt == 63, [sb_, pq_bufs[nt]], [accS_b])
        y_s, y_b = ysb[kt % 2]
        t_s, t_b = tmp2[kt % 2]
        _cp(cx, "act", t_s[:, :], accS[:, 0:256], [accS_b], [t_b])
        _tt(cx, "dve", y_s[:, 0, :], accC[:, 0:256], t_s[:, :], ALU.add, [accC_b, t_b], [y_b])
        _tt(cx, "dve", y_s[:, 1, :], accC[:, 0:256], t_s[:, :], ALU.subtract, [accC_b, t_b], [y_b])
        ob = Buf()
        P.dma("sp", y[kt, :, :, :].rearrange("a p c -> p a c"), y_s[:, :, :], reads=[y_b], writes=[ob])
        out_bufs.append(ob)
    P.fence("sp", out_bufs)
    return cx


_FN_CONST = {}


def fn_klist(half):
    base = half * 2048 + np.arange(2048, dtype=np.int64)
    extra = np.full(128, 0 if half == 0 else 4096, np.int64)
    return np.concatenate([base, extra])


def fn_consts(half):
    if half in _FN_CONST:
        return _FN_CONST[half]
    L = 8192
    n = np.arange(L, dtype=np.int64)
    k = fn_klist(half)
    idx = (n[:, None] * k[None, :]) % L
    ang = idx.astype(np.float64) * (2.0 * math.pi / L)
    sc = 1.0 / math.sqrt(L)

    def tile(m):
        return np.ascontiguousarray(m.reshape(64, 128, FN_KT, 128).transpose(2, 1, 0, 3)).astype(NPBF)
    cm = tile((np.cos(ang) * sc).astype(np.float32)); sm = tile((np.sin(ang) * sc).astype(np.float32))
    c = np.arange(64)
    a64 = (np.outer(c, c) % 64) * (2.0 * math.pi / 64)
    c64 = np.zeros((128, 128), np.float32); s64 = np.zeros((128, 128), np.float32)
    for g in range(2):
        c64[g * 64:(g + 1) * 64, g * 64:(g + 1) * 64] = np.cos(a64) / 8.0
        s64[g * 64:(g + 1) * 64, g * 64:(g + 1) * 64] = np.sin(a64) / 8.0
    _FN_CONST[half] = (cm, sm, c64.astype(NPBF), s64.astype(NPBF))
    return _FN_CONST[half]


def fn_scatter(yfull_b, ycore, half):
    L = 8192
    k = fn_klist(half)
    yk = ycore[:, 0].reshape(-1, 256); ym = ycore[:, 1].reshape(-1, 256)
    nvalid = 2048 + 1
    kk = k[:nvalid]
    yfull_b[kk] = yk[:nvalid]
    yfull_b[(L - kk) % L] = ym[:nvalid]


def _lay8(v):
    return np.ascontiguousarray(v.reshape(8, 128).T)


def _bc(v):
    return np.ascontiguousarray(np.broadcast_to(v[None, :], (128, v.shape[0]))).astype(np.float32)


def _tiles(a):
    return a.reshape(-1, 128, a.shape[-1])


def kernel(**I):
    I = {k: np.asarray(v) for k, v in I.items()}
    c = np.ascontiguousarray
    x, ctx = I["x"], I["ctx"]
    ident = np.eye(128, dtype=NPBF)
    cx = build_stage_mod()
    call = np.zeros((8, 1024), np.float32)
    call[0:4] = I["c"]; call[4] = I["c_ctx"]
    cT = c(call.reshape(8, 8, 128).transpose(2, 1, 0))
    ims = []
    for i in range(8):
        l, q = i // 4, i % 4
        ims.append(dict(cT=cT, w=c(I["mod_w"][l][:, q * 1536:(q + 1) * 1536]), bias=c(np.broadcast_to(I["mod_b"][l][None, q * 1536:(q + 1) * 1536], (8, 1536)))))
    res = run_spmd(cx, ims)
    m_all = [np.concatenate([res[l * 4 + q]["m"] for q in range(4)], 1) for l in range(2)]

    def modv(l, j):
        return m_all[l][0:4, j * 1024:(j + 1) * 1024], m_all[l][4, j * 1024:(j + 1) * 1024]

    def mod_pair(l, jsc, jsh, b, with_ctx):
        scl, scc = modv(l, jsc); shl, shc = modv(l, jsh)
        if with_ctx:
            return c(np.stack([_lay8(scl[b]), _lay8(scc)], 1)), c(np.stack([_lay8(shl[b]), _lay8(shc)], 1))
        return c(_lay8(scl[b])[:, None, :]), c(_lay8(shl[b])[:, None, :])

    def gate(l, j, b, with_ctx):
        gl, gc = modv(l, j)
        if with_ctx:
            return c(np.stack([_bc(gl[b]), _bc(gc)], 1))
        return c(_bc(gl[b])[:, None, :])

    g34 = [0] * 32 + [1] * 2
    g32 = [0] * 32

    def tok_tiles(lat, cx_arr, b, half):
        t = _tiles(lat[b, half * 4096:(half + 1) * 4096])
        if cx_arr is None:
            return c(t)
        return c(np.concatenate([t, _tiles(cx_arr[b])], 0))

    cx = build_stage_proj(34, g34, 3072)
    ims = []
    for i in range(8):
        b, half = i // 2, i % 2
        msc, msh = mod_pair(0, 1, 0, b, True)
        ims.append(dict(xt=tok_tiles(x, ctx, b, half), msc=msc, msh=msh, w=c(I["ab_w_in"][0]), ident=ident))
    res = run_spmd(cx, ims)
    u0 = np.empty((4, 8192, 3072), np.float32); uc0 = np.empty((4, 256, 3072), np.float32)
    for i in range(8):
        b, half = i // 2, i % 2
        u0[b, half * 4096:(half + 1) * 4096] = res[i]["u"][:32].reshape(4096, 3072)
        if half == 0:
            uc0[b] = res[i]["u"][32:].reshape(256, 3072)
    cx = build_stage_na(na_items())
    res = run_spmd(cx, [na_inputs(u0[i // 2], uc0[i // 2], I["na_rpb"][0], i % 2) for i in range(8)])
    y0 = np.empty((4, 8192, 1024), np.float32); yc0 = np.empty((4, 256, 1024), np.float32)
    for i in range(8):
        b, half = i // 2, i % 2
        y0[b, half * 4096:(half + 1) * 4096, 512:] = res[i]["y"][:64].reshape(4096, 512)
        yc0[b, half * 128:(half + 1) * 128, 512:] = res[i]["y"][64:].reshape(128, 512)
    cx = build_stage_hy()
    res = run_spmd(cx, [hy_inputs(u0, uc0, cg, I) for cg in range(8)])
    y0[:, :, :512] = hy_unlayout([r["ym"] for r in res], 64)
    yc0[:, :, :512] = hy_unlayout([r["yc"] for r in res], 2)
    del u0
    cx = build_stage_mix(34, g34)
    ims = []
    for i in range(8):
        b, half = i // 2, i % 2
        ims.append(dict(yt=tok_tiles(y0, yc0, b, half), xt=tok_tiles(x, ctx, b, half), w=c(I["ab_w_out"][0]), gv=gate(0, 2, b, True),
                        lng=_bc(I["ln_g"][0, 0]), lnb=_bc(I["ln_b"][0, 0]), ident=ident))
    res = run_spmd(cx, ims)
    xa = [r["xo"] for r in res]
    cx = build_stage_mlp(34, g34)
    ims = []
    for i in range(8):
        b, half = i // 2, i % 2
        msc, msh = mod_pair(0, 4, 3, b, True)
        ims.append(dict(xt=c(xa[i]), msc=msc, msh=msh, w1=c(I["mlp_w1"][0]), w2=c(I["mlp_w2"][0]), gv=gate(0, 5, b, True),
                        lng=_bc(I["ln_g"][0, 1]), lnb=_bc(I["ln_b"][0, 1]), ident=ident))
    res = run_spmd(cx, ims)
    xl0 = [r["xo"] for r in res]
    cx = build_stage_proj(34, g34, 928)
    ims = []
    for i in range(8):
        b, half = i // 2, i % 2
        msc, msh = mod_pair(1, 1, 0, b, True)
        ims.append(dict(xt=c(xl0[i]), msc=msc, msh=msh, w=c(I["cd_w_in"][0]), ident=ident))
    res = run_spmd(cx, ims)
    u1 = np.empty((4, 8192, 928), np.float32); kvc = np.empty((4, 256, 288), np.float32)
    for i in range(8):
        b, half = i // 2, i % 2
        u1[b, half * 4096:(half + 1) * 4096] = res[i]["u"][:32].reshape(4096, 928)
        if half == 0:
            kvc[b] = res[i]["u"][32:].reshape(256, 928)[:, 384:672]
    cx = build_stage_mla()
    res = run_spmd(cx, [mla_inputs(u1[i // 2], kvc[i // 2], i % 2, I) for i in range(8)])
    y1 = [np.empty((32, 128, 1024), np.float32) for _ in range(8)]
    for i in range(8):
        y1[i][:, :, :768] = res[i]["y"]
    cx = build_stage_fn()
    ims = []
    for i in range(8):
        b, half = i // 2, i % 2
        cm, sm, c64, s64 = fn_consts(half)
        ims.append(dict(ufn=c(u1[b, :, 672:928].reshape(64, 128, 256)), g=_bc(I["fn_norm_g"][0]), b=_bc(I["fn_norm_b"][0]), c64=c64, s64=s64, cm=cm, sm=sm, ident=ident))
    res = run_spmd(cx, ims)
    yfn = np.empty((4, 8192, 256), np.float32)
    for i in range(8):
        fn_scatter(yfn[i // 2], res[i]["y"], i % 2)
    for i in range(8):
        b, half = i // 2, i % 2
        y1[i][:, :, 768:] = yfn[b, half * 4096:(half + 1) * 4096].reshape(32, 128, 256)
    cx = build_stage_mix(32, g32)
    ims = []
    for i in range(8):
        b, half = i // 2, i % 2
        ims.append(dict(yt=y1[i], xt=c(xl0[i][:32]), w=c(I["cd_w_out"][0]), gv=gate(1, 2, b, False),
                        lng=_bc(I["ln_g"][1, 0]), lnb=_bc(I["ln_b"][1, 0]), ident=ident))
    res = run_spmd(cx, ims)
    xa = [r["xo"] for r in res]
    cx = build_stage_mlp(32, g32)
    ims = []
    for i in range(8):
        b, half = i // 2, i % 2
        msc, msh = mod_pair(1, 4, 3, b, False)
        ims.append(dict(xt=c(xa[i]), msc=msc, msh=msh, w1=c(I["mlp_w1"][1]), w2=c(I["mlp_w2"][1]), gv=gate(1, 5, b, False),
                        lng=_bc(I["ln_g"][1, 1]), lnb=_bc(I["ln_b"][1, 1]), ident=ident))
    res = run_spmd(cx, ims)
    out = np.empty((4, 8192, 1024), np.float32)
    for i in range(8):
        b, half = i // 2, i % 2
        out[b, half * 4096:(half + 1) * 4096] = res[i]["xo"].reshape(4096, 1024)
    return out
```

```python
import math
import numpy as np
import ml_dtypes
import concourse.bass as bass
import concourse.mybir as mybir
from concourse.bass_utils import run_bass_kernel_spmd

F32 = mybir.dt.float32
BF16 = mybir.dt.bfloat16
AF = mybir.ActivationFunctionType
ALU = mybir.AluOpType
AX = mybir.AxisListType
NPBF = ml_dtypes.bfloat16

NCORES = 8
N_DMA_SEMS = 24


class Buf:
    __slots__ = ("name", "w", "r")

    def __init__(self, name=""):
        self.name = name
        self.w = None
        self.r = {}


class _Op:
    __slots__ = ("eng", "fn", "waits", "signaled", "key", "seq", "is_dma", "count")


class Prog:
    def __init__(self, nc):
        self.nc = nc
        self.engs = {"pe": nc.tensor, "act": nc.scalar, "dve": nc.vector, "pool": nc.gpsimd, "sp": nc.sync}
        self.ops = []
        self.known = {e: {} for e in self.engs}
        self.nseq = {}
        self.dma_rr = 0
        self.dma_last = {}

    def _deps(self, E, reads, writes):
        deps = []
        for b in reads:
            if b.w is not None:
                deps.append(b.w)
        for b in writes:
            if b.w is not None:
                deps.append(b.w)
            deps.extend(b.r.values())
        return deps

    def op(self, E, fn, reads=(), writes=(), dma=False):
        o = _Op()
        o.eng = E
        o.fn = fn
        o.is_dma = dma
        o.signaled = dma
        known = self.known[E]
        waits = {}
        for (key, seq, clk, prod) in self._deps(E, reads, writes):
            if known.get(key, 0) >= seq:
                continue
            if key == "pe" and E == "pe" and not dma:
                continue
            if waits.get(key, (0, None))[0] < seq:
                waits[key] = (seq, prod)
            for k2, s2 in clk.items():
                if known.get(k2, 0) < s2:
                    known[k2] = s2
            known[key] = max(known.get(key, 0), seq)
        if dma:
            j = self.dma_rr
            self.dma_rr = (self.dma_rr + 1) % N_DMA_SEMS
            key = ("dma", j)
            prev = self.dma_last.get(j)
            if prev is not None and known.get(key, 0) < prev.seq:
                if waits.get(key, (0, None))[0] < prev.seq:
                    waits[key] = (prev.seq, prev)
                known[key] = prev.seq
            self.dma_last[j] = o
        else:
            key = E
        for k, (s, prod) in waits.items():
            prod.signaled = True
        o.waits = [(k, prod) for k, (s, prod) in waits.items()]
        seq = self.nseq.get(key, 0) + 1
        self.nseq[key] = seq
        o.key = key
        o.seq = seq
        clk = dict(known)
        clk[key] = seq
        ent = (key, seq, clk, o)
        for b in writes:
            b.w = ent
            b.r = {}
        for b in reads:
            if b in writes:
                continue
            b.r[key] = ent
        self.ops.append(o)
        return o

    def dma(self, q, out, in_, reads=(), writes=()):
        q = "sp"
        eng = self.engs[q]
        return self.op(q, lambda: eng.dma_start(out=out, in_=in_), reads, writes, dma=True)

    def fence(self, q, bufs):
        return self.op(q, None, reads=bufs)

    def emit(self):
        nc = self.nc
        sems = {}
        counts = {}

        def sem_of(key):
            if key not in sems:
                nm = key if isinstance(key, str) else f"dma{key[1]}"
                sems[key] = nc.alloc_semaphore("s_" + nm)
            return sems[key]

        for o in self.ops:
            eng = self.engs[o.eng]
            for (k, prod) in o.waits:
                eng.wait_ge(sem_of(k), prod.count)
            inst = o.fn() if o.fn is not None else None
            if o.signaled and inst is not None:
                step = 16 if o.is_dma else 1
                c = counts.get(o.key, 0) + step
                counts[o.key] = c
                o.count = c
                inst.then_inc(sem_of(o.key), step)
        self.ops = []


def new_nc():
    return bass.Bass("TRN2", target_bir_lowering=False)


class Ctx:
    def __init__(self):
        self.nc = new_nc()
        self.P = Prog(self.nc)
        self._n = 0

    def din(self, name, shape, dt=F32):
        return self.nc.dram_tensor(name, list(shape), dt, kind="ExternalInput").ap()

    def dout(self, name, shape, dt=F32):
        return self.nc.dram_tensor(name, list(shape), dt, kind="ExternalOutput").ap()

    def sb(self, shape, dt=F32, name=None):
        self._n += 1
        return self.nc.alloc_sbuf_tensor(name or f"sb{self._n}", list(shape), dt)

    def ps(self, shape, dt=F32, name=None):
        self._n += 1
        return self.nc.alloc_psum_tensor(name or f"ps{self._n}", list(shape), dt)


def run_spmd(cx, in_maps):
    cx.P.emit()
    res = run_bass_kernel_spmd(cx.nc, in_maps, core_ids=list(range(len(in_maps))))
    t_ns = getattr(res, "exec_time_ns", None)
    if t_ns:
        print(f"[stage] exec_ns={t_ns}", flush=True)
    return res.results


LN_EPS = 1e-5


def emit_ln_rows(cx, xs, xs_b, xh, xh_b, st, mv, rstd, tmp_b, ncols=1024):
    P, nc = cx.P, cx.nc
    nch = ncols // 512 if ncols >= 512 else 1
    w = min(512, ncols)
    for i in range(nch):
        P.op("dve", lambda i=i: nc.vector.bn_stats(out=st[:, i, :], in_=xs[:, i * w:(i + 1) * w]),
             reads=[xs_b], writes=[tmp_b] if i == 0 else [tmp_b])
    P.op("dve", lambda: nc.vector.bn_aggr(out=mv[:, :], in_=st[:, 0:nch, :]), reads=[tmp_b], writes=[tmp_b])
    P.op("dve", lambda: nc.vector.tensor_scalar_add(out=rstd[:, :], in0=mv[:, 1:2], scalar1=LN_EPS), reads=[tmp_b], writes=[tmp_b])
    P.op("act", lambda: nc.scalar.activation(out=rstd[:, :], in_=rstd[:, :], func=AF.Sqrt), reads=[tmp_b], writes=[tmp_b])
    P.op("dve", lambda: nc.vector.reciprocal(out=rstd[:, :], in_=rstd[:, :]), reads=[tmp_b], writes=[tmp_b])
    P.op("dve", lambda: nc.vector.tensor_scalar(out=xh[:, 0:ncols], in0=xs[:, 0:ncols], scalar1=mv[:, 0:1],
                                                scalar2=rstd[:, 0:1], op0=ALU.subtract, op1=ALU.mult),
         reads=[xs_b, tmp_b], writes=[xh_b])


def load_weight_bf16(cx, w_dram, K, N, wb, wb_bufs, stage_tiles, q="sp", cast_eng="pool", col_chunk=512):
    P, nc = cx.P, cx.nc
    kc = K // 128
    i = 0
    for k in range(kc):
        for c0 in range(0, N, col_chunk):
            cw = min(col_chunk, N - c0)
            stg, stg_b = stage_tiles[i % len(stage_tiles)]
            i += 1
            P.dma(q, stg[:, 0:cw], w_dram[k * 128:(k + 1) * 128, c0:c0 + cw], writes=[stg_b])
            eng = cx.P.engs[cast_eng]
            P.op(cast_eng, lambda eng=eng, k=k, c0=c0, cw=cw, stg=stg: eng.tensor_copy(out=wb[:, k, c0:c0 + cw], in_=stg[:, 0:cw]),
                 reads=[stg_b], writes=[wb_bufs[k][c0 // col_chunk]])


def build_stage_proj(NT, groups, NOUT, out_dt=F32):
    cx = Ctx()
    nc, P = cx.nc, cx.P
    G = max(groups) + 1
    xt = cx.din("xt", [NT, 128, 1024])
    msc = cx.din("msc", [128, G, 8])
    msh = cx.din("msh", [128, G, 8])
    w = cx.din("w", [1024, NOUT])
    ident_d = cx.din("ident", [128, 128], BF16)
    u = cx.dout("u", [NT, 128, NOUT], out_dt)

    ident = cx.sb([128, 128], BF16); ident_b = Buf()
    sc1 = cx.sb([128, G, 8]); sh1 = cx.sb([128, G, 8]); mod_b = Buf()
    P.dma("sp", ident[:, :], ident_d[:, :], writes=[ident_b])
    P.dma("sp", sc1[:, :, :], msc[:, :, :], writes=[mod_b])
    P.dma("sp", sh1[:, :, :], msh[:, :, :], writes=[mod_b])
    P.op("dve", lambda: nc.vector.tensor_scalar_add(out=sc1[:, :, :], in0=sc1[:, :, :], scalar1=1.0), reads=[mod_b], writes=[mod_b])

    wb = cx.sb([128, 8, NOUT], BF16)
    nchunk = (NOUT + 511) // 512
    wb_bufs = [[Buf() for _ in range(nchunk)] for _ in range(8)]
    stages = [(cx.sb([128, 512]), Buf()) for _ in range(3)]
    load_weight_bf16(cx, w, 1024, NOUT, wb, wb_bufs, stages, q="pool")

    NB = 3
    xs = [(cx.sb([128, 1024]), Buf()) for _ in range(NB)]
    xh = [(cx.sb([128, 1024], BF16), Buf()) for _ in range(2)]
    hT = [(cx.sb([128, 8, 128], BF16), Buf()) for _ in range(2)]
    st = cx.sb([128, 2, 6]); mv = cx.sb([128, 2]); rstd = cx.sb([128, 1]); tmp_b = Buf()
    pT = [(cx.ps([128, 1024], BF16), Buf()) for _ in range(2)]
    pO = [(cx.ps([128, 512]), Buf()) for _ in range(4)]
    ot = [(cx.sb([128, NOUT], out_dt), Buf()) for _ in range(2)]
    out_bufs = []
    po_cnt = [0]

    def prologue(t):
        g = groups[t]
        x_s, x_b = xs[t % NB]
        P.dma("sp", x_s[:, :], xt[t, :, :], writes=[x_b])
        xh_s, xh_b = xh[t % 2]
        emit_ln_rows(cx, x_s, x_b, xh_s, xh_b, st, mv, rstd, tmp_b)
        pt_s, pt_b = pT[t % 2]
        for k in range(8):
            _tr(cx, pt_s[:, k * 128:(k + 1) * 128], xh_s[:, k * 128:(k + 1) * 128], ident[:, :], [xh_b, ident_b], [pt_b])
        h_s, h_b = hT[t % 2]
        for k in range(8):
            _act(cx, h_s[:, k, :], pt_s[:, k * 128:(k + 1) * 128], AF.Identity, [pt_b, mod_b], [h_b], bias=sh1[:, g, k:k + 1], scale=sc1[:, g, k:k + 1])

    def main(t):
        h_s, h_b = hT[t % 2]
        o_s, o_b = ot[t % 2]
        for c in range(nchunk):
            c0 = c * 512
            cw = min(512, NOUT - c0)
            po_s, po_b = pO[po_cnt[0] % 4]
            po_cnt[0] += 1
            for k in range(8):
                _mm(cx, po_s[:, 0:cw], h_s[:, k, :], wb[:, k, c0:c0 + cw], k == 0, k == 7, [h_b, wb_bufs[k][c]], [po_b])
            _cp(cx, "dve" if c % 2 == 0 else "act", o_s[:, c0:c0 + cw], po_s[:, 0:cw], [po_b], [o_b])
        ob = Buf()
        P.dma("pool", u[t, :, :], o_s[:, :], reads=[o_b], writes=[ob])
        out_bufs.append(ob)

    prologue(0)
    for t in range(NT):
        if t + 1 < NT:
            prologue(t + 1)
        main(t)
    P.fence("sp", out_bufs)
    return cx


ALPHA = (2.0 * 2) ** 0.25


def emit_epilogue(cx, pos, x_s, x_b, gv, lng, lnb, vec_b, r_s, r_b, o_s, o_b, st, mv, rstd, tmp_b):
    P, nc = cx.P, cx.nc
    for (po_s, po_b, c0, cw) in pos:
        P.op("dve", lambda po_s=po_s, c0=c0, cw=cw: nc.vector.tensor_tensor(out=r_s[:, c0:c0 + cw], in0=po_s[:, 0:cw], in1=gv[:, c0:c0 + cw], op=ALU.mult),
             reads=[po_b, vec_b], writes=[r_b])
    P.op("dve", lambda: nc.vector.scalar_tensor_tensor(out=r_s[:, :], in0=x_s[:, :], scalar=ALPHA, in1=r_s[:, :], op0=ALU.mult, op1=ALU.add),
         reads=[x_b, r_b], writes=[r_b])
    for i in range(2):
        P.op("dve", lambda i=i: nc.vector.bn_stats(out=st[:, i, :], in_=r_s[:, i * 512:(i + 1) * 512]), reads=[r_b], writes=[tmp_b])
    P.op("dve", lambda: nc.vector.bn_aggr(out=mv[:, :], in_=st[:, 0:2, :]), reads=[tmp_b], writes=[tmp_b])
    P.op("dve", lambda: nc.vector.tensor_scalar_add(out=rstd[:, :], in0=mv[:, 1:2], scalar1=LN_EPS), reads=[tmp_b], writes=[tmp_b])
    P.op("act", lambda: nc.scalar.activation(out=rstd[:, :], in_=rstd[:, :], func=AF.Sqrt), reads=[tmp_b], writes=[tmp_b])
    P.op("dve", lambda: nc.vector.reciprocal(out=rstd[:, :], in_=rstd[:, :]), reads=[tmp_b], writes=[tmp_b])
    P.op("dve", lambda: nc.vector.tensor_scalar(out=r_s[:, :], in0=r_s[:, :], scalar1=mv[:, 0:1], scalar2=rstd[:, 0:1], op0=ALU.subtract, op1=ALU.mult),
         reads=[r_b, tmp_b], writes=[r_b])
    P.op("pool", lambda: nc.gpsimd.tensor_tensor(out=o_s[:, :], in0=r_s[:, :], in1=lng[:, :], op=ALU.mult), reads=[r_b, vec_b], writes=[o_b])
    P.op("pool", lambda: nc.gpsimd.tensor_tensor(out=o_s[:, :], in0=o_s[:, :], in1=lnb[:, :], op=ALU.add), reads=[o_b, vec_b], writes=[o_b])


def build_stage_mix(NT, groups):
    cx = Ctx()
    nc, P = cx.nc, cx.P
    G = max(groups) + 1
    yt = cx.din("yt", [NT, 128, 1024])
    xt = cx.din("xt", [NT, 128, 1024])
    w = cx.din("w", [1024, 1024])
    gvd = cx.din("gv", [128, G, 1024])
    lngd = cx.din("lng", [128, 1024])
    lnbd = cx.din("lnb", [128, 1024])
    ident_d = cx.din("ident", [128, 128], BF16)
    xo = cx.dout("xo", [NT, 128, 1024])

    ident = cx.sb([128, 128], BF16); ident_b = Buf()
    P.dma("sp", ident[:, :], ident_d[:, :], writes=[ident_b])
    gv = cx.sb([128, G, 1024]); lng = cx.sb([128, 1024]); lnb = cx.sb([128, 1024]); vec_b = Buf()
    P.dma("sp", gv[:, :, :], gvd[:, :, :], writes=[vec_b])
    P.dma("sp", lng[:, :], lngd[:, :], writes=[vec_b])
    P.dma("sp", lnb[:, :], lnbd[:, :], writes=[vec_b])
    wb = cx.sb([128, 8, 1024], BF16)
    wb_bufs = [[Buf() for _ in range(2)] for _ in range(8)]
    stages = [(cx.sb([128, 512]), Buf()) for _ in range(3)]
    load_weight_bf16(cx, w, 1024, 1024, wb, wb_bufs, stages, q="pool")

    ys = [(cx.sb([128, 1024]), Buf()) for _ in range(3)]
    xs = [(cx.sb([128, 1024]), Buf()) for _ in range(3)]
    yh = [(cx.sb([128, 1024], BF16), Buf()) for _ in range(2)]
    yT = [(cx.sb([128, 8, 128], BF16), Buf()) for _ in range(2)]
    rs = [(cx.sb([128, 1024]), Buf()) for _ in range(2)]
    os_ = [(cx.sb([128, 1024]), Buf()) for _ in range(2)]
    st = cx.sb([128, 2, 6]); mv = cx.sb([128, 2]); rstd = cx.sb([128, 1]); tmp_b = Buf()
    pT = [(cx.ps([128, 1024], BF16), Buf()) for _ in range(2)]
    pO = [(cx.ps([128, 512]), Buf()) for _ in range(4)]
    out_bufs = []

    def prologue(t):
        y_s, y_b = ys[t % 3]; x_s, x_b = xs[t % 3]
        P.dma("sp", y_s[:, :], yt[t, :, :], writes=[y_b])
        P.dma("sp", x_s[:, :], xt[t, :, :], writes=[x_b])
        yh_s, yh_b = yh[t % 2]
        _cp(cx, "act", yh_s[:, :], y_s[:, :], [y_b], [yh_b])
        pt_s, pt_b = pT[t % 2]
        for k in range(8):
            _tr(cx, pt_s[:, k * 128:(k + 1) * 128], yh_s[:, k * 128:(k + 1) * 128], ident[:, :], [yh_b, ident_b], [pt_b])
        yT_s, yT_b = yT[t % 2]
        _cp(cx, "act", yT_s[:, :, :], pt_s[:, :].rearrange("p (k q) -> p k q", q=128), [pt_b], [yT_b])

    def main(t):
        g = groups[t]
        x_s, x_b = xs[t % 3]
        yT_s, yT_b = yT[t % 2]
        pos = []
        for c in range(2):
            po_s, po_b = pO[(2 * t + c) % 4]
            for k in range(8):
                _mm(cx, po_s[:, :], yT_s[:, k, :], wb[:, k, c * 512:(c + 1) * 512], k == 0, k == 7, [yT_b, wb_bufs[k][c]], [po_b])
            pos.append((po_s, po_b, c * 512, 512))
        r_s, r_b = rs[t % 2]; o_s, o_b = os_[t % 2]
        emit_epilogue(cx, pos, x_s, x_b, gv[:, g, :], lng, lnb, vec_b, r_s, r_b, o_s, o_b, st, mv, rstd, tmp_b)
        ob = Buf()
        P.dma("pool", xo[t, :, :], o_s[:, :], reads=[o_b], writes=[ob])
        out_bufs.append(ob)

    prologue(0)
    for t in range(NT):
        if t + 1 < NT:
            prologue(t + 1)
        main(t)
    P.fence("sp", out_bufs)
    return cx


def build_stage_mlp(NT, groups):
    cx = Ctx()
    nc, P = cx.nc, cx.P
    G = max(groups) + 1
    xt = cx.din("xt", [NT, 128, 1024])
    msc = cx.din("msc", [128, G, 8])
    msh = cx.din("msh", [128, G, 8])
    w1 = cx.din("w1", [1024, 4096])
    w2 = cx.din("w2", [4096, 1024])
    gvd = cx.din("gv", [128, G, 1024])
    lngd = cx.din("lng", [128, 1024])
    lnbd = cx.din("lnb", [128, 1024])
    ident_d = cx.din("ident", [128, 128], BF16)
    xo = cx.dout("xo", [NT, 128, 1024])

    ident = cx.sb([128, 128], BF16); ident_b = Buf()
    P.dma("sp", ident[:, :], ident_d[:, :], writes=[ident_b])
    sc1 = cx.sb([128, G, 8]); sh1 = cx.sb([128, G, 8]); mod_b = Buf()
    P.dma("sp", sc1[:, :, :], msc[:, :, :], writes=[mod_b])
    P.dma("sp", sh1[:, :, :], msh[:, :, :], writes=[mod_b])
    P.op("dve", lambda: nc.vector.tensor_scalar_add(out=sc1[:, :, :], in0=sc1[:, :, :], scalar1=1.0), reads=[mod_b], writes=[mod_b])
    gv = cx.sb([128, G, 1024]); lng = cx.sb([128, 1024]); lnb = cx.sb([128, 1024]); vec_b = Buf()
    P.dma("sp", gv[:, :, :], gvd[:, :, :], writes=[vec_b])
    P.dma("sp", lng[:, :], lngd[:, :], writes=[vec_b])
    P.dma("sp", lnb[:, :], lnbd[:, :], writes=[vec_b])
    w1b = cx.sb([128, 8, 4096], BF16)
    w1_bufs = [[Buf() for _ in range(8)] for _ in range(8)]
    w2b = cx.sb([128, 32, 1024], BF16)
    w2_bufs = [[Buf() for _ in range(2)] for _ in range(32)]
    stages = [(cx.sb([128, 512]), Buf()) for _ in range(2)]
    load_weight_bf16(cx, w1, 1024, 4096, w1b, w1_bufs, stages, q="pool")
    load_weight_bf16(cx, w2, 4096, 1024, w2b, w2_bufs, stages, q="pool")

    assert NT % 2 == 0
    xs = [(cx.sb([128, 1024]), Buf()) for _ in range(4)]
    xh = [(cx.sb([128, 1024], BF16), Buf()) for _ in range(2)]
    hT = [cx.sb([128, 8, 256], BF16) for _ in range(2)]
    hT_b = [[Buf(), Buf()] for _ in range(2)]
    aR = [(cx.sb([128, 256], BF16), Buf()) for _ in range(2)]
    aT = cx.sb([128, 32, 256], BF16)
    aT_bufs = [Buf() for _ in range(32)]
    rs = [(cx.sb([128, 1024]), Buf()) for _ in range(1)]
    os_ = [(cx.sb([128, 1024]), Buf()) for _ in range(2)]
    st = cx.sb([128, 2, 6]); mv = cx.sb([128, 2]); rstd = cx.sb([128, 1]); tmp_b = Buf()
    pT = [(cx.ps([128, 1024], BF16), Buf()) for _ in range(2)]
    pA = [(cx.ps([128, 512]), Buf()) for _ in range(2)]
    pO = [(cx.ps([128, 512]), Buf()) for _ in range(4)]
    out_bufs = []
    ai = [0]

    def prologue(t):
        g = groups[t]
        x_s, x_b = xs[t % 4]
        P.dma("sp", x_s[:, :], xt[t, :, :], writes=[x_b])
        xh_s, xh_b = xh[t % 2]
        emit_ln_rows(cx, x_s, x_b, xh_s, xh_b, st, mv, rstd, tmp_b)
        pt_s, pt_b = pT[t % 2]
        for k in range(8):
            _tr(cx, pt_s[:, k * 128:(k + 1) * 128], xh_s[:, k * 128:(k + 1) * 128], ident[:, :], [xh_b, ident_b], [pt_b])
        pr = (t // 2) % 2
        hh = t % 2
        for k in range(8):
            _act(cx, hT[pr][:, k, hh * 128:(hh + 1) * 128], pt_s[:, k * 128:(k + 1) * 128], AF.Identity, [pt_b, mod_b], [hT_b[pr][hh]], bias=sh1[:, g, k:k + 1], scale=sc1[:, g, k:k + 1])

    def main_pair(p):
        pr = p % 2
        h_s = hT[pr]
        for j in range(32):
            pa_s, pa_b = pA[ai[0] % 2]
            ar_s, ar_b = aR[ai[0] % 2]
            ai[0] += 1
            for k in range(8):
                _mm(cx, pa_s[:, 0:256], w1b[:, k, j * 128:(j + 1) * 128], h_s[:, k, :], k == 0, k == 7, hT_b[pr] + [w1_bufs[k][j // 4]], [pa_b])
            _act(cx, ar_s[:, :], pa_s[:, 0:256], AF.Relu, [pa_b], [ar_b])
            _tt(cx, "pool", aT[:, j, :], ar_s[:, :], ar_s[:, :], ALU.mult, [ar_b], [aT_bufs[j]])
        for hh in range(2):
            t = 2 * p + hh
            g = groups[t]
            x_s, x_b = xs[t % 4]
            pos = []
            for c in range(2):
                po_s, po_b = pO[hh * 2 + c]
                for j in range(32):
                    _mm(cx, po_s[:, :], aT[:, j, hh * 128:(hh + 1) * 128], w2b[:, j, c * 512:(c + 1) * 512], j == 0, j == 31, [aT_bufs[j], w2_bufs[j][c]], [po_b])
                pos.append((po_s, po_b, c * 512, 512))
            r_s, r_b = rs[0]; o_s, o_b = os_[t % 2]
            emit_epilogue(cx, pos, x_s, x_b, gv[:, g, :], lng, lnb, vec_b, r_s, r_b, o_s, o_b, st, mv, rstd, tmp_b)
            ob = Buf()
            P.dma("pool", xo[t, :, :], o_s[:, :], reads=[o_b], writes=[ob])
            out_bufs.append(ob)

    prologue(0); prologue(1)
    for p in range(NT // 2):
        if 2 * p + 2 < NT:
            prologue(2 * p + 2); prologue(2 * p + 3)
        main_pair(p)
    P.fence("sp", out_bufs)
    return cx


def build_stage_mod():
    cx = Ctx()
    nc, P = cx.nc, cx.P
    cT = cx.din("cT", [128, 8, 8])
    w = cx.din("w", [1024, 1536])
    bias = cx.din("bias", [8, 1536])
    m = cx.dout("m", [8, 1536])
    c_s = cx.sb([128, 8, 8]); c_b = Buf()
    sg = cx.sb([128, 8, 8])
    P.dma("sp", c_s[:, :, :], cT[:, :, :], writes=[c_b])
    P.op("act", lambda: nc.scalar.activation(out=sg[:, :, :], in_=c_s[:, :, :], func=AF.Sigmoid), reads=[c_b], writes=[c_b])
    P.op("dve", lambda: nc.vector.tensor_tensor(out=c_s[:, :, :], in0=c_s[:, :, :], in1=sg[:, :, :], op=ALU.mult), reads=[c_b], writes=[c_b])
    b_s = cx.sb([8, 1536]); b_b = Buf()
    P.dma("sp", b_s[:, :], bias[:, :], writes=[b_b])
    ws = cx.sb([128, 8, 1536]); w_bufs = [Buf() for _ in range(8)]
    for k in range(8):
        P.dma("sp" if k % 2 == 0 else "pool", ws[:, k, :], w[k * 128:(k + 1) * 128, :], writes=[w_bufs[k]])
    o_s = cx.sb([8, 1536]); o_b = Buf()
    pO = [(cx.ps([8, 512]), Buf()) for _ in range(3)]
    for c in range(3):
        po_s, po_b = pO[c]
        for k in range(8):
            P.op("pe", lambda k=k, c=c, po_s=po_s: nc.tensor.matmul(out=po_s[:, :], lhsT=c_s[:, k, :], rhs=ws[:, k, c * 512:(c + 1) * 512], start=(k == 0), stop=(k == 7)),
                 reads=[c_b, w_bufs[k]], writes=[po_b])
        P.op("dve", lambda c=c, po_s=po_s: nc.vector.tensor_tensor(out=o_s[:, c * 512:(c + 1) * 512], in0=po_s[:, :], in1=b_s[:, c * 512:(c + 1) * 512], op=ALU.add),
             reads=[po_b, b_b], writes=[o_b])
    ob = Buf()
    P.dma("sp", m[:, :], o_s[:, :], reads=[o_b], writes=[ob])
    P.fence("sp", [ob])
    return cx


NA_KROWS = 68
NA_NPAT = 9
NA_WIN = 12


def na_items():
    items = []
    for j in range(64):
        w = min(max(j - 4, 0), 56)
        pid = j if j < 5 else (5 if j <= 60 else j - 55)
        items.append((j * 64, w, pid))
    items.append((64 * 64, None, None))
    items.append((65 * 64, None, None))
    return items


def build_stage_na(items):
    NI = len(items)
    NQ = NI * 64
    NK = NA_KROWS * 64
    WK = NA_WIN * 64
    cx = Ctx()
    nc, P = cx.nc, cx.P
    qT = cx.din("qT", [4, 128, NQ])
    kT = cx.din("kT", [4, 128, NK])
    vv = cx.din("v", [4, 64, NA_KROWS, 128])
    kcT = cx.din("kcT", [4, 128, 256])
    vc = cx.din("vc", [4, 64, 4, 128])
    biasd = cx.din("bias", [4, 128, NA_NPAT, WK])
    y = cx.dout("y", [NI, 64, 512])
    scale = 64 ** -0.5

    stg = [(cx.sb([128, 2048]), Buf()) for _ in range(2)]
    q_s = cx.sb([128, NQ], BF16); q_b = Buf()
    k_s = cx.sb([128, NK], BF16); k_b = Buf()
    kc_s = cx.sb([128, 256], BF16); kc_b = Buf()
    Kbd = cx.sb([128, NA_KROWS, 128], BF16); kbd_b = Buf()
    Kbc = cx.sb([128, 4, 128], BF16); kbc_b = Buf()
    v_s = cx.sb([128, NA_KROWS, 65], BF16); v_b = Buf()
    vc_s = cx.sb([128, 4, 65], BF16); vc_b = Buf()
    b_s = cx.sb([128, NA_NPAT, WK]); b_b = Buf()
    onesbd = cx.sb([128, 128], BF16); c_b = Buf()
    P.op("pool", lambda: nc.gpsimd.memset(onesbd[:, :], 0.0), writes=[c_b])
    P.op("pool", lambda: nc.gpsimd.memset(onesbd[0:64, 0:64], 1.0), writes=[c_b])
    P.op("pool", lambda: nc.gpsimd.memset(onesbd[64:128, 64:128], 1.0), writes=[c_b])
    P.op("pool", lambda: nc.gpsimd.memset(Kbd[:, :, :], 0.0), writes=[kbd_b])
    P.op("pool", lambda: nc.gpsimd.memset(Kbc[:, :, :], 0.0), writes=[kbc_b])
    P.op("pool", lambda: nc.gpsimd.memset(v_s[:, :, 64:65], 1.0), writes=[v_b])
    P.op("pool", lambda: nc.gpsimd.memset(vc_s[:, :, 64:65], 1.0), writes=[vc_b])
    sq = [(cx.sb([128, 512], BF16), Buf()) for _ in range(2)]
    acc = cx.sb([128, 32]); acc_b = Buf()
    negM = cx.sb([128, 1]); nm_b = Buf()
    s_sb = [(cx.sb([128, WK]), Buf()) for _ in range(2)]
    pT_sb = [(cx.sb([128, WK + 256], BF16), Buf()) for _ in range(3)]
    rinv = cx.sb([64, 2]); rinv_b = Buf()
    y_sb = [(cx.sb([64, 128]), Buf()) for _ in range(2)]
    pS = [cx.ps([128, 1024]) for _ in range(2)]
    pS_b = [[Buf(), Buf()] for _ in range(2)]
    pO = [[(cx.ps([64, 512]), Buf()) for _h in range(2)] for _ in range(2)]
    pM = [(pS[i][:, 0:512], pS_b[i][0]) for i in range(2)]
    out_bufs = []
    si = [0]

    def load_cast(dst_ap, src_ap, width, dst_b):
        st_s, st_b = stg[si[0] % 2]
        si[0] += 1
        P.dma("sp", st_s[:, 0:width], src_ap, writes=[st_b])
        _cp(cx, "pool", dst_ap, st_s[:, 0:width], [st_b], [dst_b])

    gi = [0]
    for hp in range(4):
        for c0 in range(0, NQ, 2048):
            cw = min(2048, NQ - c0)
            load_cast(q_s[:, c0:c0 + cw], qT[hp, :, c0:c0 + cw], cw, q_b)
        for c0 in range(0, NK, 2048):
            cw = min(2048, NK - c0)
            load_cast(k_s[:, c0:c0 + cw], kT[hp, :, c0:c0 + cw], cw, k_b)
        load_cast(kc_s[:, :], kcT[hp, :, :], 256, kc_b)
        for hh in range(2):
            ps_ = slice(hh * 64, (hh + 1) * 64)
            _cp(cx, "pool", Kbd[ps_, :, hh * 64:(hh + 1) * 64], k_s[ps_, :].rearrange("p (r c) -> p r c", c=64), [k_b], [kbd_b])
            _cp(cx, "pool", Kbc[ps_, :, hh * 64:(hh + 1) * 64], kc_s[ps_, :].rearrange("p (r c) -> p r c", c=64), [kc_b], [kbc_b])
        for r0 in range(0, NA_KROWS, 16):
            rw = min(16, NA_KROWS - r0)
            st_s, st_b = stg[si[0] % 2]; si[0] += 1
            for hh in range(2):
                P.dma("sp", st_s[hh * 64:(hh + 1) * 64, 0:rw * 128].rearrange("p (r c) -> p r c", c=128), vv[hp, :, r0:r0 + rw, :], writes=[st_b])
            for hh in range(2):
                _cp(cx, "pool", v_s[hh * 64:(hh + 1) * 64, r0:r0 + rw, 0:64],
                    st_s[hh * 64:(hh + 1) * 64, 0:rw * 128].rearrange("p (r c) -> p r c", c=128)[:, :, hh * 64:(hh + 1) * 64], [st_b], [v_b])
        st_s, st_b = stg[si[0] % 2]; si[0] += 1
        for hh in range(2):
            P.dma("sp", st_s[hh * 64:(hh + 1) * 64, 0:512].rearrange("p (r c) -> p r c", c=128), vc[hp, :, :, :], writes=[st_b])
        for hh in range(2):
            _cp(cx, "pool", vc_s[hh * 64:(hh + 1) * 64, :, 0:64],
                st_s[hh * 64:(hh + 1) * 64, 0:512].rearrange("p (r c) -> p r c", c=128)[:, :, hh * 64:(hh + 1) * 64], [st_b], [vc_b])
        P.dma("sp", b_s[:, :, :], biasd[hp, :, :, :], writes=[b_b])
        na_ = 0
        for (src, src_b, n) in ((q_s, q_b, NQ), (k_s, k_b, NK), (kc_s, kc_b, 256)):
            isq = 0 if src is q_s else 1
            for c0 in range(0, n, 512):
                cw = min(512, n - c0)
                s_s, s_bb = sq[na_ % 2]
                _act(cx, s_s[:, 0:cw], src[:, c0:c0 + cw], AF.Square, [src_b], [s_bb])
                pm, pmb = pM[na_ % 2]
                _mm(cx, pm[:, 0:cw], onesbd[:, :], s_s[:, 0:cw], True, True, [s_bb, c_b], [pmb])
                col = na_ if isq == 0 else 10 + (na_ - 9)
                P.op("dve", lambda pm=pm, cw=cw, col=col: nc.vector.reduce_max(out=acc[:, col:col + 1], in_=pm[:, 0:cw], axis=AX.X), reads=[pmb], writes=[acc_b])
                na_ += 1
        nq_ch = (NQ + 511) // 512
        assert nq_ch == 9 and na_ == 9 + 9 + 1
        P.op("dve", lambda: nc.vector.reduce_max(out=acc[:, 30:31], in_=acc[:, 0:9], axis=AX.X), reads=[acc_b], writes=[acc_b])
        P.op("dve", lambda: nc.vector.reduce_max(out=acc[:, 31:32], in_=acc[:, 10:20], axis=AX.X), reads=[acc_b], writes=[acc_b])
        _tt(cx, "dve", negM[:, :], acc[:, 30:31], acc[:, 31:32], ALU.mult, [acc_b], [nm_b])
        _act(cx, negM[:, :], negM[:, :], AF.Sqrt, [nm_b], [nm_b])
        _ts(cx, "dve", negM[:, :], negM[:, :], -scale, None, ALU.mult, None, [nm_b], [nm_b])

        def emit_S(ii, g):
            qtok0, w, pid = items[ii]
            ps = pS[g % 2]; pb = pS_b[g % 2]
            if w is not None:
                for i in range(NA_WIN):
                    bank = 0 if i < 8 else 1
                    _mm(cx, ps[:, i * 64:(i + 1) * 64], Kbd[:, w + i, :], q_s[:, qtok0:qtok0 + 64], True, True, [kbd_b, q_b], [pb[bank]])
            for blk in range(4):
                _mm(cx, ps[:, WK + blk * 64:WK + (blk + 1) * 64], Kbc[:, blk, :], q_s[:, qtok0:qtok0 + 64], True, True, [kbc_b, q_b], [pb[1]])

        emit_S(0, gi[0])
        for ii, (qtok0, w, pid) in enumerate(items):
            g = gi[0]
            if ii + 1 < NI:
                emit_S(ii + 1, g + 1)
            ps = pS[g % 2]; pb = pS_b[g % 2]
            s_s, s_bb = s_sb[g % 2]
            p_s, p_b = pT_sb[g % 3]
            lat = w is not None
            if lat:
                bview = b_s[:, pid, :]
                P.op("dve", lambda ps=ps, s_s=s_s, bview=bview: nc.vector.scalar_tensor_tensor(out=s_s[:, 0:512], in0=ps[:, 0:512], scalar=scale, in1=bview[:, 0:512], op0=ALU.mult, op1=ALU.add),
                     reads=[pb[0], b_b], writes=[s_bb])
                P.op("dve", lambda ps=ps, s_s=s_s, bview=bview: nc.vector.scalar_tensor_tensor(out=s_s[:, 512:WK], in0=ps[:, 512:WK], scalar=scale, in1=bview[:, 512:WK], op0=ALU.mult, op1=ALU.add),
                     reads=[pb[1], b_b], writes=[s_bb])
                _act(cx, p_s[:, 0:WK], s_s[:, :], AF.Exp, [s_bb, nm_b], [p_b], bias=negM[:, 0:1], scale=1.0)
            _act(cx, p_s[:, WK:WK + 256], ps[:, WK:WK + 256], AF.Exp, [pb[1], nm_b], [p_b], bias=negM[:, 0:1], scale=scale)
            ysb_s, ysb_b = y_sb[g % 2]
            for hh in range(2):
                po_s, po_b = pO[g % 2][hh]
                hs = slice(hh * 64, (hh + 1) * 64)
                first = True
                if lat:
                    for i in range(NA_WIN):
                        _mm(cx, po_s[:, 0:65], p_s[hs, i * 64:(i + 1) * 64], v_s[hs, w + i, :], first, False, [p_b, v_b], [po_b])
                        first = False
                for blk in range(4):
                    _mm(cx, po_s[:, 0:65], p_s[hs, WK + blk * 64:WK + (blk + 1) * 64], vc_s[hs, blk, :], first, blk == 3, [p_b, vc_b], [po_b])
                    first = False
            for hh in range(2):
                po_s, po_b = pO[g % 2][hh]
                P.op("dve", lambda po_s=po_s, hh=hh: nc.vector.reciprocal(out=rinv[:, hh:hh + 1], in_=po_s[:, 64:65]), reads=[po_b], writes=[rinv_b])
                _ts(cx, "dve", ysb_s[:, hh * 64:(hh + 1) * 64], po_s[:, 0:64], rinv[:, hh:hh + 1], None, ALU.mult, None, [po_b, rinv_b], [ysb_b])
            ob = Buf()
            P.dma("pool", y[ii, :, hp * 128:(hp + 1) * 128], ysb_s[:, :], reads=[ysb_b], writes=[ob])
            out_bufs.append(ob)
            gi[0] += 1
        gi[0] += 1
    P.fence("sp", out_bufs)
    return cx


def na_bias_tables(rpb, half):
    GW, WR, WC = 64, 8, 16
    R0 = 0 if half == 0 else 60
    qcol = np.arange(64)
    cs = np.clip(qcol - WC // 2, 0, GW - WC)
    kcol = np.arange(64)
    mask = (kcol[None, :] >= cs[:, None]) & (kcol[None, :] < cs[:, None] + WC)
    dc = np.clip(kcol[None, :] - qcol[:, None] + WC - 1, 0, 2 * WC - 2)
    tabs = {}
    for (qtok0, w, pid) in na_items()[:64]:
        j = qtok0 // 64
        r = 64 * half + j
        rs = int(np.clip(r - WR // 2, 0, 128 - WR))
        t = np.full((8, 64, NA_WIN, 64), -30000.0, np.float32)
        for i in range(NA_WIN):
            grow = R0 + w + i
            if rs <= grow < rs + WR:
                dr = grow - r + WR - 1
                vals = rpb[:, dr][:, dc]
                t[:, :, i, :] = np.where(mask[None], vals, np.float32(-30000.0))
        t = t.reshape(8, 64, NA_WIN * 64)
        if pid in tabs:
            assert np.array_equal(tabs[pid], t)
        tabs[pid] = t
    T = np.stack([tabs[p] for p in range(NA_NPAT)], 0)
    T = T.reshape(NA_NPAT, 4, 2, 64, NA_WIN, 64).transpose(1, 2, 5, 0, 4, 3).reshape(4, 128, NA_NPAT, NA_WIN * 64)
    return np.ascontiguousarray(T)


def na_inputs(u_b, uc_b, rpb, half):
    R0 = 0 if half == 0 else 60
    q = u_b[:, 1536:2048].reshape(128, 64, 4, 128)[64 * half:64 * half + 64]
    qc = uc_b[:, 1536:2048].reshape(4, 64, 4, 128)[2 * half:2 * half + 2]
    qT = np.concatenate([q, qc], 0).reshape(-1, 4, 128).transpose(1, 2, 0)
    k = u_b[:, 2048:2560].reshape(128, 64, 4, 128)[R0:R0 + NA_KROWS]
    kT = k.reshape(-1, 4, 128).transpose(1, 2, 0)
    v = u_b[:, 2560:3072].reshape(128, 64, 4, 128)[R0:R0 + NA_KROWS].transpose(2, 1, 0, 3)
    kcT = uc_b[:, 2048:2560].reshape(256, 4, 128).transpose(1, 2, 0)
    vc = uc_b[:, 2560:3072].reshape(4, 64, 4, 128).transpose(2, 1, 0, 3)
    c = np.ascontiguousarray
    return dict(qT=c(qT), kT=c(kT), v=c(v), kcT=c(kcT), vc=c(vc), bias=na_bias_tables(rpb, half))


HY_L = 8192
HY_DEBUG = False
HY_NBUF = 1
HY_TSPLIT = 1


def hy_perm_mats():
    def z():
        return np.zeros((128, 128), np.float32)
    m = {}
    a = z(); a[np.arange(0, 127), np.arange(1, 128)] = 1; m["down"] = a
    a = z(); a[127, 0] = 1; m["downB"] = a
    a = z(); a[np.arange(1, 128), np.arange(0, 127)] = 1; m["up"] = a
    a = z(); a[0, 127] = 1; m["upB"] = a
    a = z(); a[np.arange(128), 127 - np.arange(128)] = 1; m["J"] = a
    for nme in ("down", "downB", "up", "upB"):
        m["J" + nme] = np.ascontiguousarray(m[nme][:, ::-1])
    order = ["down", "downB", "up", "upB", "J", "Jdown", "JdownB", "Jup", "JupB"]
    return np.stack([m[k] for k in order], 1).astype(NPBF)


def hy_lag_tables(L):
    lag = np.arange(2 * L) - L
    pos = np.abs(lag).astype(np.float32)
    t = pos / np.float32(max(L - 1, 1))
    w = np.float32(2.0 * math.pi) * pos / np.float32(L)
    nb = 16
    f = np.linspace(1e-4, nb - 1, nb, dtype=np.float32)
    ang = w[:, None] * f[None, :]
    z = np.concatenate([t[:, None], np.cos(ang), -np.sin(ang)], -1).astype(np.float32)
    zT = np.ascontiguousarray(z.T)
    tb = np.ascontiguousarray(np.broadcast_to(t[None, :], (128, 2 * L))).astype(np.float32)
    return zT, tb


def build_stage_hy():
    cx = Ctx()
    nc, P = cx.nc, cx.P
    NBm, NBc = 64, 2
    um = cx.din("um", [3, 64, 128, 4 * NBm])
    uc = cx.din("uc", [3, 64, 128, 4 * NBc])
    wcd = cx.din("wc", [128, 3, 3, 64])
    permd = cx.din("perm", [128, 9, 128], BF16)
    zTm = cx.din("zTm", [33, 2 * HY_L]); tbm = cx.din("tbm", [128, 2 * HY_L])
    zTc = cx.din("zTc", [33, 512]); tbc = cx.din("tbc", [128, 512])
    w1d = cx.din("w1", [33, 64]); w2d = cx.din("w2", [64, 64])
    b1d = cx.din("b1", [64, 1]); b2d = cx.din("b2", [64, 1]); frd = cx.din("freq", [64, 1])
    w3d = cx.din("w3", [64, 2, 128])
    ldd = cx.din("ld", [128, 2])
    skd = cx.din("skip", [128, 1])
    ym = cx.dout("ym", [64, 128, 4 * NBm])
    yc = cx.dout("yc", [64, 128, 4 * NBc])
    Adm = nc.dram_tensor("Adm", [128, 2 * HY_L], BF16)
    adbg = cx.dout("adbg", [128, 2 * HY_L]) if HY_DEBUG else None
    out_bufs = []
    Adc = nc.dram_tensor("Adc", [128, 512], BF16)

    perm = cx.sb([128, 9, 128], BF16); c_b = Buf()
    wc = cx.sb([128, 3, 3, 64])
    w1 = cx.sb([33, 64]); w2 = cx.sb([64, 64]); b1 = cx.sb([64, 1]); b2 = cx.sb([64, 1]); fr = cx.sb([64, 1])
    w3 = cx.sb([64, 2, 128]); ld = cx.sb([128, 2]); sk = cx.sb([128, 1]); negpi = cx.sb([128, 1])
    for (d_, s_) in ((perm[:, :, :], permd[:, :, :]), (wc[:, :, :, :], wcd[:, :, :, :]), (w1[:, :], w1d[:, :]), (w2[:, :], w2d[:, :]),
                     (b1[:, :], b1d[:, :]), (b2[:, :], b2d[:, :]), (fr[:, :], frd[:, :]), (w3[:, :, :], w3d[:, :, :]), (ld[:, :], ldd[:, :]), (sk[:, :], skd[:, :])):
        P.dma("sp", d_, s_, writes=[c_b])
    P.op("pool", lambda: nc.gpsimd.memset(negpi[:, :], -math.pi), writes=[c_b])
    fr2 = cx.sb([64, 1])
    P.op("dve", lambda: nc.vector.tensor_scalar_mul(out=fr2[:, :], in0=fr[:, :], scalar1=1.0 / (2.0 * math.pi)), reads=[c_b], writes=[c_b])
    nea = cx.sb([128, 2])
    P.op("act", lambda: nc.scalar.activation(out=nea[:, :], in_=ld[:, :], func=AF.Exp), reads=[c_b], writes=[c_b])
    P.op("dve", lambda: nc.vector.tensor_scalar_mul(out=nea[:, :], in0=nea[:, :], scalar1=-1.0), reads=[c_b], writes=[c_b])

    pA = [(cx.ps([128, 512]), Buf()) for _ in range(8)]

    TWO_PI = 2.0 * math.pi
    zs = [(cx.sb([33, 512]), Buf()) for _ in range(4)]
    ts = [(cx.sb([128, 512]), Buf()) for _ in range(4)]
    h1 = [(cx.sb([64, 512]), Buf()) for _ in range(4)]
    h2 = [(cx.sb([64, 512]), Buf()) for _ in range(4)]
    dec = [(cx.sb([128, 512]), Buf()) for _ in range(4)]
    ab = [(cx.sb([128, 512], BF16), Buf()) for _ in range(4)]

    def gen_filter(zT, tb, L, Ad):
        a_b = Buf()
        CW = min(512, L)
        nchunk = (2 * L) // CW
        GI = 4

        def wrap_op(h_s, h_b):
            P.op("dve", lambda: nc.vector.scalar_tensor_tensor(out=h_s[:, 0:CW], in0=h_s[:, 0:CW], scalar=0.5, in1=h_s[:, 0:CW], op0=ALU.is_gt, op1=ALU.subtract), reads=[h_b], writes=[h_b])

        def chunk_steps(ci):
            m0 = ci * CW
            dirn = 1 if (m0 + CW - 1) < L else 0
            sl = ci % GI
            z_s, z_b = zs[sl]; t_s, t_b = ts[sl]
            pp_, ppb = pA[sl]
            h1_s, h1_b = h1[sl]; h2_s, h2_b = h2[sl]; d_s, d_b = dec[sl]; a_s, a_sb = ab[sl]
            steps = []

            def s0():
                P.dma("sp", z_s[:, 0:CW], zT[:, m0:m0 + CW], writes=[z_b])
                P.dma("sp", t_s[:, 0:CW], tb[:, m0:m0 + CW], writes=[t_b])
                _mm(cx, pp_[0:64, 0:CW], w1[:, :], z_s[:, 0:CW], True, True, [z_b, c_b], [ppb])
            steps.append(s0)
            steps.append(lambda: _ts(cx, "dve", h1_s[:, 0:CW], pp_[0:64, 0:CW], b1[:, 0:1], fr2[:, 0:1], ALU.add, ALU.mult, [ppb, c_b], [h1_b]))
            for _ in range(4):
                steps.append(lambda: wrap_op(h1_s, h1_b))
            steps.append(lambda: _act(cx, h1_s[:, 0:CW], h1_s[:, 0:CW], AF.Sin, [h1_b, c_b], [h1_b], scale=TWO_PI))
            steps.append(lambda: _mm(cx, pp_[0:64, 0:CW], w2[:, :], h1_s[:, 0:CW], True, True, [h1_b, c_b], [ppb]))
            steps.append(lambda: _ts(cx, "dve", h2_s[:, 0:CW], pp_[0:64, 0:CW], b2[:, 0:1], fr2[:, 0:1], ALU.add, ALU.mult, [ppb, c_b], [h2_b]))
            for _ in range(4):
                steps.append(lambda: wrap_op(h2_s, h2_b))
            steps.append(lambda: _act(cx, h2_s[:, 0:CW], h2_s[:, 0:CW], AF.Sin, [h2_b, c_b], [h2_b], scale=TWO_PI))
            steps.append(lambda: _mm(cx, pp_[:, 0:CW], w3[:, dirn, :], h2_s[:, 0:CW], True, True, [h2_b, c_b], [ppb]))
            steps.append(lambda: _act(cx, d_s[:, 0:CW], t_s[:, 0:CW], AF.Exp, [t_b, c_b], [d_b], scale=nea[:, dirn:dirn + 1]))
            steps.append(lambda: _tt(cx, "dve", d_s[:, 0:CW], pp_[:, 0:CW], d_s[:, 0:CW], ALU.mult, [ppb, d_b], [d_b]))
            if m0 <= L < m0 + CW:
                o = L - m0
                steps.append(lambda: _tt(cx, "dve", d_s[:, o:o + 1], d_s[:, o:o + 1], sk[:, 0:1], ALU.add, [d_b, c_b], [d_b]))
            else:
                steps.append(lambda: None)

            def s_last():
                _cp(cx, "pool", a_s[:, 0:CW], d_s[:, 0:CW], [d_b], [a_sb])
                P.dma("sp", Ad.ap()[:, m0:m0 + CW], a_s[:, 0:CW], reads=[a_sb], writes=[a_b])
                if HY_DEBUG and L == HY_L:
                    dbb = Buf()
                    P.dma("sp", adbg[:, m0:m0 + CW], d_s[:, 0:CW], reads=[d_b], writes=[dbb])
                    out_bufs.append(dbb)
            steps.append(s_last)
            return steps

        for g0 in range(0, nchunk, GI):
            lists = [chunk_steps(ci) for ci in range(g0, min(g0 + GI, nchunk))]
            for si_ in range(len(lists[0])):
                for l_ in lists:
                    l_[si_]()
        return a_b

    adm_b = gen_filter(zTm, tbm, HY_L, Adm)
    adc_b = gen_filter(zTc, tbc, 256, Adc)

    TWm = 2 * HY_L - 127
    Tm = [(cx.sb([128, TWm], BF16), Buf()) for _ in range(3)]
    Tc = [(cx.sb([128, 512 - 127], BF16), Buf()) for _ in range(3)]

    def path(U, Yo, Ad, ad_b, L, nblk, Tt):
        NBS = 4 * nblk
        TW = 2 * L - 127
        uf = [(cx.sb([128, 3, NBS]), Buf()) for _ in range(HY_NBUF)]
        ubf = [(cx.sb([128, 3, NBS], BF16), Buf()) for _ in range(HY_NBUF)]
        g1 = [(cx.sb([128, NBS]), Buf()) for _ in range(HY_NBUF)]
        g2 = [(cx.sb([128, NBS]), Buf()) for _ in range(HY_NBUF)]
        vr = [(cx.sb([128, NBS], BF16), Buf()) for _ in range(HY_NBUF)]
        z2 = [(cx.sb([128, NBS], BF16), Buf()) for _ in range(HY_NBUF)]
        z2r = [(cx.sb([128, NBS], BF16), Buf()) for _ in range(HY_NBUF)]
        osb = [(cx.sb([128, NBS]), Buf()) for _ in range(HY_NBUF)]
        ti = 0
        Dlist = [0]
        for dd in range(1, nblk):
            Dlist += [dd, -dd]

        def v4(ap_):
            return ap_.rearrange("p (b s) -> p b s", s=nblk)

        def conv(ps, psb, T_s, T_b, x_s, x_b):
            for n_, D in enumerate(Dlist):
                x0 = L - 127 + 128 * D
                s0, s1 = (0, nblk - D) if D >= 0 else (-D, nblk)
                t0, t1 = s0 + D, s1 + D
                P.op("pe", lambda x0=x0, s0=s0, s1=s1, t0=t0, t1=t1, n_=n_: nc.tensor.matmul(
                    out=v4(ps[:, 0:NBS])[:, :, t0:t1], lhsT=T_s[:, x0:x0 + 128], rhs=v4(x_s[:, :])[:, :, s0:s1], start=(n_ == 0), stop=(n_ == len(Dlist) - 1)),
                    reads=[T_b, x_b], writes=[psb])

        def do_channel(c, ti):
            u_s, u_b = uf[c % HY_NBUF]; ub_s, ub_b = ubf[c % HY_NBUF]
            for sg in range(3):
                P.dma("pool", u_s[:, sg, :], U[sg, c, :, :], writes=[u_b])
            P.op("pool", lambda u_s=u_s, ub_s=ub_s: nc.gpsimd.tensor_copy(out=ub_s[:, :, :], in_=u_s[:, :, :]), reads=[u_b], writes=[ub_b])
            T0, T0b = Tt[ti % 3]
            T1, T1b = Tt[(ti + 1) % 3]
            for (T_s, T_b, rc) in ((T0, T0b, c), (T1, T1b, 64 + c)):
                for qi in range(HY_TSPLIT):
                    pn = 128 // HY_TSPLIT
                    src = bass.AP(tensor=Ad, offset=rc * 2 * L + qi * pn, ap=[[1, pn], [1, TW]])
                    P.dma(("sp", "act")[qi % 2], T_s[qi * pn:(qi + 1) * pn, 0:TW], src, reads=[ad_b], writes=[T_b])
            pd, pdb = pA[0]; pu, pub = pA[1]; pvc, pvcb = pA[2]; pvdu, pvdub = pA[3]
            def shift(ps, psb, col0, mat, matB, src_ap, down):
                P.op("pe", lambda: nc.tensor.matmul(out=ps[:, col0:col0 + NBS], lhsT=perm[:, mat, :], rhs=src_ap, start=True, stop=(nblk == 1)), reads=[ub_b, c_b], writes=[psb])
                if nblk > 1:
                    if down:
                        o_ap = v4(ps[:, col0:col0 + NBS])[:, :, 1:nblk]; r_ap = v4(src_ap)[:, :, 0:nblk - 1]
                    else:
                        o_ap = v4(ps[:, col0:col0 + NBS])[:, :, 0:nblk - 1]; r_ap = v4(src_ap)[:, :, 1:nblk]
                    P.op("pe", lambda: nc.tensor.matmul(out=o_ap, lhsT=perm[:, matB, :], rhs=r_ap, start=False, stop=True), reads=[ub_b, c_b], writes=[psb])
            for sg in range(2):
                shift(pd, pdb, sg * NBS, 0, 1, ub_s[:, sg, :], True)
                shift(pu, pub, sg * NBS, 2, 3, ub_s[:, sg, :], False)
            P.op("pe", lambda: nc.tensor.matmul(out=pvc[:, 0:NBS], lhsT=perm[:, 4, :], rhs=ub_s[:, 2, :], start=True, stop=True), reads=[ub_b, c_b], writes=[pvcb])
            shift(pvdu, pvdub, 0, 5, 6, ub_s[:, 2, :], True)
            shift(pvdu, pvdub, NBS, 7, 8, ub_s[:, 2, :], False)
            g1_s, g1_b = g1[c % HY_NBUF]; g2_s, g2_b = g2[c % HY_NBUF]; vr_s, vr_b = vr[c % HY_NBUF]
            for sg, (g_s, g_b) in enumerate(((g1_s, g1_b), (g2_s, g2_b))):
                P.op("dve", lambda sg=sg, g_s=g_s: nc.vector.tensor_scalar_mul(out=g_s[:, :], in0=u_s[:, sg, :], scalar1=wc[:, 1, sg, c:c + 1]), reads=[u_b, c_b], writes=[g_b])
                P.op("dve", lambda sg=sg, g_s=g_s: nc.vector.scalar_tensor_tensor(out=g_s[:, :], in0=pd[:, sg * NBS:(sg + 1) * NBS], scalar=wc[:, 0, sg, c:c + 1], in1=g_s[:, :], op0=ALU.mult, op1=ALU.add), reads=[pdb, c_b, g_b], writes=[g_b])
                P.op("dve", lambda sg=sg, g_s=g_s: nc.vector.scalar_tensor_tensor(out=g_s[:, :], in0=pu[:, sg * NBS:(sg + 1) * NBS], scalar=wc[:, 2, sg, c:c + 1], in1=g_s[:, :], op0=ALU.mult, op1=ALU.add), reads=[pub, c_b, g_b], writes=[g_b])
            vtmp, vtmp_b = osb[c % HY_NBUF]
            P.op("dve", lambda: nc.vector.tensor_scalar_mul(out=vtmp[:, :], in0=pvc[:, 0:NBS], scalar1=wc[:, 1, 2, c:c + 1]), reads=[pvcb, c_b], writes=[vtmp_b])
            P.op("dve", lambda: nc.vector.scalar_tensor_tensor(out=vtmp[:, :], in0=pvdu[:, 0:NBS], scalar=wc[:, 0, 2, c:c + 1], in1=vtmp[:, :], op0=ALU.mult, op1=ALU.add), reads=[pvdub, c_b, vtmp_b], writes=[vtmp_b])
            P.op("dve", lambda: nc.vector.scalar_tensor_tensor(out=vr_s[:, :], in0=pvdu[:, NBS:2 * NBS], scalar=wc[:, 2, 2, c:c + 1], in1=vtmp[:, :], op0=ALU.mult, op1=ALU.add), reads=[pvdub, c_b, vtmp_b], writes=[vr_b])
            py, pyb = pA[4 + (c % HY_NBUF)]
            conv(py, pyb, T0, T0b, vr_s, vr_b)
            z2_s, z2_b = z2[c % HY_NBUF]; z2r_s, z2r_b = z2r[c % HY_NBUF]
            P.op("dve", lambda: nc.vector.tensor_tensor(out=z2_s[:, :], in0=py[:, 0:NBS], in1=g1_s[:, :], op=ALU.mult), reads=[pyb, g1_b], writes=[z2_b])
            pz, pzb = pA[6]
            P.op("pe", lambda: nc.tensor.matmul(out=pz[:, 0:NBS], lhsT=perm[:, 4, :], rhs=z2_s[:, :], start=True, stop=True), reads=[z2_b, c_b], writes=[pzb])
            P.op("act", lambda: nc.scalar.copy(out=z2r_s[:, :], in_=pz[:, 0:NBS]), reads=[pzb], writes=[z2r_b])
            py2, py2b = pA[7]
            conv(py2, py2b, T1, T1b, z2r_s, z2r_b)
            o_s, o_b = osb[c % HY_NBUF]
            P.op("dve", lambda: nc.vector.tensor_tensor(out=o_s[:, :], in0=py2[:, 0:NBS], in1=g2_s[:, :], op=ALU.mult), reads=[py2b, g2_b], writes=[o_b])
            ob = Buf()
            P.dma("pool", Yo[c, :, :], o_s[:, :], reads=[o_b], writes=[ob])
            out_bufs.append(ob)

        for c in range(64):
            do_channel(c, 2 * c)

    path(um, ym, Adm, adm_b, HY_L, NBm, Tm)
    def ctx_path():
        L, nblk, NBS = 256, 2, 8
        W = 64 * NBS
        TW = 2 * L - 127
        u_s = cx.sb([128, 3, W]); u_b = Buf()
        ub_s = cx.sb([128, 3, W], BF16); ub_b = Buf()
        for sg in range(3):
            P.dma("pool", u_s[:, sg, :].rearrange("p (c n) -> p c n", n=NBS), uc[sg, :, :, :].rearrange("c j n -> j c n"), writes=[u_b])
        _cp(cx, "pool", ub_s[:, :, :], u_s[:, :, :], [u_b], [ub_b])
        gg = [(cx.sb([128, W]), Buf()) for _ in range(2)]
        vr_s = cx.sb([128, W], BF16); vr_b = Buf()
        tmp_s = cx.sb([128, W]); tmp_bb = Buf()

        def v3(ap_):
            return ap_.rearrange("p (cb s) -> p cb s", s=nblk)

        def wbc(tap, sg):
            return wc[:, tap, sg, :].unsqueeze(2).to_broadcast([128, 64, NBS])

        def c3(ap_):
            return ap_.rearrange("p (c n) -> p c n", n=NBS)

        def shift(ps, psb, mat, matB, src_ap, down):
            _mm(cx, ps[:, 0:W], perm[:, mat, :], src_ap, True, False, [ub_b, c_b], [psb])
            if down:
                o_ap = v3(ps[:, 0:W])[:, :, 1:nblk]; r_ap = v3(src_ap)[:, :, 0:nblk - 1]
            else:
                o_ap = v3(ps[:, 0:W])[:, :, 0:nblk - 1]; r_ap = v3(src_ap)[:, :, 1:nblk]
            _mm(cx, o_ap, perm[:, matB, :], r_ap, False, True, [ub_b, c_b], [psb])

        pa, pab = pA[0]; pb, pbb = pA[1]; pc_, pcb = pA[2]
        for sg in range(2):
            g_s, g_b = gg[sg]
            shift(pa, pab, 0, 1, ub_s[:, sg, :], True)
            shift(pb, pbb, 2, 3, ub_s[:, sg, :], False)
            _tt(cx, "dve", c3(g_s[:, :]), c3(u_s[:, sg, :]), wbc(1, sg), ALU.mult, [u_b, c_b], [g_b])
            _tt(cx, "dve", c3(tmp_s[:, :]), c3(pa[:, 0:W]), wbc(0, sg), ALU.mult, [pab, c_b], [tmp_bb])
            _tt(cx, "dve", g_s[:, :], g_s[:, :], tmp_s[:, :], ALU.add, [g_b, tmp_bb], [g_b])
            _tt(cx, "dve", c3(tmp_s[:, :]), c3(pb[:, 0:W]), wbc(2, sg), ALU.mult, [pbb, c_b], [tmp_bb])
            _tt(cx, "dve", g_s[:, :], g_s[:, :], tmp_s[:, :], ALU.add, [g_b, tmp_bb], [g_b])
        _mm(cx, pc_[:, 0:W], perm[:, 4, :], ub_s[:, 2, :], True, True, [ub_b, c_b], [pcb])
        shift(pa, pab, 5, 6, ub_s[:, 2, :], True)
        shift(pb, pbb, 7, 8, ub_s[:, 2, :], False)
        acc_s = cx.sb([128, W]); acc_bb = Buf()
        _tt(cx, "dve", c3(acc_s[:, :]), c3(pc_[:, 0:W]), wbc(1, 2), ALU.mult, [pcb, c_b], [acc_bb])
        _tt(cx, "dve", c3(tmp_s[:, :]), c3(pa[:, 0:W]), wbc(0, 2), ALU.mult, [pab, c_b], [tmp_bb])
        _tt(cx, "dve", acc_s[:, :], acc_s[:, :], tmp_s[:, :], ALU.add, [acc_bb, tmp_bb], [acc_bb])
        _tt(cx, "dve", c3(tmp_s[:, :]), c3(pb[:, 0:W]), wbc(2, 2), ALU.mult, [pbb, c_b], [tmp_bb])
        _tt(cx, "dve", vr_s[:, :], acc_s[:, :], tmp_s[:, :], ALU.add, [acc_bb, tmp_bb], [vr_b])

        def load_T(order, halfi, T_s, T_b):
            rc0 = order * 64 + halfi * 32
            src = bass.AP(tensor=Adc, offset=rc0 * 2 * L, ap=[[1, 128], [2 * L, 32], [1, TW]])
            P.dma("sp", T_s[:, 0:32 * TW].rearrange("p (c x) -> p c x", x=TW), src, reads=[adc_b], writes=[T_b])

        def conv_all(ps, psb, order, x_s, x_b, Tbufs):
            for halfi in range(2):
                T_s, T_b = Tbufs[halfi]
                load_T(order, halfi, T_s, T_b)
                Tv = T_s[:, 0:32 * TW].rearrange("p (c x) -> p c x", x=TW)
                for cl in range(32):
                    c = halfi * 32 + cl
                    for n_, D in enumerate((0, 1, -1)):
                        x0 = L - 127 + 128 * D
                        s0, s1 = (0, nblk - D) if D >= 0 else (-D, nblk)
                        t0, t1 = s0 + D, s1 + D
                        xin = x_s[:, c * NBS:(c + 1) * NBS].rearrange("p (b s) -> p b s", s=nblk)[:, :, s0:s1]
                        oo = ps[:, c * NBS:(c + 1) * NBS].rearrange("p (b s) -> p b s", s=nblk)[:, :, t0:t1]
                        _mm(cx, oo, Tv[:, cl, x0:x0 + 128], xin, n_ == 0, n_ == 2, [T_b, x_b], [psb])

        py, pyb = pA[3]
        conv_all(py, pyb, 0, vr_s, vr_b, [Tm[0], Tm[1]])
        z2_s = cx.sb([128, W], BF16); z2_b = Buf()
        z2r_s = cx.sb([128, W], BF16); z2r_b = Buf()
        _tt(cx, "dve", z2_s[:, :], py[:, 0:W], gg[0][0][:, :], ALU.mult, [pyb, gg[0][1]], [z2_b])
        pz, pzb = pA[4]
        _mm(cx, pz[:, 0:W], perm[:, 4, :], z2_s[:, :], True, True, [z2_b, c_b], [pzb])
        _cp(cx, "act", z2r_s[:, :], pz[:, 0:W], [pzb], [z2r_b])
        py2, py2b = pA[5]
        conv_all(py2, py2b, 1, z2r_s, z2r_b, [Tm[2], Tm[0]])
        o_s = cx.sb([128, W]); o_b = Buf()
        _tt(cx, "dve", o_s[:, :], py2[:, 0:W], gg[1][0][:, :], ALU.mult, [py2b, gg[1][1]], [o_b])
        ob = Buf()
        P.dma("pool", yc[:, :, :].rearrange("c j n -> j c n"), o_s[:, :].rearrange("p (c n) -> p c n", n=NBS), reads=[o_b], writes=[ob])
        out_bufs.append(ob)

    ctx_path()
    P.fence("sp", out_bufs)
    return cx


def hy_inputs(u0, uc0, cg, I):
    c = np.ascontiguousarray
    def lay(u, nblk):
        out = []
        for sg in range(3):
            a = u[:, :, sg * 512 + cg * 64: sg * 512 + cg * 64 + 64].reshape(4, nblk, 128, 64)
            out.append(a.transpose(3, 2, 0, 1).reshape(64, 128, 4 * nblk))
        return c(np.stack(out, 0))
    zTm, tbm = hy_lag_tables(HY_L)
    zTc, tbc = hy_lag_tables(256)
    cw = I["hy_conv_w"][0]
    wc = np.stack([np.stack([cw[tap, sg * 512 + cg * 64: sg * 512 + cg * 64 + 64] for sg in range(3)], 0) for tap in range(3)], 0)
    wc = c(np.broadcast_to(wc[None], (128, 3, 3, 64))).astype(np.float32)
    w3 = I["hy_w3"][0].reshape(64, 2, 2, 512)[:, :, :, cg * 64:cg * 64 + 64]
    w3 = c(w3.transpose(0, 2, 1, 3).reshape(64, 2, 128))
    ld = I["hy_log_decay"][0].reshape(2, 2, 512)[:, :, cg * 64:cg * 64 + 64]
    ld = c(ld.transpose(0, 2, 1).reshape(128, 2))
    sk = c(I["hy_skip"][0][:, cg * 64:cg * 64 + 64].reshape(128, 1))
    return dict(um=lay(u0, 64), uc=lay(uc0, 2), wc=wc, perm=hy_perm_mats(), zTm=zTm, tbm=tbm, zTc=zTc, tbc=tbc,
                w1=c(I["hy_w1"][0]), w2=c(I["hy_w2"][0]), b1=c(I["hy_b1"][0].reshape(64, 1)), b2=c(I["hy_b2"][0].reshape(64, 1)),
                freq=c(I["hy_freq"][0].reshape(64, 1)), w3=w3, ld=ld, skip=sk)


def hy_unlayout(ys, nblk):
    out = np.empty((4, nblk * 128, 512), np.float32)
    for cg, y in enumerate(ys):
        a = y.reshape(64, 128, 4, nblk).transpose(2, 3, 1, 0).reshape(4, nblk * 128, 64)
        out[:, :, cg * 64:(cg + 1) * 64] = a
    return out


def _mm(cx, out, lhsT, rhs, start, stop, reads, writes):
    nc = cx.nc
    cx.P.op("pe", lambda: nc.tensor.matmul(out=out, lhsT=lhsT, rhs=rhs, start=start, stop=stop), reads=reads, writes=writes)


def _tr(cx, out, in_, ident, reads, writes):
    nc = cx.nc
    cx.P.op("pe", lambda: nc.tensor.transpose(out=out, in_=in_, identity=ident), reads=reads, writes=writes)


def _act(cx, out, in_, func, reads, writes, bias=None, scale=None, accum_out=None):
    nc = cx.nc
    kw = {}
    if bias is not None:
        kw["bias"] = bias
    if scale is not None:
        kw["scale"] = scale
    if accum_out is not None:
        kw["accum_out"] = accum_out
    cx.P.op("act", lambda: nc.scalar.activation(out=out, in_=in_, func=func, **kw), reads=reads, writes=writes)


def _ts(cx, eng, out, in0, s1, s2, op0, op1, reads, writes):
    e = cx.P.engs[eng]
    if op1 is None:
        cx.P.op(eng, lambda: e.tensor_scalar(out=out, in0=in0, scalar1=s1, scalar2=None, op0=op0), reads=reads, writes=writes)
    else:
        cx.P.op(eng, lambda: e.tensor_scalar(out=out, in0=in0, scalar1=s1, scalar2=s2, op0=op0, op1=op1), reads=reads, writes=writes)


def _tt(cx, eng, out, in0, in1, op, reads, writes):
    e = cx.P.engs[eng]
    cx.P.op(eng, lambda: e.tensor_tensor(out=out, in0=in0, in1=in1, op=op), reads=reads, writes=writes)


def _cp(cx, eng, out, in_, reads, writes):
    e = cx.P.engs[eng]
    if eng == "act":
        cx.P.op(eng, lambda: e.copy(out=out, in_=in_), reads=reads, writes=writes)
    else:
        cx.P.op(eng, lambda: e.tensor_copy(out=out, in_=in_), reads=reads, writes=writes)


def _emit_rms_rows(cx, x_s, x_b, n, xh_s, xh_b, junk, ss, tmp_b):
    nc = cx.nc
    _act(cx, junk[:, 0:n], x_s[:, 0:n], AF.Square, [x_b], [tmp_b], accum_out=ss[:, 0:1])
    _ts(cx, "dve", ss[:, 0:1], ss[:, 0:1], 1.0 / n, LN_EPS, ALU.mult, ALU.add, [tmp_b], [tmp_b])
    _act(cx, ss[:, 0:1], ss[:, 0:1], AF.Sqrt, [tmp_b], [tmp_b])
    cx.P.op("dve", lambda: nc.vector.reciprocal(out=ss[:, 0:1], in_=ss[:, 0:1]), reads=[tmp_b], writes=[tmp_b])
    _ts(cx, "dve", xh_s[:, 0:n], x_s[:, 0:n], ss[:, 0:1], None, ALU.mult, None, [x_b, tmp_b], [xh_b])


MLA_NQ = 4096
MLA_NK = 8448


def build_stage_mla():
    cx = Ctx()
    nc, P = cx.nc, cx.P
    NQT, NKT = MLA_NQ // 128, MLA_NK // 128
    scale = 96 ** -0.5
    cq = cx.din("cq", [NQT, 128, 384])
    ckv = cx.din("ckv", [NKT, 128, 256])
    kpeT = cx.din("kpeT", [32, MLA_NK]); kpeTp = cx.din("kpeTp", [32, MLA_NK])
    cosk = cx.din("cosk", [32, MLA_NK]); sink = cx.din("sink", [32, MLA_NK])
    cosq = cx.din("cosq", [32, MLA_NQ], BF16); sinq = cx.din("sinq", [32, MLA_NQ], BF16)
    wuq = cx.din("wuq", [384, 768]); wuqp = cx.din("wuqp", [384, 256])
    gq = cx.din("gq", [128, 3]); gkv = cx.din("gkv", [128, 2])
    wukv = cx.din("wukv", [256, 1280])
    ident_d = cx.din("ident", [128, 128], BF16)
    y = cx.dout("y", [NQT, 128, 768])

    ident = cx.sb([128, 128], BF16); c_b = Buf()
    P.dma("sp", ident[:, :], ident_d[:, :], writes=[c_b])
    identf = cx.sb([128, 128], F32)
    P.op("pool", lambda: nc.gpsimd.tensor_copy(out=identf[:, :], in_=ident[:, :]), reads=[c_b], writes=[c_b])
    gq_s = cx.sb([128, 3]); gkv_s = cx.sb([128, 2])
    P.dma("sp", gq_s[:, :], gq[:, :], writes=[c_b]); P.dma("sp", gkv_s[:, :], gkv[:, :], writes=[c_b])
    ones = cx.sb([128, 128], BF16)
    P.op("pool", lambda: nc.gpsimd.memset(ones[:, :], 1.0), writes=[c_b])
    stg = [(cx.sb([128, 2048]), Buf()) for _ in range(2)]
    si = [0]

    def stage():
        t = stg[si[0] % 2]; si[0] += 1
        return t

    wqb = cx.sb([128, 3, 768], BF16); wqpb = cx.sb([128, 3, 8, 96], BF16); wkvb = cx.sb([128, 2, 1280], BF16); w_b = Buf()
    P.op("pool", lambda: nc.gpsimd.memset(wqpb[:, :, :, :], 0.0), writes=[w_b])
    for k in range(3):
        st_s, st_b = stage()
        P.dma("sp", st_s[:, 0:768], wuq[k * 128:(k + 1) * 128, :], writes=[st_b])
        P.dma("sp", st_s[:, 768:1024], wuqp[k * 128:(k + 1) * 128, :], writes=[st_b])
        _ts(cx, "pool", wqb[:, k, :], st_s[:, 0:768], gq_s[:, k:k + 1], None, ALU.mult, None, [st_b, c_b], [w_b])
        _ts(cx, "pool", wqpb[:, k, :, 64:96], st_s[:, 768:1024].rearrange("p (h d) -> p h d", d=32), gq_s[:, k:k + 1], None, ALU.mult, None, [st_b, c_b], [w_b])
    for k in range(2):
        st_s, st_b = stage()
        P.dma("sp", st_s[:, 0:1280], wukv[k * 128:(k + 1) * 128, :], writes=[st_b])
        _ts(cx, "pool", wkvb[:, k, :], st_s[:, 0:1280], gkv_s[:, k:k + 1], None, ALU.mult, None, [st_b, c_b], [w_b])

    cosq_s = cx.sb([96, MLA_NQ], BF16); sinq_s = cx.sb([96, MLA_NQ], BF16); tq_b = Buf()
    P.dma("sp", cosq_s[64:96, :], cosq[:, :], writes=[tq_b]); P.dma("sp", sinq_s[64:96, :], sinq[:, :], writes=[tq_b])
    KT = cx.sb([96, MLA_NK], BF16); kt_b = Buf()
    ktmp = [(cx.sb([96, 4, 1056]), Buf()) for _ in range(1)]
    for c0 in range(0, MLA_NK, 1056):
        k_s, k_b = ktmp[0]
        for i_, src in enumerate((kpeT, cosk, kpeTp, sink)):
            P.dma("sp", k_s[64:96, i_, :], src[:, c0:c0 + 1056], writes=[k_b])
        _tt(cx, "dve", k_s[64:96, 0, :], k_s[64:96, 0, :], k_s[64:96, 1, :], ALU.mult, [k_b], [k_b])
        _tt(cx, "dve", k_s[64:96, 2, :], k_s[64:96, 2, :], k_s[64:96, 3, :], ALU.mult, [k_b], [k_b])
        _tt(cx, "dve", KT[64:96, c0:c0 + 1056], k_s[64:96, 0, :], k_s[64:96, 2, :], ALU.add, [k_b], [kt_b])

    qnT = cx.sb([128, 3, MLA_NQ], BF16); qn_b = Buf()
    kvnT = cx.sb([128, 2, MLA_NK], BF16); kvn_b = Buf()
    xs = [(cx.sb([128, 384]), Buf()) for _ in range(2)]
    xh = [(cx.sb([128, 384], BF16), Buf()) for _ in range(2)]
    junk = cx.sb([128, 384]); ss = cx.sb([128, 1]); tmp_b = Buf()
    NPS = 3
    PS2 = [cx.ps([128, 1024]) for _ in range(NPS)]
    ps2_bufs = [[Buf(), Buf()] for _ in range(NPS)]
    pOt = cx.ps([128, 512]); pO_b = Buf()
    _pp = (cx.ps([128, 512]), Buf())
    prepP = [_pp, _pp, _pp]
    for (src, ntile, ncol, dst, dst_b) in ((cq, NQT, 384, qnT, qn_b), (ckv, NKT, 256, kvnT, kvn_b)):
        nk = ncol // 128
        for t in range(ntile):
            x_s, x_b = xs[t % 2]; xh_s, xh_b = xh[t % 2]
            P.dma("sp", x_s[:, 0:ncol], src[t, :, :], writes=[x_b])
            _emit_rms_rows(cx, x_s, x_b, ncol, xh_s, xh_b, junk, ss, tmp_b)
            pt_s, pt_b = prepP[t % 2]
            ptv = pt_s[:, :].bitcast(BF16)
            for k in range(nk):
                _tr(cx, ptv[:, k * 128:(k + 1) * 128], xh_s[:, k * 128:(k + 1) * 128], ident[:, :], [xh_b, c_b], [pt_b])
            _cp(cx, "act" if t % 2 == 0 else "dve", dst[:, 0:nk, t * 128:(t + 1) * 128], ptv[:, 0:nk * 128].rearrange("p (k q) -> p k q", q=128), [pt_b], [dst_b])

    NQC = MLA_NQ // 512
    QT = [cx.sb([96, MLA_NQ], BF16) for _ in range(2)]
    qt_bufs = [[Buf() for _ in range(NQC)] for _ in range(2)]
    nkc = (MLA_NK + 511) // 512
    kt_bufs = [Buf() for _ in range(nkc)]
    NVG = (NKT + 4) // 5
    V = [cx.sb([128, NKT, 97], BF16) for _ in range(2)]
    v1_b = Buf()
    v_bufs = [[Buf() for _ in range(NVG)] for _ in range(2)]
    for i_ in range(2):
        P.op("pool", lambda i_=i_: nc.gpsimd.memset(V[i_][:, :, 96:97], 1.0), writes=[v1_b])
    sq = [(cx.sb([96, 512], BF16), Buf()) for _ in range(3)]
    accq = [cx.sb([128, 32]) for _ in range(2)]; acc_b = [Buf(), Buf()]
    negM = [cx.sb([128, 1]) for _ in range(2)]; nm_b = [Buf(), Buf()]
    rope_t = [(cx.sb([96, 2, 512]), Buf()) for _ in range(2)]
    pt = [(cx.sb([128, 1024], BF16), Buf()) for _ in range(3)]
    rinv = cx.sb([128, 4]); rinv_b = Buf()
    ysb = [(cx.sb([128, 96]), Buf()) for _ in range(4)]
    oT_s = cx.sb([97, 512]); oT_b = Buf()
    pTr, pTr_b = prepP[2]
    out_bufs = []
    NPR = NKT // 2
    sqi = [0]

    def unit_qa(h, ci):
        hb = h % 2
        cs = slice(ci * 512, (ci + 1) * 512)
        pq, pqb = prepP[0]; pp, ppb = prepP[1]
        r_s, r_b = rope_t[ci % 2]
        for k in range(3):
            _mm(cx, pq[0:96, :], wqb[:, k, h * 96:(h + 1) * 96], qnT[:, k, cs], k == 0, k == 2, [w_b, qn_b], [pqb])
        _cp(cx, "dve", QT[hb][0:64, cs], pq[0:64, :], [pqb], [qt_bufs[hb][ci]])
        _tt(cx, "dve", r_s[64:96, 0, :], pq[64:96, :], cosq_s[64:96, cs], ALU.mult, [pqb, tq_b], [r_b])
        for k in range(3):
            _mm(cx, pp[0:96, :], wqpb[:, k, h, :], qnT[:, k, cs], k == 0, k == 2, [w_b, qn_b], [ppb])
        _tt(cx, "dve", r_s[64:96, 1, :], pp[64:96, :], sinq_s[64:96, cs], ALU.mult, [ppb, tq_b], [r_b])
        _tt(cx, "dve", QT[hb][64:96, cs], r_s[64:96, 0, :], r_s[64:96, 1, :], ALU.add, [r_b], [qt_bufs[hb][ci]])

    def unit_va(h, gi):
        hb = h % 2
        g0 = gi * 5
        gn = min(5, NKT - g0)
        pv, pvb = prepP[2]
        for t in range(gn):
            for k in range(2):
                _mm(cx, pv[:, t * 96:(t + 1) * 96], kvnT[:, k, (g0 + t) * 128:(g0 + t + 1) * 128], wkvb[:, k, h * 160 + 64:h * 160 + 160], k == 0, k == 1, [w_b, kvn_b], [pvb])
        _cp(cx, "dve", V[hb][:, g0:g0 + gn, 0:96], pv[:, 0:gn * 96].rearrange("p (t d) -> p t d", d=96), [pvb], [v_bufs[hb][gi]])

    def unit_qb(h, ci):
        hb = h % 2
        cs = slice(ci * 512, (ci + 1) * 512)
        s_s, s_b = sq[sqi[0] % 3]; sqi[0] += 1
        _act(cx, s_s[:, :], QT[hb][:, cs], AF.Square, [qt_bufs[hb][ci]], [s_b])
        pm, pmb = prepP[2]
        _mm(cx, pm[:, :], ones[0:96, :], s_s[:, :], True, True, [s_b, c_b], [pmb])
        P.op("dve", lambda: nc.vector.reduce_max(out=accq[hb][:, ci:ci + 1], in_=pm[:, :], axis=AX.X), reads=[pmb], writes=[acc_b[hb]])

    def units_early(h):
        return ([lambda ci=ci: unit_qa(h, ci) for ci in range(NQC)] + [lambda gi=gi: unit_va(h, gi) for gi in range(NVG)]
                + [lambda ci=ci: unit_qb(h, ci) for ci in range(NQC)])

    def prep_k(h):
        hb = h % 2
        for ci in range(nkc):
            c0 = ci * 512; cw = min(512, MLA_NK - c0)
            pk, pkb = prepP[ci % 3]
            for k in range(2):
                _mm(cx, pk[0:64, 0:cw], wkvb[:, k, h * 160:h * 160 + 64], kvnT[:, k, c0:c0 + cw], k == 0, k == 1, [w_b, kvn_b], [pkb])
            _cp(cx, "act" if ci % 2 == 0 else "dve", KT[0:64, c0:c0 + cw], pk[0:64, 0:cw], [pkb], [kt_bufs[ci]])
        for ci in range(nkc):
            c0 = ci * 512; cw = min(512, MLA_NK - c0)
            s_s, s_b = sq[sqi[0] % 3]; sqi[0] += 1
            _act(cx, s_s[:, 0:cw], KT[:, c0:c0 + cw], AF.Square, [kt_bufs[ci], kt_b], [s_b])
            pm, pmb = prepP[ci % 3]
            _mm(cx, pm[:, 0:cw], ones[0:96, :], s_s[:, 0:cw], True, True, [s_b, c_b], [pmb])
            P.op("dve", lambda pm=pm, ci=ci, cw=cw: nc.vector.reduce_max(out=accq[hb][:, 8 + ci:9 + ci], in_=pm[:, 0:cw], axis=AX.X), reads=[pmb], writes=[acc_b[hb]])
        P.op("dve", lambda: nc.vector.reduce_max(out=accq[hb][:, 30:31], in_=accq[hb][:, 0:8], axis=AX.X), reads=[acc_b[hb]], writes=[acc_b[hb]])
        P.op("dve", lambda: nc.vector.reduce_max(out=accq[hb][:, 31:32], in_=accq[hb][:, 8:8 + nkc], axis=AX.X), reads=[acc_b[hb]], writes=[acc_b[hb]])
        _tt(cx, "dve", negM[hb][:, :], accq[hb][:, 30:31], accq[hb][:, 31:32], ALU.mult, [acc_b[hb]], [nm_b[hb]])
        _act(cx, negM[hb][:, :], negM[hb][:, :], AF.Sqrt, [nm_b[hb]], [nm_b[hb]])
        _ts(cx, "dve", negM[hb][:, :], negM[hb][:, :], -scale, None, ALU.mult, None, [nm_b[hb]], [nm_b[hb]])

    def attention(h, pending):
        hb = h % 2

        def emit_S(st, pr, nn):
            qs = slice(st * 512, (st + 1) * 512)
            for hf in range(2):
                kc = 2 * pr + hf
                _mm(cx, PS2[nn % NPS][:, hf * 512:(hf + 1) * 512], KT[:, kc * 128:(kc + 1) * 128], QT[hb][:, qs], True, True,
                    [kt_bufs[kc // 4], kt_b, qt_bufs[hb][st]], [ps2_bufs[nn % NPS][hf]])
        seqs = [(st, pr) for st in range(NQC) for pr in range(NPR)]
        stride = max(1, (len(seqs) - 8) // max(1, len(pending)))
        emit_S(*seqs[0], 0)
        emit_S(*seqs[1], 1)
        for n_, (st, pr) in enumerate(seqs):
            if n_ + 2 < len(seqs):
                emit_S(*seqs[n_ + 2], n_ + 2)
            p_s, p_b = pt[n_ % 3]
            _act(cx, p_s[:, :], PS2[n_ % NPS][:, :], AF.Exp, ps2_bufs[n_ % NPS] + [nm_b[hb]], [p_b], bias=negM[hb][:, 0:1], scale=scale)
            for hf in range(2):
                kc = 2 * pr + hf
                _mm(cx, pOt[0:97, :], V[hb][:, kc, :], p_s[:, hf * 512:(hf + 1) * 512], kc == 0, kc == NKT - 1, [p_b, v_bufs[hb][kc // 5], v1_b], [pO_b])
            if pending and n_ % stride == stride // 2:
                pending.pop(0)()
            if pr == NPR - 1:
                _cp(cx, "dve", oT_s[:, :], pOt[0:97, :], [pO_b], [oT_b])
                for sub in range(4):
                    _tr(cx, pTr[:, sub * 97:(sub + 1) * 97], oT_s[:, sub * 128:(sub + 1) * 128], identf[0:97, 0:97], [oT_b, c_b], [pTr_b])
                for sub in range(4):
                    y_s, y_b = ysb[sub]
                    P.op("dve", lambda sub=sub: nc.vector.reciprocal(out=rinv[:, sub:sub + 1], in_=pTr[:, sub * 97 + 96:sub * 97 + 97]), reads=[pTr_b], writes=[rinv_b])
                    _ts(cx, "dve", y_s[:, :], pTr[:, sub * 97:sub * 97 + 96], rinv[:, sub:sub + 1], None, ALU.mult, None, [pTr_b, rinv_b], [y_b])
                    ob = Buf()
                    P.dma("pool", y[st * 4 + sub, :, h * 96:(h + 1) * 96], y_s[:, :], reads=[y_b], writes=[ob])
                    out_bufs.append(ob)
        while pending:
            pending.pop(0)()

    for u_ in units_early(0):
        u_()
    prep_k(0)
    for h in range(8):
        pending = units_early(h + 1) if h + 1 < 8 else []
        attention(h, pending)
        if h + 1 < 8:
            prep_k(h + 1)
    P.fence("sp", out_bufs)
    return cx


ROPE_PERM = np.concatenate([np.arange(8, 16), np.arange(0, 8), np.arange(24, 32), np.arange(16, 24)])
ROPE_SIGN = np.concatenate([-np.ones(8), np.ones(8), -np.ones(8), np.ones(8)]).astype(np.float32)


def rope_tables(L):
    t = np.arange(L)
    rows = (t // 64).astype(np.float32); cols = (t % 64).astype(np.float32)
    half = 16
    inv = (np.float32(10000.0) ** (-np.arange(0, half, 2, dtype=np.float32) / np.float32(half))).astype(np.float32)
    ar = rows[:, None] * inv[None, :]; ac = cols[:, None] * inv[None, :]
    ang = np.concatenate([ar, ar, ac, ac], -1)
    return np.cos(ang).astype(np.float32), np.sin(ang).astype(np.float32)


def mla_inputs(u1_b, kvc_b, half, I):
    c = np.ascontiguousarray
    cos, sin = rope_tables(8192)
    cosT = cos.T; sinS = (sin * ROPE_SIGN[None, :]).T
    qs = slice(half * 4096, half * 4096 + 4096)
    cq = u1_b[qs, 0:384].reshape(32, 128, 384)
    ckv = np.concatenate([u1_b[:, 384:640], kvc_b[:, 0:256]], 0).reshape(66, 128, 256)
    kpe = np.concatenate([u1_b[:, 640:672], kvc_b[:, 256:288]], 0)
    cosk = np.concatenate([cosT, np.ones((32, 256), np.float32)], 1)
    sink = np.concatenate([sinS, np.zeros((32, 256), np.float32)], 1)
    wuq = I["mla_w_uq"][0]
    wuqp = wuq.reshape(384, 8, 96)[:, :, 64:96][:, :, ROPE_PERM].reshape(384, 256)
    return dict(cq=c(cq), ckv=c(ckv), kpeT=c(kpe.T), kpeTp=c(kpe[:, ROPE_PERM].T), cosk=c(cosk), sink=c(sink),
                cosq=c(cosT[:, qs]).astype(NPBF), sinq=c(sinS[:, qs]).astype(NPBF), wuq=c(wuq), wuqp=c(wuqp),
                gq=c(I["mla_q_norm"][0].reshape(3, 128).T), gkv=c(I["mla_kv_norm"][0].reshape(2, 128).T),
                wukv=c(I["mla_w_ukv"][0]), ident=np.eye(128, dtype=NPBF))


FN_KT = 17


def build_stage_fn():
    cx = Ctx()
    nc, P = cx.nc, cx.P
    ufn = cx.din("ufn", [64, 128, 256])
    gd = cx.din("g", [128, 256]); bd = cx.din("b", [128, 256])
    c64d = cx.din("c64", [128, 128], BF16); s64d = cx.din("s64", [128, 128], BF16)
    cmd = cx.din("cm", [FN_KT, 128, 64, 128], BF16); smd = cx.din("sm", [FN_KT, 128, 64, 128], BF16)
    ident_d = cx.din("ident", [128, 128], BF16)
    y = cx.dout("y", [FN_KT, 2, 128, 256])
    ident = cx.sb([128, 128], BF16); c64 = cx.sb([128, 128], BF16); s64 = cx.sb([128, 128], BF16); g_s = cx.sb([128, 256]); b_s = cx.sb([128, 256]); c_b = Buf()
    for (d_, s_) in ((ident, ident_d), (c64, c64d), (s64, s64d), (g_s, gd), (b_s, bd)):
        P.dma("sp", d_[:, :], s_[:, :], writes=[c_b])
    PQ = cx.sb([128, 64, 512], BF16); pq_bufs = [Buf() for _ in range(64)]
    xs = [(cx.sb([128, 256]), Buf()) for _ in range(2)]
    xn = [(cx.sb([128, 256]), Buf()) for _ in range(2)]
    xg = [(cx.sb([128, 256], BF16), Buf()) for _ in range(2)]
    xT = [(cx.sb([128, 2, 128], BF16), Buf()) for _ in range(2)]
    st = cx.sb([128, 4, 6]); mv = cx.sb([128, 4, 2]); rstd = cx.sb([128, 4]); tmp_b = Buf()
    pA = [(cx.ps([128, 512]), Buf()) for _ in range(8)]
    def part1(t):
        x_s, x_b = xs[t % 2]; n_s, n_b = xn[t % 2]; g_t, g_b = xg[t % 2]
        P.dma("sp", x_s[:, :], ufn[t, :, :], writes=[x_b])
        for gi in range(4):
            P.op("dve", lambda gi=gi, x_s=x_s: nc.vector.bn_stats(out=st[:, gi, :], in_=x_s[:, gi * 64:(gi + 1) * 64]), reads=[x_b], writes=[tmp_b])
        for gi in range(4):
            P.op("dve", lambda gi=gi: nc.vector.bn_aggr(out=mv[:, gi, :], in_=st[:, gi:gi + 1, :]), reads=[tmp_b], writes=[tmp_b])
        _ts(cx, "dve", rstd[:, :], mv[:, :, 1], LN_EPS, None, ALU.add, None, [tmp_b], [tmp_b])
        _act(cx, rstd[:, :], rstd[:, :], AF.Sqrt, [tmp_b], [tmp_b])
        P.op("dve", lambda: nc.vector.reciprocal(out=rstd[:, :], in_=rstd[:, :]), reads=[tmp_b], writes=[tmp_b])
        for gi in range(4):
            _ts(cx, "dve", n_s[:, gi * 64:(gi + 1) * 64], x_s[:, gi * 64:(gi + 1) * 64], mv[:, gi, 0:1], rstd[:, gi:gi + 1], ALU.subtract, ALU.mult, [x_b, tmp_b], [n_b])
        _tt(cx, "pool", n_s[:, :], n_s[:, :], g_s[:, :], ALU.mult, [n_b, c_b], [n_b])
        _tt(cx, "pool", g_t[:, :], n_s[:, :], b_s[:, :], ALU.add, [n_b, c_b], [g_b])

    def part2(t):
        g_t, g_b = xg[t % 2]; t_s, t_b = xT[t % 2]
        pt_s, pt_b = pA[4 + (t % 2)]
        ptv = pt_s[:, :].bitcast(BF16)
        for k in range(2):
            _tr(cx, ptv[:, k * 128:(k + 1) * 128], g_t[:, k * 128:(k + 1) * 128], ident[:, :], [g_b, c_b], [pt_b])
        _cp(cx, "act", t_s[:, :, :], ptv[:, 0:256].rearrange("p (k q) -> p k q", q=128), [pt_b], [t_b])
        pp_s, pp_b = pA[6 + (t % 2)]
        for k in range(2):
            _mm(cx, pp_s[:, k * 128:(k + 1) * 128], t_s[:, k, :], c64[:, :], True, True, [t_b, c_b], [pp_b])
            _mm(cx, pp_s[:, 256 + k * 128:256 + (k + 1) * 128], t_s[:, k, :], s64[:, :], True, True, [t_b, c_b], [pp_b])
        _cp(cx, "act", PQ[:, t, 0:256], pp_s[:, 0:256], [pp_b], [pq_bufs[t]])
        _ts(cx, "dve", PQ[:, t, 256:512], pp_s[:, 256:512], -1.0, None, ALU.mult, None, [pp_b], [pq_bufs[t]])

    part1(0)
    for t in range(64):
        if t + 1 < 64:
            part1(t + 1)
        part2(t)
    ck = [(cx.sb([128, 64, 128], BF16), Buf()) for _ in range(2)]
    sk = [(cx.sb([128, 64, 128], BF16), Buf()) for _ in range(2)]
    ysb = [(cx.sb([128, 2, 256]), Buf()) for _ in range(2)]
    tmp2 = [(cx.sb([128, 256]), Buf()) for _ in range(2)]
    out_bufs = []
    for kt in range(FN_KT):
        c_s, cb = ck[kt % 2]; s_s, sb_ = sk[kt % 2]
        P.dma("sp", c_s[:, :, :], cmd[kt, :, :, :], writes=[cb])
        P.dma("pool", s_s[:, :, :], smd[kt, :, :, :], writes=[sb_])
        accC, accC_b = pA[2 * (kt % 2)]
        accS, accS_b = pA[2 * (kt % 2) + 1]
        for nt in range(64):
            _mm(cx, accC[:, 0:256], c_s[:, nt, :], PQ[:, nt, 0:256], nt == 0, nt == 63, [cb, pq_bufs[nt]], [accC_b])
            _mm(cx, accS[:, 0:256], s_s[:, nt, :], PQ[:, nt, 256:512], nt == 0, nt == 63, [sb_, pq_bufs[nt]], [accS_b])
        y_s, y_b = ysb[kt % 2]
        t_s, t_b = tmp2[kt % 2]
        _cp(cx, "act", t_s[:, :], accS[:, 0:256], [accS_b], [t_b])
        _tt(cx, "dve", y_s[:, 0, :], accC[:, 0:256], t_s[:, :], ALU.add, [accC_b, t_b], [y_b])
        _tt(cx, "dve", y_s[:, 1, :], accC[:, 0:256], t_s[:, :], ALU.subtract, [accC_b, t_b], [y_b])
        ob = Buf()
        P.dma("sp", y[kt, :, :, :].rearrange("a p c -> p a c"), y_s[:, :, :], reads=[y_b], writes=[ob])
        out_bufs.append(ob)
    P.fence("sp", out_bufs)
    return cx


_FN_CONST = {}


def fn_klist(half):
    base = half * 2048 + np.arange(2048, dtype=np.int64)
    extra = np.full(128, 0 if half == 0 else 4096, np.int64)
    return np.concatenate([base, extra])


def fn_consts(half):
    if half in _FN_CONST:
        return _FN_CONST[half]
    L = 8192
    n = np.arange(L, dtype=np.int64)
    k = fn_klist(half)
    idx = (n[:, None] * k[None, :]) % L
    ang = idx.astype(np.float64) * (2.0 * math.pi / L)
    sc = 1.0 / math.sqrt(L)

    def tile(m):
        return np.ascontiguousarray(m.reshape(64, 128, FN_KT, 128).transpose(2, 1, 0, 3)).astype(NPBF)
    cm = tile((np.cos(ang) * sc).astype(np.float32)); sm = tile((np.sin(ang) * sc).astype(np.float32))
    c = np.arange(64)
    a64 = (np.outer(c, c) % 64) * (2.0 * math.pi / 64)
    c64 = np.zeros((128, 128), np.float32); s64 = np.zeros((128, 128), np.float32)
    for g in range(2):
        c64[g * 64:(g + 1) * 64, g * 64:(g + 1) * 64] = np.cos(a64) / 8.0
        s64[g * 64:(g + 1) * 64, g * 64:(g + 1) * 64] = np.sin(a64) / 8.0
    _FN_CONST[half] = (cm, sm, c64.astype(NPBF), s64.astype(NPBF))
    return _FN_CONST[half]


def fn_scatter(yfull_b, ycore, half):
    L = 8192
    k = fn_klist(half)
    yk = ycore[:, 0].reshape(-1, 256); ym = ycore[:, 1].reshape(-1, 256)
    nvalid = 2048 + 1
    kk = k[:nvalid]
    yfull_b[kk] = yk[:nvalid]
    yfull_b[(L - kk) % L] = ym[:nvalid]


def _lay8(v):
    return np.ascontiguousarray(v.reshape(8, 128).T)


def _bc(v):
    return np.ascontiguousarray(np.broadcast_to(v[None, :], (128, v.shape[0]))).astype(np.float32)


def _tiles(a):
    return a.reshape(-1, 128, a.shape[-1])


def kernel(**I):
    I = {k: np.asarray(v) for k, v in I.items()}
    c = np.ascontiguousarray
    x, ctx = I["x"], I["ctx"]
    ident = np.eye(128, dtype=NPBF)
    cx = build_stage_mod()
    call = np.zeros((8, 1024), np.float32)
    call[0:4] = I["c"]; call[4] = I["c_ctx"]
    cT = c(call.reshape(8, 8, 128).transpose(2, 1, 0))
    ims = []
    for i in range(8):
        l, q = i // 4, i % 4
        ims.append(dict(cT=cT, w=c(I["mod_w"][l][:, q * 1536:(q + 1) * 1536]), bias=c(np.broadcast_to(I["mod_b"][l][None, q * 1536:(q + 1) * 1536], (8, 1536)))))
    res = run_spmd(cx, ims)
    m_all = [np.concatenate([res[l * 4 + q]["m"] for q in range(4)], 1) for l in range(2)]

    def modv(l, j):
        return m_all[l][0:4, j * 1024:(j + 1) * 1024], m_all[l][4, j * 1024:(j + 1) * 1024]

    def mod_pair(l, jsc, jsh, b, with_ctx):
        scl, scc = modv(l, jsc); shl, shc = modv(l, jsh)
        if with_ctx:
            return c(np.stack([_lay8(scl[b]), _lay8(scc)], 1)), c(np.stack([_lay8(shl[b]), _lay8(shc)], 1))
        return c(_lay8(scl[b])[:, None, :]), c(_lay8(shl[b])[:, None, :])

    def gate(l, j, b, with_ctx):
        gl, gc = modv(l, j)
        if with_ctx:
            return c(np.stack([_bc(gl[b]), _bc(gc)], 1))
        return c(_bc(gl[b])[:, None, :])

    g34 = [0] * 32 + [1] * 2
    g32 = [0] * 32

    def tok_tiles(lat, cx_arr, b, half):
        t = _tiles(lat[b, half * 4096:(half + 1) * 4096])
        if cx_arr is None:
            return c(t)
        return c(np.concatenate([t, _tiles(cx_arr[b])], 0))

    cx = build_stage_proj(34, g34, 3072)
    ims = []
    for i in range(8):
        b, half = i // 2, i % 2
        msc, msh = mod_pair(0, 1, 0, b, True)
        ims.append(dict(xt=tok_tiles(x, ctx, b, half), msc=msc, msh=msh, w=c(I["ab_w_in"][0]), ident=ident))
    res = run_spmd(cx, ims)
    u0 = np.empty((4, 8192, 3072), np.float32); uc0 = np.empty((4, 256, 3072), np.float32)
    for i in range(8):
        b, half = i // 2, i % 2
        u0[b, half * 4096:(half + 1) * 4096] = res[i]["u"][:32].reshape(4096, 3072)
        if half == 0:
            uc0[b] = res[i]["u"][32:].reshape(256, 3072)
    cx = build_stage_na(na_items())
    res = run_spmd(cx, [na_inputs(u0[i // 2], uc0[i // 2], I["na_rpb"][0], i % 2) for i in range(8)])
    y0 = np.empty((4, 8192, 1024), np.float32); yc0 = np.empty((4, 256, 1024), np.float32)
    for i in range(8):
        b, half = i // 2, i % 2
        y0[b, half * 4096:(half + 1) * 4096, 512:] = res[i]["y"][:64].reshape(4096, 512)
        yc0[b, half * 128:(half + 1) * 128, 512:] = res[i]["y"][64:].reshape(128, 512)
    cx = build_stage_hy()
    res = run_spmd(cx, [hy_inputs(u0, uc0, cg, I) for cg in range(8)])
    y0[:, :, :512] = hy_unlayout([r["ym"] for r in res], 64)
    yc0[:, :, :512] = hy_unlayout([r["yc"] for r in res], 2)
    del u0
    cx = build_stage_mix(34, g34)
    ims = []
    for i in range(8):
        b, half = i // 2, i % 2
        ims.append(dict(yt=tok_tiles(y0, yc0, b, half), xt=tok_tiles(x, ctx, b, half), w=c(I["ab_w_out"][0]), gv=gate(0, 2, b, True),
                        lng=_bc(I["ln_g"][0, 0]), lnb=_bc(I["ln_b"][0, 0]), ident=ident))
    res = run_spmd(cx, ims)
    xa = [r["xo"] for r in res]
    cx = build_stage_mlp(34, g34)
    ims = []
    for i in range(8):
        b, half = i // 2, i % 2
        msc, msh = mod_pair(0, 4, 3, b, True)
        ims.append(dict(xt=c(xa[i]), msc=msc, msh=msh, w1=c(I["mlp_w1"][0]), w2=c(I["mlp_w2"][0]), gv=gate(0, 5, b, True),
                        lng=_bc(I["ln_g"][0, 1]), lnb=_bc(I["ln_b"][0, 1]), ident=ident))
    res = run_spmd(cx, ims)
    xl0 = [r["xo"] for r in res]
    cx = build_stage_proj(34, g34, 928)
    ims = []
    for i in range(8):
        b, half = i // 2, i % 2
        msc, msh = mod_pair(1, 1, 0, b, True)
        ims.append(dict(xt=c(xl0[i]), msc=msc, msh=msh, w=c(I["cd_w_in"][0]), ident=ident))
    res = run_spmd(cx, ims)
    u1 = np.empty((4, 8192, 928), np.float32); kvc = np.empty((4, 256, 288), np.float32)
    for i in range(8):
        b, half = i // 2, i % 2
        u1[b, half * 4096:(half + 1) * 4096] = res[i]["u"][:32].reshape(4096, 928)
        if half == 0:
            kvc[b] = res[i]["u"][32:].reshape(256, 928)[:, 384:672]
    cx = build_stage_mla()
    res = run_spmd(cx, [mla_inputs(u1[i // 2], kvc[i // 2], i % 2, I) for i in range(8)])
    y1 = [np.empty((32, 128, 1024), np.float32) for _ in range(8)]
    for i in range(8):
        y1[i][:, :, :768] = res[i]["y"]
    cx = build_stage_fn()
    ims = []
    for i in range(8):
        b, half = i // 2, i % 2
        cm, sm, c64, s64 = fn_consts(half)
        ims.append(dict(ufn=c(u1[b, :, 672:928].reshape(64, 128, 256)), g=_bc(I["fn_norm_g"][0]), b=_bc(I["fn_norm_b"][0]), c64=c64, s64=s64, cm=cm, sm=sm, ident=ident))
    res = run_spmd(cx, ims)
    yfn = np.empty((4, 8192, 256), np.float32)
    for i in range(8):
        fn_scatter(yfn[i // 2], res[i]["y"], i % 2)
    for i in range(8):
        b, half = i // 2, i % 2
        y1[i][:, :, 768:] = yfn[b, half * 4096:(half + 1) * 4096].reshape(32, 128, 256)
    cx = build_stage_mix(32, g32)
    ims = []
    for i in range(8):
        b, half = i // 2, i % 2
        ims.append(dict(yt=y1[i], xt=c(xl0[i][:32]), w=c(I["cd_w_out"][0]), gv=gate(1, 2, b, False),
                        lng=_bc(I["ln_g"][1, 0]), lnb=_bc(I["ln_b"][1, 0]), ident=ident))
    res = run_spmd(cx, ims)
    xa = [r["xo"] for r in res]
    cx = build_stage_mlp(32, g32)
    ims = []
    for i in range(8):
        b, half = i // 2, i % 2
        msc, msh = mod_pair(1, 4, 3, b, False)
        ims.append(dict(xt=c(xa[i]), msc=msc, msh=msh, w1=c(I["mlp_w1"][1]), w2=c(I["mlp_w2"][1]), gv=gate(1, 5, b, False),
                        lng=_bc(I["ln_g"][1, 1]), lnb=_bc(I["ln_b"][1, 1]), ident=ident))
    res = run_spmd(cx, ims)
    out = np.empty((4, 8192, 1024), np.float32)
    for i in range(8):
        b, half = i // 2, i % 2
        out[b, half * 4096:(half + 1) * 4096] = res[i]["xo"].reshape(4096, 1024)
    return out
```
